# Optimizing a Trainium2 kernel written in Bass

```python
import jax, jax.numpy as jnp
from jax import lax
import numpy as np

D_MODEL = 1024
BATCH = 8
SEQ = 2048
DEPTH = 2
DEC_BATCH = 32
DEC_SEQ = 4
PAST_LEN = 8192
PAGE_SIZE = 128

N_BRANCH = 4
BRANCH_W = D_MODEL // 4
NORM_EPS = 1e-6
A_BLOCKS = 4
A_BLOCK_W = BRANCH_W // A_BLOCKS
A_CONV_W = 4
A_C = 8.0
B_HEAD = 64
B_HEADS = BRANCH_W // B_HEAD
B_DECAY_LORA = 64
B_AAA_LORA = 64
B_GATE_LORA = 128
B_GN_EPS = 64e-5
B_SPLITS = (BRANCH_W, BRANCH_W, BRANCH_W, B_DECAY_LORA, B_AAA_LORA, B_GATE_LORA)
B_COLS = BRANCH_W * 3 + B_DECAY_LORA + B_AAA_LORA + B_GATE_LORA
C_HEADS = 4
C_EXPAND = 128
C_VDIM = BRANCH_W // C_HEADS
C_FDIM = C_HEADS * C_EXPAND
C_CHUNK = 64
D_HEADS = 4
D_HEAD = BRANCH_W // D_HEADS
D_PATTERNS = ((128, 1), (512, 4), (2048, 16))
D_WIN_MAX = 2048
D_QBLOCK = 128
ROPE_THETA = 10000.0
NEG_INF = -1e30
D_FF = 3 * D_MODEL
FFN_CONV_W = 3
IN_SPLITS = (BRANCH_W, BRANCH_W,
             B_COLS,
             C_FDIM, C_FDIM, BRANCH_W, BRANCH_W,
             BRANCH_W, BRANCH_W, BRANCH_W,
             N_BRANCH * D_MODEL)
N_IN = 2 * BRANCH_W + B_COLS + 2 * C_FDIM + 5 * BRANCH_W + N_BRANCH * D_MODEL

kernel_name = 'hybrid_rglru_rwkv7_hgrn2_dilated_step'


def split_cols(c, sizes):
    out, start = [], 0
    for s in sizes:
        out.append(c[..., start:start + s])
        start += s
    return out


def rms_norm(x, g):
    x32 = x.astype(jnp.float32)
    y = x32 * lax.rsqrt(jnp.mean(x32 * x32, axis=-1, keepdims=True) + NORM_EPS)
    return (y * g.astype(jnp.float32)).astype(x.dtype)


def causal_dwconv(x, buf, w, b):
    t_ = x.shape[1]
    xp = jnp.concatenate([buf.astype(jnp.float32), x.astype(jnp.float32)], axis=1)
    w32 = w.astype(jnp.float32)
    y = b.astype(jnp.float32)
    for j in range(w.shape[0]):
        y = y + w32[j] * xp[:, j:j + t_]
    return y, xp[:, t_:]


def rope(t, pos):
    half = t.shape[-1] // 2
    inv = ROPE_THETA ** (-jnp.arange(half, dtype=jnp.float32) / half)
    ang = pos.astype(jnp.float32)[:, None] * inv[None, :]
    cos = jnp.cos(ang)[None, :, None, :]
    sin = jnp.sin(ang)[None, :, None, :]
    t = t.astype(jnp.float32)
    t1, t2 = t[..., :half], t[..., half:]
    return jnp.concatenate([t1 * cos - t2 * sin, t2 * cos + t1 * sin], axis=-1)


def _lin_combine(e1, e2):
    a1, b1 = e1
    a2, b2 = e2
    return a1 * a2, a2 * b1 + b2


def rglru_branch(ax, ag, h0, conv_buf, conv_w, conv_b, gx_w, gx_b, ga_w, ga_b, lam):
    f32 = jnp.float32
    xc, new_buf = causal_dwconv(ax, conv_buf, conv_w, conv_b)
    b_, t_ = xc.shape[:2]
    xb = xc.reshape(b_, t_, A_BLOCKS, A_BLOCK_W)
    gate_x = jax.nn.sigmoid(jnp.einsum('bthi,hij->bthj', xb, gx_w.astype(f32)).reshape(b_, t_, BRANCH_W) + gx_b)
    gate_a = jax.nn.sigmoid(jnp.einsum('bthi,hij->bthj', xb, ga_w.astype(f32)).reshape(b_, t_, BRANCH_W) + ga_b)
    log_a = -A_C * gate_a * jax.nn.softplus(-lam.astype(f32))
    a = jnp.exp(log_a)
    b_in = jnp.sqrt(-jnp.expm1(2.0 * log_a)) * (gate_x * xc)
    a_cum, h_zero = lax.associative_scan(_lin_combine, (a, b_in), axis=1)
    h = a_cum * h0.astype(f32)[:, None] + h_zero
    y = h * jax.nn.gelu(ag.astype(f32))
    return y, h[:, -1], new_buf


def rwkv7_scan(r, w, k, v, a, b, s0):
    def step(s, inp):
        r_t, w_t, k_t, v_t, a_t, b_t = inp
        sa = jnp.einsum('bhvk,bhk->bhv', s, a_t)
        s = s * w_t[:, :, None, :] + sa[..., None] * b_t[:, :, None, :] + v_t[..., None] * k_t[:, :, None, :]
        return s, jnp.einsum('bhvk,bhk->bhv', s, r_t)
    xs = tuple(jnp.moveaxis(z, 1, 0) for z in (r, w, k, v, a, b))
    s_fin, y = lax.scan(step, s0, xs)
    return jnp.moveaxis(y, 0, 1), s_fin


def rwkv7_branch(cb, prev, s0, mu, w0, w2, a0, a2, g2, k_k, k_a, r_k, ln_w, ln_b):
    f32 = jnp.float32
    cb = cb.astype(f32)
    b_, t_ = cb.shape[:2]
    shifted = jnp.concatenate([prev.astype(f32)[:, None], cb[:, :-1]], axis=1)
    cm = cb + (shifted - cb) * mu.astype(f32)
    r, k, v, wl, al, gl = split_cols(cm, B_SPLITS)
    w = -jax.nn.softplus(-(w0 + jnp.tanh(wl) @ w2.astype(f32))) - 0.5
    decay = jnp.exp(-jnp.exp(w))
    a = jax.nn.sigmoid(a0 + al @ a2.astype(f32))
    g = jax.nn.sigmoid(gl) @ g2.astype(f32)

    def heads(z):
        return z.reshape(b_, t_, B_HEADS, B_HEAD)
    kk = heads(k * k_k)
    kk = kk / jnp.maximum(jnp.sqrt(jnp.sum(kk * kk, axis=-1, keepdims=True)), 1e-12)
    k = k * (1.0 + (a - 1.0) * k_a)
    rh, kh, vh = heads(r), heads(k), heads(v)
    y, s_new = rwkv7_scan(rh, heads(decay), kh, vh, -kk, kk * heads(a), s0.astype(f32))
    mean = jnp.mean(y, axis=-1, keepdims=True)
    var = jnp.mean(jnp.square(y - mean), axis=-1, keepdims=True)
    y = ((y - mean) * lax.rsqrt(var + B_GN_EPS)).reshape(b_, t_, BRANCH_W) * ln_w + ln_b
    bonus = jnp.sum(rh * kh * r_k, axis=-1, keepdims=True) * vh
    out = (y + bonus.reshape(b_, t_, BRANCH_W)) * g
    return out, cb[:, -1], s_new


def gla_chunked(q, k, v, logf, s0):
    b_, t_, h_, kd = q.shape
    vd = v.shape[-1]
    c_ = C_CHUNK if t_ % C_CHUNK == 0 else t_
    n_ = t_ // c_

    def chunks(z):
        return jnp.moveaxis(z.reshape(b_, n_, c_, h_, z.shape[-1]), 1, 0)
    causal = jnp.tril(jnp.ones((c_, c_), dtype=bool))[None, :, :, None, None]

    def step(s, inp):
        qc, kc, vc, gc = inp
        bcum = jnp.cumsum(gc, axis=1)
        o_inter = jnp.einsum('bthk,bhkv->bthv', qc * jnp.exp(bcum), s)
        diff = bcum[:, :, None] - bcum[:, None, :]
        dec = jnp.exp(jnp.where(causal, diff, -jnp.inf))
        att = jnp.einsum('bthk,bshk,btshk->btsh', qc, kc, dec)
        o_intra = jnp.einsum('btsh,bshv->bthv', att, vc)
        b_last = bcum[:, -1]
        s = s * jnp.exp(b_last)[..., None] + jnp.einsum('bshk,bshv->bhkv', kc * jnp.exp(b_last[:, None] - bcum), vc)
        return s, o_inter + o_intra
    s_fin, o = lax.scan(step, s0, (chunks(q), chunks(k), chunks(v), chunks(logf)))
    return jnp.moveaxis(o, 0, 1).reshape(b_, t_, h_, vd), s_fin


def hgrn2_branch(cq, cf, ci, cg, s0, lb, norm_g):
    f32 = jnp.float32
    b_, t_ = cq.shape[:2]
    q = cq.astype(f32).reshape(b_, t_, C_HEADS, C_EXPAND)
    f = cf.astype(f32).reshape(b_, t_, C_HEADS, C_EXPAND)
    v = ci.astype(f32).reshape(b_, t_, C_HEADS, C_VDIM)
    lbh = lb.reshape(C_HEADS, C_EXPAND)
    fg = lbh + (1.0 - lbh) * jax.nn.sigmoid(f)
    o, s_new = gla_chunked(q, 1.0 - fg, v, jnp.log(fg), s0.astype(f32))
    o = o * lax.rsqrt(jnp.mean(o * o, axis=-1, keepdims=True) + NORM_EPS) * norm_g.astype(f32)
    o = o.reshape(b_, t_, BRANCH_W) * jax.nn.silu(cg.astype(f32))
    return o, s_new


def dilated_block(q_blk, s0, k_all, v_all):
    f32 = jnp.float32
    qb = q_blk.shape[1]
    qi = s0 + jnp.arange(qb)
    q32 = q_blk.astype(f32) * (D_HEAD ** -0.5)
    stats = []
    for win, dil in D_PATTERNS:
        nk = win // dil + 1
        idx = qi[:, None] - dil * jnp.arange(nk)[None, :]
        valid = idx >= 0
        idx = jnp.maximum(idx, 0)
        kg = jnp.take(k_all, idx, axis=1).astype(f32)
        vg = jnp.take(v_all, idx, axis=1).astype(f32)
        s = jnp.einsum('bqhd,bqjhd->bhqj', q32, kg)
        s = jnp.where(valid[None, None], s, NEG_INF)
        m = jnp.max(s, axis=-1)
        e = jnp.exp(s - m[..., None])
        stats.append((m, jnp.sum(e, axis=-1), jnp.einsum('bhqj,bqjhd->bhqd', e, vg)))
    m_all = stats[0][0]
    for m, _, _ in stats[1:]:
        m_all = jnp.maximum(m_all, m)
    num = jnp.zeros_like(stats[0][2])
    den = jnp.zeros_like(stats[0][1])
    for m, l, n in stats:
        c = jnp.exp(m - m_all)
        num = num + c[..., None] * n
        den = den + c * l
    return jnp.transpose(num / den[..., None], (0, 2, 1, 3))


def dilated_branch(dq, dk, dv, pos, k_cache, v_cache):
    b_, t_ = dq.shape[:2]

    def hs(z):
        return z.reshape(b_, t_, D_HEADS, D_HEAD)
    q = rope(hs(dq), pos)
    k = rope(hs(dk), pos)
    v = hs(dv).astype(jnp.float32)
    if k_cache is None:
        k_all, v_all, off = k, v, 0
    else:
        k_all = jnp.concatenate([k_cache.astype(jnp.float32), k], axis=1)
        v_all = jnp.concatenate([v_cache.astype(jnp.float32), v], axis=1)
        off = k_cache.shape[1]
    qb = D_QBLOCK if t_ % D_QBLOCK == 0 else t_
    nb = t_ // qb
    q_blocks = jnp.moveaxis(q.reshape(b_, nb, qb, D_HEADS, D_HEAD), 1, 0)
    starts = off + qb * jnp.arange(nb)
    out = lax.map(lambda args: dilated_block(args[0], args[1], k_all, v_all), (q_blocks, starts))
    return jnp.moveaxis(out, 0, 1).reshape(b_, t_, BRANCH_W), k, v


def run_trunk(x, pos, h_a, conv_a, wkv_b, shift_b, s_c, k_d, v_d, conv_ffn, W):
    f32 = jnp.float32
    b_, t_, _ = x.shape
    lb_all = jax.nn.softmax(W['c_lb'].astype(f32), axis=0)
    lb_all = jnp.cumsum(lb_all, axis=0) - lb_all[0]
    keep = min(D_WIN_MAX, t_) if k_d is None else t_
    o_ha, o_ca, o_wkv, o_sh, o_sc, o_k, o_v, o_cf = [], [], [], [], [], [], [], []
    for l in range(DEPTH):
        u = rms_norm(x, W['norm1_g'][l])
        cols = jnp.einsum('btd,dc->btc', u, W['w_in'][l])
        ax, ag, cb, cq, cf, ci, cg, dq, dk, dv, gt = split_cols(cols, IN_SPLITS)
        y_a, h_new, ca_new = rglru_branch(ax, ag, h_a[l], conv_a[l], W['a_conv_w'][l], W['a_conv_b'][l],
                                          W['a_gx_w'][l], W['a_gx_b'][l], W['a_ga_w'][l], W['a_ga_b'][l],
                                          W['a_lambda'][l])
        y_b, sh_new, wkv_new = rwkv7_branch(cb, shift_b[l], wkv_b[l], W['b_mu'][l], W['b_w0'][l], W['b_w2'][l],
                                            W['b_a0'][l], W['b_a2'][l], W['b_g2'][l], W['b_k_k'][l],
                                            W['b_k_a'][l], W['b_r_k'][l], W['b_ln_w'][l], W['b_ln_b'][l])
        y_c, sc_new = hgrn2_branch(cq, cf, ci, cg, s_c[l], lb_all[l], W['c_norm_g'][l])
        kc = None if k_d is None else k_d[l]
        vc = None if v_d is None else v_d[l]
        y_d, k_rows, v_rows = dilated_branch(dq, dk, dv, pos, kc, vc)
        branches = jnp.stack([y_a, y_b, y_c, y_d], axis=2).astype(x.dtype)
        z = jnp.einsum('btnc,ncd->btnd', branches, W['w_branch'][l]).astype(f32)
        gates = jax.nn.sigmoid(gt.astype(f32).reshape(b_, t_, N_BRANCH, D_MODEL))
        merged = jnp.sum(gates * z, axis=2).astype(x.dtype)
        x = x + jnp.einsum('btd,de->bte', merged, W['w_out'][l])
        v_in = rms_norm(x, W['norm2_g'][l])
        hg = jnp.einsum('btd,df->btf', v_in, W['ffn_w_gate'][l])
        hu = jnp.einsum('btd,df->btf', v_in, W['ffn_w_up'][l])
        hg_c, cf_new = causal_dwconv(hg, conv_ffn[l], W['ffn_conv_w'][l], W['ffn_conv_b'][l])
        hmid = (jax.nn.gelu(hg_c) * hu.astype(f32)).astype(x.dtype)
        x = x + jnp.einsum('btf,fd->btd', hmid, W['ffn_w_down'][l])
        o_ha.append(h_new)
        o_ca.append(ca_new)
        o_wkv.append(wkv_new)
        o_sh.append(sh_new)
        o_sc.append(sc_new)
        o_k.append(k_rows[:, t_ - keep:])
        o_v.append(v_rows[:, t_ - keep:])
        o_cf.append(cf_new)
    y = rms_norm(x, W['final_norm_g'])
    return (y, jnp.stack(o_ha), jnp.stack(o_ca), jnp.stack(o_wkv), jnp.stack(o_sh), jnp.stack(o_sc),
            jnp.stack(o_k), jnp.stack(o_v), jnp.stack(o_cf))


def setup_inputs(seed: int = 0) -> dict:
    key = jax.random.key(seed)
    ks = iter(jax.random.split(key, 64))
    f32 = jnp.float32

    def nrm(shape, s):
        return jax.random.normal(next(ks), shape, f32) * s
    cache_win = min(D_WIN_MAX, PAST_LEN)
    u = jax.random.uniform(next(ks), (DEPTH, BRANCH_W), f32, minval=0.9, maxval=0.999)
    p = u ** (1.0 / A_C)
    a_lambda = jnp.log(p) - jnp.log1p(-p)
    return {
        'x_prompt': nrm((BATCH, SEQ, D_MODEL), 1.0),
        'x_sample': nrm((DEC_BATCH, DEC_SEQ, D_MODEL), 1.0),
        'state_a_h': nrm((DEPTH, DEC_BATCH, BRANCH_W), 0.5),
        'state_a_conv': nrm((DEPTH, DEC_BATCH, A_CONV_W - 1, BRANCH_W), 1.0),
        'state_b_wkv': nrm((DEPTH, DEC_BATCH, B_HEADS, B_HEAD, B_HEAD), 0.5),
        'state_b_shift': nrm((DEPTH, DEC_BATCH, B_COLS), 1.0),
        'state_c_s': nrm((DEPTH, DEC_BATCH, C_HEADS, C_EXPAND, C_VDIM), 0.5),
        'cache_d_k': nrm((DEPTH, DEC_BATCH, cache_win, D_HEADS, D_HEAD), 1.0),
        'cache_d_v': nrm((DEPTH, DEC_BATCH, cache_win, D_HEADS, D_HEAD), 1.0),
        'state_ffn_conv': nrm((DEPTH, DEC_BATCH, FFN_CONV_W - 1, D_FF), 1.0),
        'norm1_g': 1.0 + nrm((DEPTH, D_MODEL), 0.05),
        'w_in': nrm((DEPTH, D_MODEL, N_IN), D_MODEL ** -0.5),
        'a_conv_w': nrm((DEPTH, A_CONV_W, BRANCH_W), 0.5),
        'a_conv_b': nrm((DEPTH, BRANCH_W), 0.02),
        'a_gx_w': nrm((DEPTH, A_BLOCKS, A_BLOCK_W, A_BLOCK_W), A_BLOCK_W ** -0.5),
        'a_gx_b': nrm((DEPTH, BRANCH_W), 0.1),
        'a_ga_w': nrm((DEPTH, A_BLOCKS, A_BLOCK_W, A_BLOCK_W), A_BLOCK_W ** -0.5),
        'a_ga_b': nrm((DEPTH, BRANCH_W), 0.1),
        'a_lambda': a_lambda,
        'b_mu': jax.random.uniform(next(ks), (DEPTH, B_COLS), f32),
        'b_w0': nrm((DEPTH, BRANCH_W), 0.5) - 1.0,
        'b_w2': nrm((DEPTH, B_DECAY_LORA, BRANCH_W), 0.1),
        'b_a0': nrm((DEPTH, BRANCH_W), 0.1),
        'b_a2': nrm((DEPTH, B_AAA_LORA, BRANCH_W), 0.1),
        'b_g2': nrm((DEPTH, B_GATE_LORA, BRANCH_W), B_GATE_LORA ** -0.5),
        'b_k_k': 0.85 + nrm((DEPTH, BRANCH_W), 0.05),
        'b_k_a': 1.0 + nrm((DEPTH, BRANCH_W), 0.05),
        'b_r_k': nrm((DEPTH, B_HEADS, B_HEAD), 0.1),
        'b_ln_w': 1.0 + nrm((DEPTH, BRANCH_W), 0.05),
        'b_ln_b': nrm((DEPTH, BRANCH_W), 0.02),
        'c_lb': nrm((DEPTH, C_FDIM), 0.5),
        'c_norm_g': 1.0 + nrm((DEPTH, C_VDIM), 0.05),
        'w_branch': nrm((DEPTH, N_BRANCH, BRANCH_W, D_MODEL), BRANCH_W ** -0.5),
        'w_out': nrm((DEPTH, D_MODEL, D_MODEL), D_MODEL ** -0.5),
        'norm2_g': 1.0 + nrm((DEPTH, D_MODEL), 0.05),
        'ffn_w_gate': nrm((DEPTH, D_MODEL, D_FF), D_MODEL ** -0.5),
        'ffn_w_up': nrm((DEPTH, D_MODEL, D_FF), D_MODEL ** -0.5),
        'ffn_conv_w': nrm((DEPTH, FFN_CONV_W, D_FF), FFN_CONV_W ** -0.5),
        'ffn_conv_b': nrm((DEPTH, D_FF), 0.02),
        'ffn_w_down': nrm((DEPTH, D_FF, D_MODEL), D_FF ** -0.5),
        'final_norm_g': 1.0 + nrm((D_MODEL,), 0.05),
    }


def reference(x_prompt, x_sample, state_a_h, state_a_conv, state_b_wkv, state_b_shift, state_c_s,
              cache_d_k, cache_d_v, state_ffn_conv, norm1_g, w_in, a_conv_w, a_conv_b, a_gx_w, a_gx_b,
              a_ga_w, a_ga_b, a_lambda, b_mu, b_w0, b_w2, b_a0, b_a2, b_g2, b_k_k, b_k_a, b_r_k, b_ln_w,
              b_ln_b, c_lb, c_norm_g, w_branch, w_out, norm2_g, ffn_w_gate, ffn_w_up, ffn_conv_w,
              ffn_conv_b, ffn_w_down, final_norm_g):
    W = {'norm1_g': norm1_g, 'w_in': w_in, 'a_conv_w': a_conv_w, 'a_conv_b': a_conv_b, 'a_gx_w': a_gx_w,
         'a_gx_b': a_gx_b, 'a_ga_w': a_ga_w, 'a_ga_b': a_ga_b, 'a_lambda': a_lambda, 'b_mu': b_mu,
         'b_w0': b_w0, 'b_w2': b_w2, 'b_a0': b_a0, 'b_a2': b_a2, 'b_g2': b_g2, 'b_k_k': b_k_k,
         'b_k_a': b_k_a, 'b_r_k': b_r_k, 'b_ln_w': b_ln_w, 'b_ln_b': b_ln_b, 'c_lb': c_lb,
         'c_norm_g': c_norm_g, 'w_branch': w_branch, 'w_out': w_out, 'norm2_g': norm2_g,
         'ffn_w_gate': ffn_w_gate, 'ffn_w_up': ffn_w_up, 'ffn_conv_w': ffn_conv_w,
         'ffn_conv_b': ffn_conv_b, 'ffn_w_down': ffn_w_down, 'final_norm_g': final_norm_g}
    f32 = jnp.float32
    b_p, t_p = x_prompt.shape[:2]
    t_s = x_sample.shape[1]

    def zeros(*s):
        return jnp.zeros((DEPTH, b_p) + s, f32)
    (y_prompt, p_a_h, p_a_conv, p_b_wkv, p_b_shift, p_c_s, p_d_k, p_d_v, p_ffn_conv) = run_trunk(
        x_prompt, jnp.arange(t_p), zeros(BRANCH_W), zeros(A_CONV_W - 1, BRANCH_W),
        zeros(B_HEADS, B_HEAD, B_HEAD), zeros(B_COLS), zeros(C_HEADS, C_EXPAND, C_VDIM),
        None, None, zeros(FFN_CONV_W - 1, D_FF), W)
    (y_sample, s_a_h, s_a_conv, s_b_wkv, s_b_shift, s_c_s, s_d_k, s_d_v, s_ffn_conv) = run_trunk(
        x_sample, PAST_LEN + jnp.arange(t_s), state_a_h, state_a_conv, state_b_wkv, state_b_shift,
        state_c_s, cache_d_k, cache_d_v, state_ffn_conv, W)
    return (y_prompt, y_sample, p_a_h, p_a_conv, p_b_wkv, p_b_shift, p_c_s, p_d_k, p_d_v, p_ffn_conv,
            s_a_h, s_a_conv, s_b_wkv, s_b_shift, s_c_s, s_d_k, s_d_v, s_ffn_conv)
```

```python
import numpy as np
import ml_dtypes
import concourse.bass as bass
import concourse.mybir as mybir
from concourse.bass_utils import run_bass_kernel_spmd

F32 = mybir.dt.float32
BF16 = mybir.dt.bfloat16
ALU = mybir.AluOpType
AF = mybir.ActivationFunctionType
AX = mybir.AxisListType

NT = 2064
NTILES = [(0, 512), (512, 512), (1024, 512), (1536, 512), (2048, 16)]
TT128 = [(t * 128, 128) for t in range(16)] + [(2048, 16)]
DEPTH = 2
EM05 = float(np.exp(-0.5))


class Ctx:
    NDS = 8

    def __init__(self, nc):
        self.nc = nc
        self.engs = {'pe': nc.tensor, 'act': nc.scalar, 'dve': nc.vector, 'pool': nc.gpsimd, 'sp': nc.sync}
        self.csem = {e: nc.alloc_semaphore(name="c_" + e) for e in ('pe', 'act', 'dve', 'pool')}
        self.ccnt = {e: 0 for e in self.csem}
        self.dsem = {q: [nc.alloc_semaphore(name="d_%s%d" % (q, i)) for i in range(self.NDS)]
                     for q in ('sp', 'pool', 'act')}
        self.dcnt = {q: 0 for q in self.dsem}
        self.waited = {e: {} for e in self.engs}
        self.last_w = {}
        self.readers = {}
        self.n_inst = 0
        self.n_wait = 0

    def _wait(self, engine, ev):
        sem, val, src = ev
        w = self.waited[engine]
        k = id(sem)
        if w.get(k, 0) >= val:
            return
        if src == 'pe' and engine == 'pe':
            return
        self.engs[engine].wait_ge(sem, val)
        w[k] = val
        self.n_wait += 1

    def _deps(self, engine, reads, writes):
        for k in reads:
            ev = self.last_w.get(k)
            if ev is not None:
                self._wait(engine, ev)
        for k in writes:
            ev = self.last_w.get(k)
            if ev is not None:
                self._wait(engine, ev)
            for ev in self.readers.get(k, ()):
                self._wait(engine, ev)

    def _commit(self, ev, reads, writes):
        for k in writes:
            self.last_w[k] = ev
            self.readers[k] = []
        for k in reads:
            if k not in writes:
                self.readers.setdefault(k, []).append(ev)

    def op(self, engine, fn, reads=(), writes=()):
        pr = [k for k in reads if k == 'psb' or (isinstance(k, tuple) and k[0] == 'ps')]
        if pr:
            reads = [k for k in reads if k not in pr]
            writes = list(writes) + [k for k in pr if k not in writes]
        self._deps(engine, reads, writes)
        inst = fn(self.engs[engine])
        self.ccnt[engine] += 1
        inst.then_inc(self.csem[engine], 1)
        ev = (self.csem[engine], self.ccnt[engine], engine)
        self._commit(ev, reads, writes)
        self.n_inst += 1
        return inst

    def dma(self, q, out, in_, reads=(), writes=(), **kw):
        n = self.dcnt[q]
        sem = self.dsem[q][n % self.NDS]
        target = 16 * (n // self.NDS + 1)
        if n >= self.NDS:
            self._wait(q, (sem, target - 16, 'dma'))
        self._deps(q, reads, writes)
        inst = self.engs[q].dma_start(out=out, in_=in_, **kw)
        inst.then_inc(sem, 16)
        self.dcnt[q] = n + 1
        ev = (sem, target, 'dma')
        self._commit(ev, reads, writes)
        self.n_inst += 1
        return inst

    def _all_events(self):
        evs = []
        for q in self.dsem:
            n = self.dcnt[q]
            for i in range(min(n, self.NDS)):
                cnt = (n - 1 - i) // self.NDS + 1
                evs.append((self.dsem[q][i], 16 * cnt, 'dma'))
        for e in self.csem:
            if self.ccnt[e]:
                evs.append((self.csem[e], self.ccnt[e], 'x'))
        return evs

    def barrier(self):
        evs = self._all_events()
        for e in self.engs:
            for ev in evs:
                self._wait(e, ev)
        self.last_w = {}
        self.readers = {}

    def finish(self):
        for ev in self._all_events():
            self._wait('sp', ev)


class Arena:
    def __init__(self, nc, name, nbytes):
        self.words = nbytes // 4
        self.t = nc.alloc_sbuf_tensor(name, [128, self.words], F32).ap()
        self.off = 0

    def reset(self):
        self.off = 0

    def alloc(self, shape, dtype=F32):
        n = int(np.prod(shape))
        words = n if dtype == F32 else (n + 1) // 2
        words = (words + 7) // 8 * 8
        assert self.off + words <= self.words, ("arena overflow", self.off, words, self.words)
        ap = self.t[:, self.off:self.off + words]
        self.off += words
        if dtype != F32:
            ap = ap.bitcast(dtype)
        ap = ap[:, 0:n]
        if len(shape) == 2:
            return ap.rearrange("p (a b) -> p a b", a=shape[0])
        if len(shape) == 3:
            return ap.rearrange("p (a b c) -> p a b c", a=shape[0], b=shape[1])
        return ap


class CtxTag:
    SHARED = ('ST', 'STb', 'ppf', 'pder', 'spf', 'pmat', 'cbb', 'cbf', 'caf', 'osm', 'psb')

    def __init__(self, c, tag):
        self.c = c
        self.tag = tag

    def _m(self, keys):
        out = []
        for k in keys:
            if k in self.SHARED or (isinstance(k, tuple) and k[0] in ('ps', 'b1', 'cols')):
                out.append(k)
            else:
                out.append((self.tag, k))
        return out

    def op(self, engine, fn, reads=(), writes=()):
        return self.c.op(engine, fn, self._m(reads), self._m(writes))

    def dma(self, q, out, in_, reads=(), writes=(), **kw):
        return self.c.dma(q, out, in_, reads=self._m(reads), writes=self._m(writes), **kw)


def _mk_layout(entries):
    off, d = 0, {}
    for name, n in entries:
        d[name] = off
        off += n
    return d, off


PK, NPK = _mk_layout([
    ('a_cw', 8), ('a_cb', 2), ('a_gxb', 2), ('a_gab', 2), ('a_lam', 2),
    ('b_mu', 8), ('b_w0', 2), ('b_a0', 2), ('b_kk', 2), ('b_ka', 2), ('b_rk', 2), ('b_lnw', 2), ('b_lnb', 2),
    ('c_lb0', 4), ('c_lb1', 4), ('c_ng', 1), ('f_cw', 72), ('f_cb', 24), ('pad', 1),
    ('b64_mu', 14), ('b_mug', 1), ('b64_w0', 4), ('b64_a0', 4), ('b64_kk', 4), ('b64_ka', 4), ('b64_rk', 4), ('b64_lnw', 4),
    ('b64_lnb', 4), ('pad2', 5),
    ('gxw', 256), ('gaw', 256), ('w2', 256), ('a2', 256), ('g2', 256)])
PMAT0 = PK['gxw']
SP, NSP = _mk_layout([('a_h', 8), ('a_conv', 24), ('b_shift', 32), ('f_conv', 192), ('b_sh64', 56), ('b_shg', 4)])
OS, NOS = _mk_layout([('a_h', 10), ('a_conv', 30), ('b_shift', 40), ('f_conv', 240), ('b_sh64', 70), ('b_shg', 5)])
CB, NCB = _mk_layout([('ident', 128), ('tri_incl', 64), ('tri_su', 64), ('tri_sl', 64), ('blk64', 128),
                      ('maskS', 68), ('maskN', 16), ('ones', 64)])
CA, NCA = _mk_layout([('rope', 17 * 64), ('reset', 512)])


def _fm(v, nchunk):
    return np.ascontiguousarray(np.asarray(v, np.float32).reshape(nchunk, 128).T)


def _mult(dist):
    dist = np.asarray(dist)
    m = ((dist >= 0) & (dist <= 128)).astype(np.float32)
    m += ((dist >= 0) & (dist % 4 == 0) & (dist <= 512))
    m += ((dist >= 0) & (dist % 16 == 0) & (dist <= 2048))
    return m.astype(np.float32)


def host_consts():
    cb = np.zeros((128, NCB), np.float32)
    cb[:, CB['ident']:CB['ident'] + 128] = np.eye(128, dtype=np.float32)
    i = np.arange(64)
    cb[:64, CB['tri_incl']:CB['tri_incl'] + 64] = (i[:, None] <= i[None, :])
    cb[:64, CB['tri_su']:CB['tri_su'] + 64] = (i[:, None] < i[None, :])
    cb[:64, CB['tri_sl']:CB['tri_sl'] + 64] = (i[:, None] > i[None, :])
    blk = np.zeros((128, 128), np.float32)
    blk[:64, :64] = 1.0
    blk[64:, 64:] = 1.0
    cb[:, CB['blk64']:CB['blk64'] + 128] = blk
    cb[:, CB['ones']:CB['ones'] + 64] = 1.0
    p = np.arange(128)
    ms = np.zeros((128, 17, 4), np.float32)
    for b in range(16):
        for q in range(4):
            ms[:, b, q] = _mult(2048 + q - (b * 128 + p))
    cb[:, CB['maskS']:CB['maskS'] + 68] = ms.reshape(128, 68)
    mn = np.zeros((16, 4, 4), np.float32)
    for s2 in range(4):
        for i2 in range(4):
            for q in range(4):
                mn[s2 * 4 + i2, s2, q] = _mult(q - i2)
    cb[:16, CB['maskN']:CB['maskN'] + 16] = mn.reshape(16, 16)
    ca = np.zeros((128, NCA), np.float32)
    half = 32
    inv = (10000.0 ** (-np.arange(half, dtype=np.float32) / half)).astype(np.float32)
    rope = np.zeros((128, 17, 64), np.float32)
    for t in range(17):
        if t < 16:
            pos = (t * 128 + np.arange(128)).astype(np.float32)
        else:
            pos = np.zeros(128, np.float32)
            pos[:16] = (8192 + np.tile(np.arange(4), 4)).astype(np.float32)
        ang = (pos[:, None] * inv[None, :]).astype(np.float32)
        rope[:, t, :32] = np.cos(ang)
        rope[:, t, 32:] = np.sin(ang)
    ca[:, CA['rope']:CA['rope'] + 17 * 64] = rope.reshape(128, -1)
    rs = np.ones(512, np.float32)
    rs[::64] = 0.0
    ca[:, CA['reset']:CA['reset'] + 512] = rs[None, :]
    mp = np.zeros((128, 17, 128), np.float32)
    for d in range(17):
        mp[:, d, :] = _mult((d * 128 + p[None, :]) - p[:, None])
    return cb, ca, mp.reshape(128, 17 * 128)


def host_ppack(W, l):
    pk = np.zeros((128, NPK), np.float32)

    def put(name, arr):
        arr = np.asarray(arr, np.float32)
        pk[:arr.shape[0], PK[name]:PK[name] + arr.shape[1]] = arr
    cw = np.asarray(W['a_conv_w'][l], np.float32)
    put('a_cw', np.stack([_fm(cw[j], 2) for j in range(4)], axis=2).reshape(128, 8))
    put('a_cb', _fm(W['a_conv_b'][l], 2))
    put('a_gxb', _fm(W['a_gx_b'][l], 2))
    put('a_gab', _fm(W['a_ga_b'][l], 2))
    put('a_lam', _fm(W['a_lambda'][l], 2))
    put('b_mu', _fm(W['b_mu'][l], 8))
    put('b_w0', _fm(W['b_w0'][l], 2))
    put('b_a0', _fm(W['b_a0'][l], 2))
    put('b_kk', _fm(W['b_k_k'][l], 2))
    put('b_ka', _fm(W['b_k_a'][l], 2))
    put('b_rk', _fm(np.asarray(W['b_r_k'][l]).reshape(256), 2))
    put('b_lnw', _fm(W['b_ln_w'][l], 2))
    put('b_lnb', _fm(W['b_ln_b'][l], 2))
    put('c_lb0', _fm(W['c_lb'][0], 4))
    put('c_lb1', _fm(W['c_lb'][1], 4))
    put('c_ng', np.tile(np.asarray(W['c_norm_g'][l], np.float32), 2)[:, None])
    fw = np.asarray(W['ffn_conv_w'][l], np.float32)
    put('f_cw', np.stack([_fm(fw[j], 24) for j in range(3)], axis=2).reshape(128, 72))
    put('f_cb', _fm(W['ffn_conv_b'][l], 24))
    for nm, key in (('gxw', 'a_gx_w'), ('gaw', 'a_ga_w')):
        g = np.asarray(W[key][l], np.float32)
        m = np.zeros((128, 256), np.float32)
        for c in range(2):
            for hh in range(2):
                m[hh * 64:(hh + 1) * 64, c * 128 + hh * 64:c * 128 + (hh + 1) * 64] = g[c * 2 + hh]
        put(nm, m)
    put('w2', np.asarray(W['b_w2'][l], np.float32))
    put('a2', np.asarray(W['b_a2'][l], np.float32))
    mu = np.asarray(W['b_mu'][l], np.float32)
    put('b64_mu', mu[:896].reshape(14, 64).T)
    put('b_mug', mu[896:].reshape(128, 1))
    for nm, key in (('b64_w0', 'b_w0'), ('b64_a0', 'b_a0'), ('b64_kk', 'b_k_k'), ('b64_ka', 'b_k_a'), ('b64_rk', 'b_r_k'),
                    ('b64_lnw', 'b_ln_w'), ('b64_lnb', 'b_ln_b')):
        put(nm, np.asarray(W[key][l], np.float32).reshape(4, 64).T)
    put('g2', np.asarray(W['b_g2'][l], np.float32))
    return pk


STOP = None


class _Stop(Exception):
    pass


def build_program(enable=('A', 'B', 'C', 'D')):
    nc = bass.Bass("TRN2", target_bir_lowering=False)
    try:
        _build(nc, enable)
    except _Stop:
        pass
    return nc


def _build(nc, enable):

    def din(name, shape):
        return nc.dram_tensor(name, list(shape), F32, kind="ExternalInput").ap()

    def dout(name, shape):
        return nc.dram_tensor(name, list(shape), F32, kind="ExternalOutput").ap()
    x_in = din("x", [NT, 1024])
    w_in = din("w_in", [2, 1024, 7936])
    w_br = din("w_branch", [2, 1024, 1024])
    w_out = din("w_out", [2, 1024, 1024])
    w_g = din("ffn_w_gate", [2, 1024, 3072])
    w_u = din("ffn_w_up", [2, 1024, 3072])
    w_d = din("ffn_w_down", [2, 3072, 1024])
    ppack = din("ppack", [2, 128, NPK])
    gtab = din("gtab", [5, 1024])
    spack = din("spack", [2, 128, NSP])
    wkv_in = din("wkvT", [2, 64, 4 * 4 * 64])
    cs_in = din("cs0", [2, 128, 4 * 4 * 64])
    kcT = din("kcT", [2, 4, 128, 2 * 2048])
    vcd = din("vc", [2, 4, 128, 16 * 256])
    cb_in = din("cbpack", [128, NCB])
    ca_in = din("capack", [128, NCA])
    mp_in = din("maskP", [128, 17 * 128])

    y_out = dout("y", [NT, 1024])
    o_dk = dout("o_dk", [2, NT, 256])
    o_dv = dout("o_dv", [2, NT, 256])
    o_small = dout("o_small", [2, 128, NOS])
    o_wkv = dout("o_wkv", [2, 5, 64, 256])
    o_cs = dout("o_cs", [2, 5, 128, 256])

    xres = nc.dram_tensor("xres", [NT, 1024], F32).ap()
    colsT = nc.dram_tensor("colsT", [7936, NT], F32).ap()
    hmT = nc.dram_tensor("hmT", [3072, NT], BF16).ap()
    gT = nc.dram_tensor("gT", [4096, NT], BF16).ap()

    c = Ctx(nc)
    sb = nc.alloc_sbuf_tensor

    def chk(name):
        if STOP == name:
            c.finish()
            print('STOP at', name, 'instructions', c.n_inst)
            raise _Stop()

    b1 = sb("b1", [128, 8, NT], BF16).ap()
    QT = sb("QT", [128, 2, NT], BF16).ap()
    KT = sb("KT", [128, 2, NT], BF16).ap()
    Vall = sb("Vall", [128, 17, 256], BF16).ap()
    cbf = sb("cbf", [128, NCB], F32).ap()
    cbb = sb("cbb", [128, NCB], BF16).ap()
    caf = sb("caf", [128, NCA], F32).ap()
    mpb = sb("mpb", [128, 17, 128], BF16).ap()
    arena2 = Arena(nc, "arena2", 56 * 1024)
    wslot = [arena2.alloc([6144], BF16) for i in range(2)]
    stg = [arena2.alloc([512]) for i in range(4)]
    xbuf = [arena2.alloc([1024]) for i in range(3)]
    sqj = arena2.alloc([1024])
    ubuf = [arena2.alloc([1024], BF16) for i in range(2)]
    ssb = [sb("ss%d" % i, [128, 4], F32).ap() for i in range(4)]
    gbc = arena2.alloc([1024])
    ppf = sb("ppf", [128, NPK], F32).ap()
    stgb = [sb("stgb%d" % i, [128, 512], BF16).ap() for i in range(2)]
    pmat = sb("pmat", [128, 1280], BF16).ap()
    spf = sb("spf", [128, NSP], F32).ap()
    pder = sb("pder", [128, 16], F32).ap()
    osm = sb("osm", [128, NOS], F32).ap()
    kcbuf = [sb("kcb%d" % i, [128, 2, 512], BF16).ap() for i in range(2)]
    vcbuf = [sb("vcb%d" % i, [128, 4, 256], BF16).ap() for i in range(2)]
    arena = Arena(nc, "arena", 56 * 1024)
    print("sbuf remaining after alloc", nc.sbuf_bytes_remaining)

    psf = [nc.alloc_psum_tensor("ps%d" % i, [128, 512], F32).ap() for i in range(7)]
    psb = nc.alloc_psum_tensor("psb", [128, 1024], BF16).ap()
    pstate = {'i': 0, 'w': 0, 's': 0, 'a': 0}

    def nextps():
        i = pstate['i'] % 5
        pstate['i'] += 1
        return psf[i], ('ps', i)

    def mkps(idx):
        st = {'i': 0}

        def f():
            i = idx[st['i'] % len(idx)]
            st['i'] += 1
            return psf[i], ('ps', i)
        return f

    def nextacc():
        i = 5 + pstate['a'] % 2
        pstate['a'] += 1
        return psf[i], ('ps', i)

    identb = cbb[:, CB['ident']:CB['ident'] + 128]
    identf = cbf[:, CB['ident']:CB['ident'] + 128]
    blk64b = cbb[:, CB['blk64']:CB['blk64'] + 128]
    onesb = cbb[:, CB['ones']:CB['ones'] + 64]
    tri_incl = cbf[0:64, CB['tri_incl']:CB['tri_incl'] + 64]
    tri_su = cbf[0:64, CB['tri_su']:CB['tri_su'] + 64]
    tri_sl = cbf[0:64, CB['tri_sl']:CB['tri_sl'] + 64]
    maskSb = cbb[:, CB['maskS']:CB['maskS'] + 68].rearrange("p (b q) -> p b q", b=17)
    maskNb = cbb[0:16, CB['maskN']:CB['maskN'] + 16].rearrange("p (s q) -> p s q", s=4)
    ropet = caf[:, CA['rope']:CA['rope'] + 17 * 64].rearrange("p (t d) -> p t d", t=17)
    resetm = caf[:, CA['reset']:CA['reset'] + 512]

    def pcol(name, i=0):
        return ppf[:, PK[name] + i:PK[name] + i + 1]

    def k_b1(cs, n0, nn):
        return [('b1', cc, t) for cc in cs for t in range(n0 // 128, (n0 + nn + 127) // 128)]

    c.dma('sp', cbf, cb_in, writes=['cbf'])
    c.dma('pool', cbb, cb_in, writes=['cbb'])
    c.dma('sp', caf, ca_in, writes=['caf'])
    c.dma('pool', mpb.rearrange("p a b -> p (a b)"), mp_in, writes=['mpb'])
    for t, (r0, R) in enumerate(TT128):
        xt = xbuf[t % 3]
        c.dma('sp', xt[:R], x_in[r0:r0 + R, :], writes=[('xt', t % 3)])
        c.dma('sp', xres[r0:r0 + R, :], xt[:R], reads=[('xt', t % 3)], writes=[('xres', t)])

    chk('setup')

    def norm_tile(xt, xk, t, r0, R, final=False):
        ss = ssb[t % 4]
        sk = ('ss', t % 4)
        c.op('pool', lambda e: e.tensor_tensor(out=sqj[:R], in0=xt[:R], in1=xt[:R], op=ALU.mult), [xk], ['sqj'])
        c.op('dve', lambda e: e.reduce_sum(out=ss[:R, 0:1], in_=sqj[:R], axis=AX.X), ['sqj'], [sk])
        c.op('act', lambda e: e.activation(out=ss[:R, 1:2], in_=ss[:R, 0:1], func=AF.Sqrt, bias=1e-6, scale=1.0 / 1024),
             [sk], [sk])
        c.op('dve', lambda e: e.reciprocal(out=ss[:R, 2:3], in_=ss[:R, 1:2]), [sk], [sk])
        if final:
            c.op('dve', lambda e: e.scalar_tensor_tensor(out=sqj[:R], in0=xt[:R], scalar=ss[:R, 2:3], in1=gbc[:R],
                                                         op0=ALU.mult, op1=ALU.mult), [xk, sk, 'gbc'], ['sqj'])
            c.dma('sp', y_out[r0:r0 + R, :], sqj[:R], reads=['sqj'])
            return
        ub = ubuf[t % 2]
        uk = ('ub', t % 2)
        c.op('dve', lambda e: e.scalar_tensor_tensor(out=ub[:R], in0=xt[:R], scalar=ss[:R, 2:3], in1=gbc[:R],
                                                     op0=ALU.mult, op1=ALU.mult), [xk, sk, 'gbc'], [uk])
        for cc in range(8):
            c.op('pe', lambda e: e.transpose(psb[:, cc * 128:cc * 128 + R], ub[:R, cc * 128:(cc + 1) * 128], identb[:R, :R]),
                 [uk, 'cbb'], ['psb'])
        c.op('act', lambda e: e.activation(out=b1[:, :, r0:r0 + R],
                                           in_=psb.rearrange("p (c t) -> p c t", c=8)[:, :, :R], func=AF.Copy),
             ['psb'], [('b1', cc, t) for cc in range(8)])

    def load_gains(gidx):
        c.dma('sp', gbc, gtab[gidx].partition_broadcast(128), writes=['gbc'])

    def norm_phase(gidx, final=False):
        load_gains(gidx)
        for t, (r0, R) in enumerate(TT128):
            xt = xbuf[t % 3]
            xk = ('xt', t % 3)
            c.dma('sp', xt[:R], xres[r0:r0 + R, :], reads=[('xres', t)], writes=[xk])
            norm_tile(xt, xk, t, r0, R, final)

    def getslot():
        i = pstate['w'] % 2
        pstate['w'] += 1
        return wslot[i], ('ws', i)

    def getstg():
        i = pstate['s'] % 4
        pstate['s'] += 1
        return stg[i], ('stg', i)

    def ldcols(dst, row0, nch, n0, T, q='sp'):
        src = colsT[row0:row0 + nch * 128].rearrange("(c p) t -> p c t", p=128)[:, :, n0:n0 + T]
        return src, [('cols', row0 // 128 + cc, n0 // 512) for cc in range(nch)]

    for l in range(DEPTH):
        c.dma('sp', ppf, ppack[l], writes=['ppf'])
        c.dma('pool', pmat, ppack[l][:, PMAT0:PMAT0 + 1280], writes=['pmat'])
        c.dma('sp', spf, spack[l], writes=['spf'])
        c.op('act', lambda e: e.activation(out=pder[:, 0:2], in_=ppf[:, PK['a_lam']:PK['a_lam'] + 2], func=AF.Exp, scale=-1.0),
             ['ppf'], ['pder'])
        c.op('act', lambda e: e.activation(out=pder[:, 0:2], in_=pder[:, 0:2], func=AF.Ln, bias=1.0, scale=1.0), ['pder'], ['pder'])
        c.op('dve', lambda e: e.tensor_scalar(out=pder[:, 2:4], in0=pder[:, 0:2], scalar1=-16.0, scalar2=None, op0=ALU.mult),
             ['pder'], ['pder'])
        c.op('dve', lambda e: e.tensor_scalar(out=pder[:, 0:2], in0=pder[:, 0:2], scalar1=-8.0, scalar2=None, op0=ALU.mult),
             ['pder'], ['pder'])
        if l == 0:
            c.op('dve', lambda e: e.memset(pder[:, 4:8], 0.0), [], ['pder'])
        else:
            c.op('dve', lambda e: e.tensor_tensor(out=pder[:, 4:8], in0=ppf[:, PK['c_lb1']:PK['c_lb1'] + 4],
                                                  in1=ppf[:, PK['c_lb0']:PK['c_lb0'] + 4], op=ALU.subtract), ['ppf'], ['pder'])
            c.op('act', lambda e: e.activation(out=pder[:, 4:8], in_=pder[:, 4:8], func=AF.Sigmoid), ['pder'], ['pder'])
        c.op('dve', lambda e: e.tensor_scalar(out=pder[:, 8:12], in0=pder[:, 4:8], scalar1=-1.0, scalar2=1.0,
                                              op0=ALU.mult, op1=ALU.add), ['pder'], ['pder'])
        c.op('dve', lambda e: e.tensor_scalar(out=pder[:, 12:16], in0=ppf[:, PK['b64_ka']:PK['b64_ka'] + 4], scalar1=-1.0,
                                              scalar2=1.0, op0=ALU.mult, op1=ALU.add), ['ppf'], ['pder'])
        c.op('dve', lambda e: e.memset(osm, 0.0), [], ['osm'])

        if l == 0:
            norm_phase(0)

        chk('norm1')
        wv = w_in[l].rearrange("(k p) c -> p k c", p=128)
        blocks = [(c0, 768) for c0 in (0, 768, 1536, 2304)] + [(3840 + i * 768, 768) for i in range(5)] + [(7680, 256)]
        ev = 0
        for (c0, ncol) in blocks:
            ws, wk = getslot()
            ws3 = ws.rearrange("p (k c) -> p k c", k=8)
            c.dma('pool', ws3[:, :, :ncol], wv[:, :, c0:c0 + ncol], writes=[wk])
            for j, (n0, nn) in enumerate(NTILES):
                for cc in range(ncol // 128):
                    ps, pk = nextps()
                    for k in range(8):
                        c.op('pe', lambda e: e.matmul(ps[:, :nn], lhsT=ws3[:, k, cc * 128:(cc + 1) * 128], rhs=b1[:, k, n0:n0 + nn],
                                                      start=(k == 0), stop=(k == 7)), [wk] + k_b1([k], n0, nn), [pk])
                    st, sk = getstg()
                    if c0 >= 3840:
                        sgb, sgk = stgb[ev % 2], ('stgb', ev % 2)
                        c.op('act', lambda e: e.activation(out=sgb[:, :nn], in_=ps[:, :nn], func=AF.Sigmoid), [pk], [sgk])
                        ev += 1
                        rb = c0 // 128 + cc
                        c.dma('sp', gT[(rb - 30) * 128:(rb - 29) * 128, n0:n0 + nn], sgb[:, :nn], reads=[sgk], writes=[('cols', rb, j)])
                        continue
                    elif ev % 2 == 0:
                        c.op('act', lambda e: e.activation(out=st[:, :nn], in_=ps[:, :nn], func=AF.Copy), [pk], [sk])
                    else:
                        c.op('dve', lambda e: e.tensor_copy(out=st[:, :nn], in_=ps[:, :nn]), [pk], [sk])
                    ev += 1
                    rb = c0 // 128 + cc
                    c.dma('sp', colsT[rb * 128:(rb + 1) * 128, n0:n0 + nn], st[:, :nn], reads=[sk], writes=[('cols', rb, j)])

        chk('inproj')
        c.barrier()
        arena.reset()
        NB3 = 3
        dqk = [arena.alloc([512]) for _ in range(NB3)]
        dvv = [arena.alloc([256]) for _ in range(NB3)]
        rot = [arena.alloc([512]) for _ in range(NB3)]
        rtmp = [arena.alloc([512]) for _ in range(NB3)]
        rbb = [arena.alloc([512], BF16) for _ in range(NB3)]
        ws, wk = getslot()
        ws3 = ws.rearrange("p (k c) -> p k c", k=8)
        c.dma('pool', ws3, wv[:, :, 3072:3840], writes=[wk])

        def qkvA(t, r0, R):
            i2 = t % NB3
            psA, pkA = nextps()
            psB, pkB = nextps()
            for k in range(8):
                c.op('pe', lambda e: e.matmul(psA[:R, :512], lhsT=b1[:, k, r0:r0 + R], rhs=ws3[:, k, 0:512], start=(k == 0), stop=(k == 7)),
                     [wk, ('b1', k, t)], [pkA])
            for k in range(8):
                c.op('pe', lambda e: e.matmul(psB[:R, :256], lhsT=b1[:, k, r0:r0 + R], rhs=ws3[:, k, 512:768], start=(k == 0), stop=(k == 7)),
                     [wk, ('b1', k, t)], [pkB])
            qk, vv = dqk[i2], dvv[i2]
            kq, kv = ('dqk', i2), ('dvv', i2)
            c.op('act', lambda e: e.activation(out=qk[:R], in_=psA[:R, :512], func=AF.Copy), [pkA], [kq])
            c.op('dve', lambda e: e.tensor_copy(out=vv[:R], in_=psB[:R, :256]), [pkB], [kv])
            c.dma('sp', o_dv[l, r0:r0 + R, :], vv[:R], reads=[kv])
            c.op('pool', lambda e: e.tensor_copy(out=Vall[:R, t, :], in_=vv[:R]), [kv], [('Vall', t)])

        def qkvB(t, r0, R):
            i2 = t % NB3
            qk, ro, rt_, rb_ = dqk[i2], rot[i2], rtmp[i2], rbb[i2]
            kq, kr, kt_, kb_ = ('dqk', i2), ('rot', i2), ('rtmp', i2), ('rbb', i2)
            q4 = qk[:R].rearrange("p (h two d) -> p h two d", h=8, two=2)
            r4 = ro[:R].rearrange("p (h two d) -> p h two d", h=8, two=2)
            t4 = rt_[:R].rearrange("p (h two d) -> p h two d", h=8, two=2)
            cosb = ropet[:R, t, 0:32].unsqueeze(1).to_broadcast([R, 8, 32])
            sinb = ropet[:R, t, 32:64].unsqueeze(1).to_broadcast([R, 8, 32])
            c.op('dve', lambda e: e.tensor_tensor(out=r4[:, :, 0, :], in0=q4[:, :, 0, :], in1=cosb, op=ALU.mult), [kq, 'caf'], [kr])
            c.op('dve', lambda e: e.tensor_tensor(out=t4[:, :, 0, :], in0=q4[:, :, 1, :], in1=sinb, op=ALU.mult), [kq, 'caf'], [kt_])
            c.op('dve', lambda e: e.tensor_tensor(out=r4[:, :, 0, :], in0=r4[:, :, 0, :], in1=t4[:, :, 0, :], op=ALU.subtract), [kr, kt_], [kr])
            c.op('dve', lambda e: e.tensor_tensor(out=r4[:, :, 1, :], in0=q4[:, :, 1, :], in1=cosb, op=ALU.mult), [kq, 'caf'], [kr])
            c.op('dve', lambda e: e.tensor_tensor(out=t4[:, :, 1, :], in0=q4[:, :, 0, :], in1=sinb, op=ALU.mult), [kq, 'caf'], [kt_])
            c.op('dve', lambda e: e.tensor_tensor(out=r4[:, :, 1, :], in0=r4[:, :, 1, :], in1=t4[:, :, 1, :], op=ALU.add), [kr, kt_], [kr])
            c.dma('sp', o_dk[l, r0:r0 + R, :], ro[:R, 256:512], reads=[kr])
            c.op('act', lambda e: e.activation(out=rb_[:R], in_=ro[:R], func=AF.Copy), [kr], [kb_])

        def qkvC(t, r0, R):
            i2 = t % NB3
            rb_, kb_ = rbb[i2], ('rbb', i2)
            for i in range(4):
                c.op('pe', lambda e: e.transpose(psb[:, i * 128:i * 128 + R], rb_[:R, i * 128:(i + 1) * 128], identb[:R, :R]),
                     [kb_, 'cbb'], ['psb'])
            pv = psb[:, 0:512].rearrange("p (i t) -> p i t", i=4)
            c.op('act', lambda e: e.activation(out=QT[:, :, r0:r0 + R], in_=pv[:, 0:2, :R], func=AF.Copy), ['psb'], [('QT', t)])
            c.op('dve', lambda e: e.tensor_copy(out=KT[:, :, r0:r0 + R], in_=pv[:, 2:4, :R]), ['psb'], [('KT', t)])
        ntt = len(TT128)
        for it_ in range(ntt + 2):
            if it_ < ntt:
                qkvA(it_, *TT128[it_])
            if 0 <= it_ - 1 < ntt:
                qkvB(it_ - 1, *TT128[it_ - 1])
            if 0 <= it_ - 2 < ntt:
                qkvC(it_ - 2, *TT128[it_ - 2])

        chk('qkv')
        for bi, nm in enumerate('ABCD'):
            if nm not in enable:
                c.op('pool', lambda e: e.memset(b1[:, 2 * bi:2 * bi + 2, :], 0.0), [],
                     [('b1', cc, t) for cc in (2 * bi, 2 * bi + 1) for t in range(17)])

        def gen_D(arena):
            nextps = cad_ps
            nextacc = mkps([6])
            dcnt = {'i': 0}
            esb = [arena.alloc([512], BF16) for _ in range(3)]
            recb = [arena.alloc([256]) for _ in range(2)]
            for qb in range(16):
                pso, pko = nextacc()
                for kb in range(qb + 1):
                    pss2 = [nextps(), nextps()]
                    for h in range(4):
                        hh, hp = h % 2, h // 2
                        pss, pks = pss2[hh]
                        c.op('pe', lambda e: e.matmul(pss[:, hp * 128:(hp + 1) * 128], lhsT=KT[hh * 64:(hh + 1) * 64, hp, kb * 128:(kb + 1) * 128],
                                                      rhs=QT[hh * 64:(hh + 1) * 64, hp, qb * 128:(qb + 1) * 128], start=True, stop=True),
                             [('KT', kb), ('QT', qb)], [pks])
                    dcnt['i'] += 1
                    ei = dcnt['i'] % 3
                    es, ek = esb[ei], ('esb', ei)
                    e3 = es.rearrange("p (h q) -> p h q", h=4)
                    for hh in range(2):
                        pss, pks = pss2[hh]
                        c.op('act', lambda e: e.activation(out=e3[:, hh::2, :], in_=pss[:, 0:256].rearrange("p (a q) -> p a q", a=2),
                                                           func=AF.Exp, scale=0.125), [pks], [ek])
                    c.op('dve', lambda e: e.tensor_tensor(out=e3, in0=e3, in1=mpb[:, qb - kb, :].unsqueeze(1).to_broadcast([128, 4, 128]),
                                                          op=ALU.mult), [ek, 'mpb'], [ek])
                    for h in range(4):
                        hh, hp = h % 2, h // 2
                        c.op('pe', lambda e: e.matmul(pso[hh * 64:(hh + 1) * 64, hp * 128:(hp + 1) * 128], lhsT=Vall[:, kb, h * 64:(h + 1) * 64],
                                                      rhs=es[:, h * 128:(h + 1) * 128], start=(kb == 0 and hp == 0), stop=(kb == qb)),
                             [('Vall', kb), ek], [pko])
                        c.op('pe', lambda e: e.matmul(pso[hh * 64:(hh + 1) * 64, 256 + hp * 128:256 + (hp + 1) * 128], lhsT=onesb,
                                                      rhs=es[:, h * 128:(h + 1) * 128], start=False, stop=(kb == qb)),
                             ['cbb', ek], [pko])
                    yield
                rc, rk = recb[qb % 2], ('recb', qb % 2)
                c.op('dve', lambda e: e.reciprocal(out=rc, in_=pso[:, 256:512]), [pko], [rk])
                c.op('dve', lambda e: e.tensor_tensor(out=b1[:, 6:8, qb * 128:(qb + 1) * 128],
                                                      in0=pso[:, 0:256].rearrange("p (a t) -> p a t", a=2),
                                                      in1=rc.rearrange("p (a t) -> p a t", a=2), op=ALU.mult),
                     [pko, rk], [('b1', 6, qb), ('b1', 7, qb)])
            for s in range(4):
                q0 = 2048 + 4 * s
                pso, pko = nextacc()
                for g in range(4):
                    kct, kck = kcbuf[g % 2], ('kcb', g % 2)
                    vct, vck = vcbuf[g % 2], ('vcb', g % 2)
                    c.dma('pool', kct, kcT[l, s].rearrange("p (a k) -> p a k", a=2)[:, :, g * 512:(g + 1) * 512], writes=[kck])
                    c.dma('pool', vct, vcd[l, s].rearrange("p (b d) -> p b d", b=16)[:, g * 4:(g + 1) * 4, :], writes=[vck])
                    pss2 = [nextps(), nextps()]
                    for kb in range(4):
                        for h in range(4):
                            hh, hp = h % 2, h // 2
                            pss, pks = pss2[hh]
                            o0 = (kb * 2 + hp) * 4
                            c.op('pe', lambda e: e.matmul(pss[:, o0:o0 + 4], lhsT=kct[hh * 64:(hh + 1) * 64, hp, kb * 128:(kb + 1) * 128],
                                                          rhs=QT[hh * 64:(hh + 1) * 64, hp, q0:q0 + 4], start=True, stop=True),
                                 [kck, ('QT', 16)], [pks])
                    dcnt['i'] += 1
                    ei = dcnt['i'] % 3
                    es, ek = esb[ei], ('esb', ei)
                    e5 = es[:, 0:64].rearrange("p (b a x q) -> p b a x q", b=4, a=2, x=2)
                    for hh in range(2):
                        pss, pks = pss2[hh]
                        c.op('act', lambda e: e.activation(out=e5[:, :, :, hh, :], in_=pss[:, 0:32].rearrange("p (b a q) -> p b a q", b=4, a=2),
                                                           func=AF.Exp, scale=0.125), [pks], [ek])
                    e4 = es[:, 0:64].rearrange("p (b h q) -> p b h q", b=4, h=4)
                    c.op('dve', lambda e: e.tensor_tensor(out=e4, in0=e4, in1=maskSb[:, g * 4:(g + 1) * 4, :].unsqueeze(2).to_broadcast([128, 4, 4, 4]),
                                                          op=ALU.mult), [ek, 'cbb'], [ek])
                    for kb in range(4):
                        for h in range(4):
                            hh, hp = h % 2, h // 2
                            o0 = (kb * 4 + h) * 4
                            first = (g == 0 and kb == 0)
                            c.op('pe', lambda e: e.matmul(pso[hh * 64:(hh + 1) * 64, hp * 4:hp * 4 + 4], lhsT=vct[:, kb, h * 64:(h + 1) * 64],
                                                          rhs=es[:, o0:o0 + 4], start=(first and hp == 0), stop=False), [vck, ek], [pko])
                            c.op('pe', lambda e: e.matmul(pso[hh * 64:(hh + 1) * 64, 8 + hp * 4:8 + hp * 4 + 4], lhsT=onesb,
                                                          rhs=es[:, o0:o0 + 4], start=False, stop=False), ['cbb', ek], [pko])
                    yield
                pss2 = [nextps(), nextps()]
                for h in range(4):
                    hh, hp = h % 2, h // 2
                    pss, pks = pss2[hh]
                    c.op('pe', lambda e: e.matmul(pss[0:16, hp * 4:hp * 4 + 4], lhsT=KT[hh * 64:(hh + 1) * 64, hp, 2048:2064],
                                                  rhs=QT[hh * 64:(hh + 1) * 64, hp, q0:q0 + 4], start=True, stop=True),
                         [('KT', 16), ('QT', 16)], [pks])
                dcnt['i'] += 1
                ei = dcnt['i'] % 3
                es, ek = esb[ei], ('esb', ei)
                e3n = es[0:16, 0:16].rearrange("p (a x q) -> p a x q", a=2, x=2)
                for hh in range(2):
                    pss, pks = pss2[hh]
                    c.op('act', lambda e: e.activation(out=e3n[:, :, hh, :], in_=pss[0:16, 0:8].rearrange("p (a q) -> p a q", a=2),
                                                       func=AF.Exp, scale=0.125), [pks], [ek])
                e3 = es[0:16, 0:16].rearrange("p (h q) -> p h q", h=4)
                c.op('dve', lambda e: e.tensor_tensor(out=e3, in0=e3, in1=maskNb[:, s, :].unsqueeze(1).to_broadcast([16, 4, 4]), op=ALU.mult),
                     [ek, 'cbb'], [ek])
                for h in range(4):
                    hh, hp = h % 2, h // 2
                    c.op('pe', lambda e: e.matmul(pso[hh * 64:(hh + 1) * 64, hp * 4:hp * 4 + 4], lhsT=Vall[0:16, 16, h * 64:(h + 1) * 64],
                                                  rhs=es[0:16, h * 4:h * 4 + 4], start=False, stop=True), [('Vall', 16), ek], [pko])
                    c.op('pe', lambda e: e.matmul(pso[hh * 64:(hh + 1) * 64, 8 + hp * 4:8 + hp * 4 + 4], lhsT=onesb[0:16, :],
                                                  rhs=es[0:16, h * 4:h * 4 + 4], start=False, stop=True), ['cbb', ek], [pko])
                rc, rk = recb[s % 2], ('recb', s % 2)
                c.op('dve', lambda e: e.reciprocal(out=rc[:, 0:8], in_=pso[:, 8:16]), [pko], [rk])
                c.op('dve', lambda e: e.tensor_tensor(out=b1[:, 6:8, q0:q0 + 4], in0=pso[:, 0:8].rearrange("p (a t) -> p a t", a=2),
                                                      in1=rc[:, 0:8].rearrange("p (a t) -> p a t", a=2), op=ALU.mult),
                     [pko, rk], [('b1', 6, 16), ('b1', 7, 16)])

        def gen_A(arena):
            nextps = cad_ps
            TA = 128
            axp = arena.alloc([2, TA + 3])
            agt = arena.alloc([2, TA])
            xc = arena.alloc([2, TA])
            xcb = arena.alloc([2, TA], BF16)
            gx = arena.alloc([2, TA])
            ga = arena.alloc([2, TA])
            av = arena.alloc([2, TA])
            bi_ = arena.alloc([2, TA])
            hh_ = arena.alloc([2, TA])
            hst = arena.alloc([2])
            segs = [(n0, TA, 0, n0 == 0, n0 + TA == 2048) for n0 in range(0, 2048, TA)] + \
                   [(2048 + 4 * s, 4, 1 + s, True, True) for s in range(4)]
            for (n0, T, sq, first, last) in segs:
                if first:
                    if sq == 0:
                        c.op('dve', lambda e: e.memset(axp[:, :, 0:3], 0.0), [], ['axp'])
                        c.op('dve', lambda e: e.memset(hst, 0.0), [], ['hst'])
                    else:
                        s = sq - 1
                        c.op('dve', lambda e: e.tensor_copy(out=axp[:, :, 0:3],
                                                            in_=spf[:, SP['a_conv'] + s * 6:SP['a_conv'] + s * 6 + 6].rearrange("p (c j) -> p c j", c=2)),
                             ['spf'], ['axp'])
                        c.op('dve', lambda e: e.tensor_copy(out=hst, in_=spf[:, SP['a_h'] + s * 2:SP['a_h'] + s * 2 + 2]), ['spf'], ['hst'])
                else:
                    c.op('dve', lambda e: e.tensor_copy(out=axp[:, :, 0:3], in_=axp[:, :, TA:TA + 3]), ['axp'], ['axp'])
                src, ks = ldcols(None, 0, 2, n0, T)
                c.dma('sp', axp[:, :, 3:3 + T], src, reads=ks, writes=['axp'])
                src, ks = ldcols(None, 256, 2, n0, T)
                c.dma('sp', agt[:, :, :T], src, reads=ks, writes=['agt'])
                for cc in range(2):
                    c.op('dve', lambda e: e.tensor_scalar(out=xc[:, cc, :T], in0=axp[:, cc, 0:T], scalar1=pcol('a_cw', cc * 4),
                                                          scalar2=pcol('a_cb', cc), op0=ALU.mult, op1=ALU.add), ['axp', 'ppf'], ['xc'])
                    for j in range(1, 4):
                        c.op('dve', lambda e: e.scalar_tensor_tensor(out=xc[:, cc, :T], in0=axp[:, cc, j:j + T], scalar=pcol('a_cw', cc * 4 + j),
                                                                     in1=xc[:, cc, :T], op0=ALU.mult, op1=ALU.add), ['axp', 'ppf', 'xc'], ['xc'])
                c.op('act', lambda e: e.activation(out=xcb[:, :, :T], in_=xc[:, :, :T], func=AF.Copy), ['xc'], ['xcb'])
                yield
                psx, pkx = nextps()
                for cc in range(2):
                    c.op('pe', lambda e: e.matmul(psx[:, cc * 128:cc * 128 + T], lhsT=pmat[:, cc * 128:(cc + 1) * 128], rhs=xcb[:, cc, :T],
                                                  start=True, stop=True), ['pmat', 'xcb'], [pkx])
                for cc in range(2):
                    c.op('pe', lambda e: e.matmul(psx[:, 256 + cc * 128:256 + cc * 128 + T], lhsT=pmat[:, 256 + cc * 128:256 + (cc + 1) * 128], rhs=xcb[:, cc, :T],
                                                  start=True, stop=True), ['pmat', 'xcb'], [pkx])
                for cc in range(2):
                    c.op('act', lambda e: e.activation(out=gx[:, cc, :T], in_=psx[:, cc * 128:cc * 128 + T], func=AF.Sigmoid, bias=pcol('a_gxb', cc), scale=1.0),
                         [pkx, 'ppf'], ['gx'])
                    c.op('act', lambda e: e.activation(out=ga[:, cc, :T], in_=psx[:, 256 + cc * 128:256 + cc * 128 + T], func=AF.Sigmoid, bias=pcol('a_gab', cc), scale=1.0),
                         [pkx, 'ppf'], ['ga'])
                yield
                for cc in range(2):
                    c.op('act', lambda e: e.activation(out=av[:, cc, :T], in_=ga[:, cc, :T], func=AF.Exp, scale=pder[:, cc:cc + 1]), ['ga', 'pder'], ['av'])
                    c.op('act', lambda e: e.activation(out=bi_[:, cc, :T], in_=ga[:, cc, :T], func=AF.Exp, scale=pder[:, 2 + cc:3 + cc]), ['ga', 'pder'], ['bi'])
                c.op('act', lambda e: e.activation(out=bi_[:, :, :T], in_=bi_[:, :, :T], func=AF.Sqrt, bias=1.0, scale=-1.0), ['bi'], ['bi'])
                c.op('dve', lambda e: e.tensor_tensor(out=bi_[:, :, :T], in0=bi_[:, :, :T], in1=gx[:, :, :T], op=ALU.mult), ['bi', 'gx'], ['bi'])
                c.op('dve', lambda e: e.tensor_tensor(out=bi_[:, :, :T], in0=bi_[:, :, :T], in1=xc[:, :, :T], op=ALU.mult), ['bi', 'xc'], ['bi'])
                for cc in range(2):
                    c.op('dve', lambda e: e.tensor_tensor_scan(out=hh_[:, cc, :T], data0=av[:, cc, :T], data1=bi_[:, cc, :T], initial=hst[:, cc:cc + 1],
                                                               op0=ALU.mult, op1=ALU.add), ['av', 'bi', 'hst'], ['hh'])
                c.op('dve', lambda e: e.tensor_copy(out=hst, in_=hh_[:, :, T - 1]), ['hh'], ['hst'])
                yield
                c.op('act', lambda e: e.activation(out=agt[:, :, :T], in_=agt[:, :, :T], func=AF.Gelu_apprx_tanh), ['agt'], ['agt'])
                c.op('dve', lambda e: e.tensor_tensor(out=b1[:, 0:2, n0:n0 + T], in0=hh_[:, :, :T], in1=agt[:, :, :T], op=ALU.mult),
                     ['hh', 'agt'], k_b1([0, 1], n0, T))
                if last:
                    c.op('dve', lambda e: e.tensor_copy(out=osm[:, OS['a_h'] + sq * 2:OS['a_h'] + sq * 2 + 2], in_=hst), ['hst'], ['osm'])
                    c.op('dve', lambda e: e.tensor_copy(out=osm[:, OS['a_conv'] + sq * 6:OS['a_conv'] + sq * 6 + 6].rearrange("p (c j) -> p c j", c=2),
                                                        in_=axp[:, :, T:T + 3]), ['axp'], ['osm'])

        c.barrier()
        arena.reset()
        arena2.reset()
        gens = []
        cad_ps = mkps([4, 5])
        if 'B' in enable:
            ST_ = arena2.alloc([4, 64])
            STb_ = arena2.alloc([4, 64], BF16)
            tok = {'next': 0}
            for tid in range(2):
                bps = mkps([0, 1] if tid == 0 else [2, 3])
                gens.append(build_B(c, nc, l, arena, bps, psb, b1, colsT, ppf, pder, spf, pmat, wkv_in, o_wkv, osm, cbf, cbb, caf, k_b1, ldcols, pcol,
                                    tid, ST_, STb_, tok, arena2))
        if 'C' in enable:
            gens.append(build_C(c, nc, l, arena2, cad_ps, psb, b1, colsT, ppf, pder, spf, cs_in, o_cs, cbf, cbb, caf, k_b1, ldcols, pcol))
        if 'D' in enable:
            gens.append(gen_D(arena2))
        if 'A' in enable:
            gens.append(gen_A(arena2))
        while gens:
            for g in list(gens):
                try:
                    next(g)
                except StopIteration:
                    gens.remove(g)
        chk('B')
        c.barrier()
        arena.reset()
        wo = arena.alloc([8, 1024], BF16)
        gtb = [arena.alloc([4, 512], BF16) for _ in range(3)]
        prodb = [arena.alloc([4, 512]) for _ in range(2)]
        mg = arena.alloc([8, 512], BF16)
        wbr = []
        for i in range(2):
            ws, wk = getslot()
            w4 = ws[:, 0:4096].rearrange("p (k c) -> p k c", k=4)
            c.dma('pool', w4, w_br[l].rearrange("(k p) c -> p k c", p=128)[:, i * 4:(i + 1) * 4, :], writes=[wk])
            wbr.append((w4, wk))
        c.dma('pool', wo, w_out[l].rearrange("(k p) c -> p k c", p=128), writes=['wo'])
        gview = gT.rearrange("(n c p) t -> p n c t", n=4, c=8, p=128)
        gi = 0
        load_gains(2 + l)
        pendN = [None]
        for j, (n0, nn) in enumerate(NTILES):
            for dj in range(8):
                gt, gk = gtb[gi % 3], ('gtb', gi % 3)
                pr, prk = prodb[gi % 2], ('prodb', gi % 2)
                gi += 1
                c.dma('sp', gt[:, :, :nn], gview[:, :, dj, n0:n0 + nn], reads=[('cols', 30 + n * 8 + dj, j) for n in range(4)], writes=[gk])
                for n in range(4):
                    ps, pk = nextps()
                    w4, wk = wbr[n // 2]
                    for kc in range(2):
                        c.op('pe', lambda e: e.matmul(ps[:, :nn], lhsT=w4[:, (n % 2) * 2 + kc, dj * 128:(dj + 1) * 128], rhs=b1[:, 2 * n + kc, n0:n0 + nn],
                                                      start=(kc == 0), stop=(kc == 1)), [wk] + k_b1([2 * n + kc], n0, nn), [pk])
                    c.op('dve', lambda e: e.tensor_tensor(out=pr[:, n, :nn], in0=ps[:, :nn], in1=gt[:, n, :nn], op=ALU.mult), [pk, gk], [prk])
                c.op('pool', lambda e: e.tensor_tensor(out=pr[:, 0, :nn], in0=pr[:, 0, :nn], in1=pr[:, 1, :nn], op=ALU.add), [prk], [prk])
                c.op('pool', lambda e: e.tensor_tensor(out=pr[:, 2, :nn], in0=pr[:, 2, :nn], in1=pr[:, 3, :nn], op=ALU.add), [prk], [prk])
                c.op('pool', lambda e: e.tensor_tensor(out=mg[:, dj, :nn], in0=pr[:, 0, :nn], in1=pr[:, 2, :nn], op=ALU.add), [prk], [('mg', dj)])
            for st in range((nn + 127) // 128):
                R = min(128, nn - st * 128)
                t = n0 // 128 + st
                xt, xk = xbuf[t % 3], ('xt', t % 3)
                c.dma('sp', xt[:R], xres[n0 + st * 128:n0 + st * 128 + R, :], reads=[('xres', t)], writes=[xk])
                for half in range(2):
                    ps, pk = nextps()
                    for kc in range(8):
                        c.op('pe', lambda e: e.matmul(ps[:R, :512], lhsT=mg[:, kc, st * 128:st * 128 + R], rhs=wo[:, kc, half * 512:(half + 1) * 512],
                                                      start=(kc == 0), stop=(kc == 7)), [('mg', kc), 'wo'], [pk])
                    c.op('dve', lambda e: e.tensor_tensor(out=xt[:R, half * 512:(half + 1) * 512], in0=ps[:R, :512],
                                                          in1=xt[:R, half * 512:(half + 1) * 512], op=ALU.add), [pk, xk], [xk])
                c.dma('sp', xres[n0 + st * 128:n0 + st * 128 + R, :], xt[:R], reads=[xk], writes=[('xres', t)])
                if pendN[0] is not None:
                    pendN[0]()
                pendN[0] = (lambda xt=xt, xk=xk, t=t, r0=n0 + st * 128, R=R: norm_tile(xt, xk, t, r0, R))

        if pendN[0] is not None:
            pendN[0]()
        chk('merge')
        c.barrier()
        arena.reset()
        hgp = [arena.alloc([516]) for _ in range(2)]
        hgs = [arena.alloc([4, 6]) for _ in range(2)]
        cvb = [arena.alloc([512]) for _ in range(3)]
        hmo = [arena.alloc([512], BF16) for _ in range(3)]
        hub = [arena.alloc([512], BF16) for _ in range(3)]
        pendB = [None]
        wgv = w_g[l].rearrange("(k p) c -> p k c", p=128)
        wuv = w_u[l].rearrange("(k p) c -> p k c", p=128)
        hi = 0
        def load_fb(fb):
            ws, wk = getslot()
            w5 = ws[:, 0:4096].rearrange("p (g k c) -> p g k c", g=2, k=8)
            c.dma('pool', w5[:, 0], wgv[:, :, fb * 256:(fb + 1) * 256], writes=[wk])
            c.dma('pool', w5[:, 1], wuv[:, :, fb * 256:(fb + 1) * 256], writes=[wk])
            return w5, wk
        nxtw = load_fb(0)
        for fb in range(12):
            w5, wk = nxtw
            if fb + 1 < 12:
                nxtw = load_fb(fb + 1)
            for fc in range(2):
                f = fb * 2 + fc
                hg, hk = hgp[fc], ('hgp', fc)
                hs, hsk = hgs[fc], ('hgs', fc)
                c.op('pool', lambda e: e.memset(hg[:, 0:2], 0.0), [], [hk])
                for j, (n0, nn) in enumerate(NTILES):
                    psg, pkg = nextps()
                    psu, pku = nextps()
                    for k in range(8):
                        c.op('pe', lambda e: e.matmul(psg[:, :nn], lhsT=w5[:, 0, k, fc * 128:(fc + 1) * 128], rhs=b1[:, k, n0:n0 + nn],
                                                      start=(k == 0), stop=(k == 7)), [wk] + k_b1([k], n0, nn), [pkg])
                    for k in range(8):
                        c.op('pe', lambda e: e.matmul(psu[:, :nn], lhsT=w5[:, 1, k, fc * 128:(fc + 1) * 128], rhs=b1[:, k, n0:n0 + nn],
                                                      start=(k == 0), stop=(k == 7)), [wk] + k_b1([k], n0, nn), [pku])
                    cv, ck = cvb[hi % 3], ('cvb', hi % 3)
                    ho, hok = hmo[hi % 3], ('hmo', hi % 3)
                    hi += 1
                    w0, w1, w2_, bb = pcol('f_cw', f * 3), pcol('f_cw', f * 3 + 1), pcol('f_cw', f * 3 + 2), pcol('f_cb', f)
                    hu, huk = hub[hi % 3], ('hub', hi % 3)
                    c.op('act', lambda e: e.activation(out=hu[:, :nn], in_=psu[:, :nn], func=AF.Copy), [pku], [huk])
                    if j < 4:
                        c.op('act', lambda e: e.activation(out=hg[:, 2:2 + nn], in_=psg[:, :nn], func=AF.Copy), [pkg], [hk])
                        c.op('dve', lambda e: e.tensor_scalar(out=cv[:, :nn], in0=hg[:, 0:nn], scalar1=w0, scalar2=bb, op0=ALU.mult, op1=ALU.add),
                             [hk, 'ppf'], [ck])
                        c.op('dve', lambda e: e.scalar_tensor_tensor(out=cv[:, :nn], in0=hg[:, 1:1 + nn], scalar=w1, in1=cv[:, :nn], op0=ALU.mult, op1=ALU.add),
                             [hk, 'ppf', ck], [ck])
                        c.op('dve', lambda e: e.scalar_tensor_tensor(out=cv[:, :nn], in0=hg[:, 2:2 + nn], scalar=w2_, in1=cv[:, :nn], op0=ALU.mult, op1=ALU.add),
                             [hk, 'ppf', ck], [ck])
                        if j == 3:
                            c.op('pool', lambda e: e.tensor_copy(out=osm[:, OS['f_conv'] + f * 2:OS['f_conv'] + f * 2 + 2], in_=hg[:, nn:nn + 2]), [hk], ['osm'])
                        else:
                            c.op('pool', lambda e: e.tensor_copy(out=hg[:, 0:2], in_=hg[:, nn:nn + 2]), [hk], [hk])
                    else:
                        c.op('pool', lambda e: e.tensor_copy(out=hs[:, :, 0:2],
                                                             in_=spf[:, SP['f_conv']:SP['f_conv'] + 192].rearrange("p (s f j) -> p s f j", s=4, f=24)[:, :, f, :]),
                             ['spf'], [hsk])
                        c.op('act', lambda e: e.activation(out=hs[:, :, 2:6], in_=psg[:, 0:16].rearrange("p (s i) -> p s i", s=4), func=AF.Copy), [pkg], [hsk])
                        cv3 = cv[:, 0:16].rearrange("p (s i) -> p s i", s=4)
                        c.op('dve', lambda e: e.tensor_scalar(out=cv3, in0=hs[:, :, 0:4], scalar1=w0, scalar2=bb, op0=ALU.mult, op1=ALU.add),
                             [hsk, 'ppf'], [ck])
                        c.op('dve', lambda e: e.scalar_tensor_tensor(out=cv3, in0=hs[:, :, 1:5], scalar=w1, in1=cv3, op0=ALU.mult, op1=ALU.add),
                             [hsk, 'ppf', ck], [ck])
                        c.op('dve', lambda e: e.scalar_tensor_tensor(out=cv3, in0=hs[:, :, 2:6], scalar=w2_, in1=cv3, op0=ALU.mult, op1=ALU.add),
                             [hsk, 'ppf', ck], [ck])
                        for s in range(4):
                            o0 = OS['f_conv'] + (1 + s) * 48 + f * 2
                            c.op('pool', lambda e: e.tensor_copy(out=osm[:, o0:o0 + 2], in_=hs[:, s, 4:6]), [hsk], ['osm'])
                    def stageB(cv=cv, ck=ck, ho=ho, hok=hok, hu=hu, huk=huk, f=f, j=j, n0=n0, nn=nn):
                        c.op('act', lambda e: e.activation(out=cv[:, :nn], in_=cv[:, :nn], func=AF.Gelu_apprx_tanh), [ck], [ck])
                        c.op('dve', lambda e: e.tensor_tensor(out=ho[:, :nn], in0=hu[:, :nn], in1=cv[:, :nn], op=ALU.mult), [huk, ck], [hok])
                        c.dma('sp', hmT[f * 128:(f + 1) * 128, n0:n0 + nn], ho[:, :nn], reads=[hok], writes=[('hmT', f, j)])
                    if pendB[0] is not None:
                        pendB[0]()
                    pendB[0] = stageB
        if pendB[0] is not None:
            pendB[0]()
        chk('ffn1')
        c.barrier()
        arena.reset()
        hm2 = arena.alloc([24, 1040], BF16)
        xqb = [arena.alloc([256]) for _ in range(4)]
        wdv = w_d[l].rearrange("(f p) c -> p f c", p=128)
        hmv = hmT.rearrange("(f p) t -> p f t", p=128)
        lastl = (l == DEPTH - 1)
        pendF = [None]
        load_gains(4 if lastl else l + 1)
        NT2 = [(0, 1024), (1024, 1040)]
        wseq = [(j_, q_, k_) for j_ in range(2) for q_ in range(4) for k_ in range(2)]

        def load_wd(i_):
            j_, q_, k_ = wseq[i_]
            ws, wk = getslot()
            w3 = ws[:, 0:3072].rearrange("p (f c) -> p f c", f=12)
            c.dma('pool', w3, wdv[:, k_ * 12:(k_ + 1) * 12, q_ * 256:(q_ + 1) * 256], writes=[wk])
            return w3, wk
        wq = 0
        nxtw = load_wd(0)
        xi = 0
        for j2, (n0, nn) in enumerate(NT2):
            jold = sorted(set(range(n0 // 512, min(4, (n0 + nn - 1) // 512) + 1)) | ({4} if n0 + nn > 2048 else set()))
            c.dma('sp', hm2[:, :, :nn], hmv[:, :, n0:n0 + nn], reads=[('hmT', f, j_) for f in range(24) for j_ in jold], writes=['hm2'])
            accs = [(a_, min(128, nn - a_ * 128)) for a_ in range((nn + 127) // 128)]
            for q in range(4):
                for kg in range(2):
                    w3, wk = nxtw
                    wq += 1
                    if wq < len(wseq):
                        nxtw = load_wd(wq)
                    for fk in range(12):
                        for a_, R in accs:
                            bnk = a_ // 2
                            co = (a_ % 2) * 256
                            c.op('pe', lambda e: e.matmul(psf[bnk][:R, co:co + 256], lhsT=hm2[:, kg * 12 + fk, a_ * 128:a_ * 128 + R], rhs=w3[:, fk, :],
                                                          start=(kg == 0 and fk == 0 and a_ % 2 == 0), stop=(kg == 1 and fk == 11)),
                                 ['hm2', wk], [('ps', bnk)])
                for a_, R in accs:
                    bnk = a_ // 2
                    co = (a_ % 2) * 256
                    t = n0 // 128 + a_
                    r0 = n0 + a_ * 128
                    xq, xqk = xqb[xi % 4], ('xqb', xi % 4)
                    xi += 1
                    c.dma('sp', xq[:R], xres[r0:r0 + R, q * 256:(q + 1) * 256], reads=[('xres', t)], writes=[xqk])
                    c.op('dve', lambda e: e.tensor_tensor(out=xq[:R], in0=psf[bnk][:R, co:co + 256], in1=xq[:R], op=ALU.add), [('ps', bnk), xqk], [xqk])
                    c.dma('sp', xres[r0:r0 + R, q * 256:(q + 1) * 256], xq[:R], reads=[xqk], writes=[('xres', t)])
            for a_, R in accs:
                t = n0 // 128 + a_
                r0 = n0 + a_ * 128
                xt, xk = xbuf[t % 3], ('xt', t % 3)
                c.dma('sp', xt[:R], xres[r0:r0 + R, :], reads=[('xres', t)], writes=[xk])
                if pendF[0] is not None:
                    pendF[0]()
                pendF[0] = (lambda xt=xt, xk=xk, t=t, r0=r0, R=R: norm_tile(xt, xk, t, r0, R, final=lastl))
        if pendF[0] is not None:
            pendF[0]()
            pendF[0] = None
        chk('ffn2')
        c.dma('sp', o_small[l], osm, reads=['osm'])
        c.barrier()

    c.finish()
    print("instructions", c.n_inst, "waits", c.n_wait)


def build_C(c, nc, l, arena, nextps, psb, b1, colsT, ppf, pder, spf, cs_in, o_cs, cbf, cbb, caf, k_b1, ldcols, pcol):
    TC = 128
    identb = cbb[:, CB['ident']:CB['ident'] + 128]
    blk64b = cbb[:, CB['blk64']:CB['blk64'] + 128]
    tri_incl = cbf[0:64, CB['tri_incl']:CB['tri_incl'] + 64]
    resetm = caf[:, CA['reset']:CA['reset'] + 512]
    qf = arena.alloc([4, TC])
    ff = arena.alloc([4, TC])
    kf = arena.alloc([4, TC])
    bc = arena.alloc([4, TC])
    ep = arena.alloc([4, TC])
    vf = arena.alloc([2, TC])
    gf = arena.alloc([2, TC])
    oraw = arena.alloc([2, TC])
    rs = arena.alloc([2, TC])
    qt = arena.alloc([4, TC], BF16)
    kt = arena.alloc([4, TC], BF16)
    vb = arena.alloc([2, TC], BF16)
    sqb = arena.alloc([2, TC], BF16)
    elb = arena.alloc([4, 4])
    attb = [arena.alloc([4, 64], BF16) for _ in range(2)]
    tmb = [arena.alloc([768], BF16) for _ in range(2)]
    S = arena.alloc([4, 64])
    Sb = arena.alloc([4, 64], BF16)
    segs = [(n0, TC, 0, n0 == 0, n0 + TC == 2048) for n0 in range(0, 2048, TC)] + \
           [(2048 + 4 * s, 4, 1 + s, True, True) for s in range(4)]
    it = 0
    for (n0, T, sq, first, last) in segs:
        CL = 64 if T >= 64 else T
        nck = T // CL
        if first:
            if sq == 0:
                c.op('dve', lambda e: e.memset(S, 0.0), [], ['S'])
            else:
                c.dma('sp', S.rearrange("p h v -> p (h v)"), cs_in[l][:, (sq - 1) * 256:sq * 256], writes=['S'])
            c.op('act', lambda e: e.activation(out=Sb, in_=S, func=AF.Copy), ['S'], ['Sb'])
        src, ks = ldcols(None, 1536, 4, n0, T)
        c.dma('sp', qf[:, :, :T], src, reads=ks, writes=['qf'])
        src, ks = ldcols(None, 2048, 4, n0, T)
        c.dma('sp', ff[:, :, :T], src, reads=ks, writes=['ff'])
        src, ks = ldcols(None, 2560, 2, n0, T)
        c.dma('sp', vf[:, :, :T], src, reads=ks, writes=['vf'])
        src, ks = ldcols(None, 2816, 2, n0, T)
        c.dma('sp', gf[:, :, :T], src, reads=ks, writes=['gf'])
        yield
        c.op('act', lambda e: e.activation(out=ff[:, :, :T], in_=ff[:, :, :T], func=AF.Sigmoid), ['ff'], ['ff'])
        yield
        for h in range(4):
            c.op('dve', lambda e: e.tensor_scalar(out=ff[:, h, :T], in0=ff[:, h, :T], scalar1=pder[:, 8 + h:9 + h], scalar2=pder[:, 4 + h:5 + h],
                                                  op0=ALU.mult, op1=ALU.add), ['ff', 'pder'], ['ff'])
        yield
        c.op('dve', lambda e: e.tensor_scalar(out=kf[:, :, :T], in0=ff[:, :, :T], scalar1=-1.0, scalar2=1.0, op0=ALU.mult, op1=ALU.add), ['ff'], ['kf'])
        yield
        c.op('act', lambda e: e.activation(out=ff[:, :, :T], in_=ff[:, :, :T], func=AF.Ln), ['ff'], ['ff'])
        yield
        for h in range(4):
            c.op('dve', lambda e: e.tensor_tensor_scan(out=bc[:, h, :T], data0=resetm[:, 0:T], data1=ff[:, h, :T], initial=0.0,
                                                       op0=ALU.mult, op1=ALU.add), ['ff', 'caf'], ['bc'])
        yield
        c.op('act', lambda e: e.activation(out=ep[:, :, :T], in_=bc[:, :, :T], func=AF.Exp), ['bc'], ['ep'])
        yield
        c.op('dve', lambda e: e.tensor_tensor(out=qt[:, :, :T], in0=qf[:, :, :T], in1=ep[:, :, :T], op=ALU.mult), ['qf', 'ep'], ['qt'])
        yield
        c.op('dve', lambda e: e.tensor_copy(out=elb[:, :, 0:nck], in_=ep[:, :, CL - 1:T:CL]), ['ep'], ['elb'])
        yield
        c.op('act', lambda e: e.activation(out=bc[:, :, :T], in_=bc[:, :, :T], func=AF.Exp, scale=-1.0), ['bc'], ['bc'])
        yield
        c.op('dve', lambda e: e.tensor_tensor(out=kt[:, :, :T], in0=kf[:, :, :T], in1=bc[:, :, :T], op=ALU.mult), ['kf', 'bc'], ['kt'])
        yield
        c.op('act', lambda e: e.activation(out=vb[:, :, :T], in_=vf[:, :, :T], func=AF.Copy), ['vf'], ['vb'])
        yield
        for ck in range(nck):
            cs_ = slice(ck * CL, (ck + 1) * CL)
            at, atk = attb[it % 2], ('attb', it % 2)
            tm, tmk = tmb[it % 2], ('tmb', it % 2)
            it += 1
            pa, pka = nextps()
            for h in range(4):
                c.op('pe', lambda e: e.matmul(pa[0:CL, h * 64:h * 64 + CL], lhsT=kt[:, h, cs_], rhs=qt[:, h, cs_], start=True, stop=True),
                     ['kt', 'qt'], [pka])
            c.op('dve', lambda e: e.tensor_tensor(out=at[0:CL, :, 0:CL], in0=pa[0:CL, 0:256].rearrange("p (h t) -> p h t", h=4)[:, :, 0:CL],
                                                  in1=tri_incl[0:CL, 0:CL].unsqueeze(1).to_broadcast([CL, 4, CL]), op=ALU.mult), [pka, 'cbf'], [atk])
            for h in range(4):
                c.op('pe', lambda e: e.transpose(psb[0:CL, h * 128:(h + 1) * 128], kt[:, h, cs_], identb), ['kt', 'cbb'], ['psb'])
            for hp in range(2):
                c.op('pe', lambda e: e.transpose(psb[0:CL, 512 + hp * 128:512 + (hp + 1) * 128], vb[:, hp, cs_], identb), ['vb', 'cbb'], ['psb'])
            c.op('act', lambda e: e.activation(out=tm[0:CL, :], in_=psb[0:CL, 0:768], func=AF.Copy), ['psb'], [tmk])
            yield
            po, pko = nextps()
            for h in range(4):
                hh, hp = h % 2, h // 2
                c.op('pe', lambda e: e.matmul(po[hh * 64:(hh + 1) * 64, hp * 64:hp * 64 + CL], lhsT=Sb[:, h, :], rhs=qt[:, h, cs_], start=True, stop=False),
                     ['Sb', 'qt'], [pko])
                c.op('pe', lambda e: e.matmul(po[hh * 64:(hh + 1) * 64, hp * 64:hp * 64 + CL], lhsT=tm[0:CL, 512 + h * 64:512 + (h + 1) * 64],
                                              rhs=at[0:CL, h, 0:CL], start=False, stop=True), [tmk, atk], [pko])
            c.op('act', lambda e: e.activation(out=oraw[:, :, cs_], in_=po[:, 0:128].rearrange("p (a t) -> p a t", a=2)[:, :, 0:CL], func=AF.Copy),
                 [pko], ['oraw'])
            pS, pkS = nextps()
            for h in range(4):
                c.op('pe', lambda e: e.matmul(pS[:, h * 64:(h + 1) * 64], lhsT=tm[0:CL, h * 128:(h + 1) * 128], rhs=tm[0:CL, 512 + h * 64:512 + (h + 1) * 64],
                                              start=True, stop=True), [tmk], [pkS])
            c.op('dve', lambda e: e.tensor_tensor(out=S, in0=pS[:, 0:256].rearrange("p (h v) -> p h v", h=4), in1=S, op=ALU.add), [pkS, 'S'], ['S'])
            c.op('dve', lambda e: e.tensor_tensor(out=S, in0=S, in1=elb[:, :, ck:ck + 1].to_broadcast([128, 4, 64]), op=ALU.mult), ['S', 'elb'], ['S'])
            c.op('act', lambda e: e.activation(out=Sb, in_=S, func=AF.Copy), ['S'], ['Sb'])
            yield
        c.op('pool', lambda e: e.tensor_tensor(out=sqb[:, :, :T], in0=oraw[:, :, :T], in1=oraw[:, :, :T], op=ALU.mult), ['oraw'], ['sqb'])
        pn, pkn = nextps()
        for hp in range(2):
            c.op('pe', lambda e: e.matmul(pn[:, hp * 256:hp * 256 + T], lhsT=blk64b, rhs=sqb[:, hp, :T], start=True, stop=True), ['sqb', 'cbb'], [pkn])
        c.op('act', lambda e: e.activation(out=rs[:, :, :T], in_=pn.rearrange("p (a t) -> p a t", a=2)[:, :, :T], func=AF.Ln, bias=1e-6, scale=1.0 / 64),
             [pkn], ['rs'])
        c.op('act', lambda e: e.activation(out=rs[:, :, :T], in_=rs[:, :, :T], func=AF.Exp, scale=-0.5), ['rs'], ['rs'])
        c.op('dve', lambda e: e.tensor_tensor(out=rs[:, :, :T], in0=rs[:, :, :T], in1=oraw[:, :, :T], op=ALU.mult), ['rs', 'oraw'], ['rs'])
        c.op('act', lambda e: e.activation(out=gf[:, :, :T], in_=gf[:, :, :T], func=AF.Silu), ['gf'], ['gf'])
        c.op('dve', lambda e: e.scalar_tensor_tensor(out=b1[:, 4:6, n0:n0 + T], in0=rs[:, :, :T], scalar=pcol('c_ng'), in1=gf[:, :, :T],
                                                     op0=ALU.mult, op1=ALU.mult), ['rs', 'gf', 'ppf'], k_b1([4, 5], n0, T))
        if last:
            c.dma('pool', o_cs[l, sq], S.rearrange("p h v -> p (h v)"), reads=['S'])
        yield


def build_B(c, nc, l, arena, nextps, psb, b1, colsT, ppf, pder, spf, pmat, wkv_in, o_wkv, osm, cbf, cbb, caf, k_b1, ldcols, pcol,
            tid, ST, STb, tok, xarena):
    c = CtxTag(c, 'B%d' % tid)
    TB = 64
    identb = cbb[:, CB['ident']:CB['ident'] + 128]
    identf = cbf[:, CB['ident']:CB['ident'] + 128]
    ones64 = cbb[0:64, CB['ones']:CB['ones'] + 64]
    mask2 = cbf[0:64, CB['tri_incl']:CB['tri_incl'] + 128].rearrange("p (a t) -> p a t", a=2)
    tri_sl = cbf[0:64, CB['tri_sl']:CB['tri_sl'] + 64]
    resetm = caf[:, CA['reset']:CA['reset'] + 512]
    w2b = pmat[0:64, 512:768]
    a2b = pmat[0:64, 768:1024]
    g2b = pmat[:, 1024:1280]

    def p64(name, n):
        return ppf[0:64, PK[name]:PK[name] + n]
    Xb = [xarena.alloc([14, TB + 1]) for _ in range(2)]
    Gb = [xarena.alloc([TB + 1]) for _ in range(2)]
    CM = arena.alloc([14, TB])
    gm = arena.alloc([TB])
    tw = arena.alloc([TB], BF16)
    alb = arena.alloc([TB], BF16)
    gs = arena.alloc([TB], BF16)
    ld = arena.alloc([4, TB])
    asg = arena.alloc([4, TB])
    gg = arena.alloc([4, TB])
    kk = arena.alloc([4, TB])
    km = arena.alloc([4, TB])
    cs = arena.alloc([4, TB])
    eb = arena.alloc([4, TB])
    t1 = arena.alloc([4, TB])
    yraw = arena.alloc([4, TB])
    AR = arena.alloc([4, 1, 128], BF16)
    Bt = arena.alloc([4, TB], BF16)
    Kt = arena.alloc([4, TB], BF16)
    Vt = arena.alloc([4, TB], BF16)
    h16 = arena.alloc([4, TB], BF16)
    outB = arena.alloc([4, TB], BF16)
    PC = arena.alloc([4, 2])
    tmb = [arena.alloc([768], BF16) for _ in range(1)]
    G1 = arena.alloc([4, 128], BF16)
    G2 = arena.alloc([4, 128], BF16)
    Wb = [arena.alloc([4, 192], BF16) for _ in range(2)]
    Zs = arena.alloc([4, 64], BF16)
    Us = arena.alloc([4, 64], BF16)
    allsegs = [(n0, TB, 0, n0 == 0, n0 + TB == 2048) for n0 in range(0, 2048, TB)] + \
              [(2048 + 4 * s, 4, 1 + s, True, True) for s in range(4)]
    it = 0
    mysegs = [(gi, sg) for gi, sg in enumerate(allsegs) if gi % 2 == tid]
    xsrc = colsT[512:1408].rearrange("(j k) t -> k j t", k=64)

    def issue_load(k):
        gi, (n0, T, sq, first, last) = mysegs[k]
        X, G, kx, kg = Xb[k % 2], Gb[k % 2], ('X', k % 2), ('G', k % 2)
        if first:
            if sq == 0:
                c.op('pool', lambda e: e.memset(X[0:64, :, 0:1], 0.0), [], [kx])
                c.op('pool', lambda e: e.memset(G[:, 0:1], 0.0), [], [kg])
            else:
                s_ = sq - 1
                c.op('pool', lambda e: e.tensor_copy(out=X[0:64, :, 0], in_=spf[0:64, SP['b_sh64'] + s_ * 14:SP['b_sh64'] + s_ * 14 + 14]), ['spf'], [kx])
                c.op('pool', lambda e: e.tensor_copy(out=G[:, 0:1], in_=spf[:, SP['b_shg'] + s_:SP['b_shg'] + s_ + 1]), ['spf'], [kg])
            c.dma('sp', X[0:64, :, 1:1 + T], xsrc[:, :, n0:n0 + T], reads=[('cols', rb, n0 // 512) for rb in range(4, 11)], writes=[kx])
            c.dma('sp', G[:, 1:1 + T], colsT[1408:1536, n0:n0 + T], reads=[('cols', 11, n0 // 512)], writes=[kg])
        else:
            jj = sorted(set([(n0 - 1) // 512, n0 // 512]))
            c.dma('sp', X[0:64, :, 0:1 + T], xsrc[:, :, n0 - 1:n0 + T], reads=[('cols', rb, j_) for rb in range(4, 11) for j_ in jj], writes=[kx])
            c.dma('sp', G[:, 0:1 + T], colsT[1408:1536, n0 - 1:n0 + T], reads=[('cols', 11, j_) for j_ in jj], writes=[kg])
    issue_load(0)
    for k, (gi, (n0, T, sq, first, last)) in enumerate(mysegs):
        if k + 1 < len(mysegs):
            issue_load(k + 1)
        X, G, kx, kg = Xb[k % 2], Gb[k % 2], ('X', k % 2), ('G', k % 2)
        CL = 64 if T >= 64 else T
        nck = T // CL
        nr = 6 if CL == 64 else 2
        yield
        c.op('dve', lambda e: e.tensor_tensor(out=CM[0:64, :, :T], in0=X[0:64, :, 0:T], in1=X[0:64, :, 1:1 + T], op=ALU.subtract), [kx], ['CM'])
        yield
        c.op('dve', lambda e: e.tensor_tensor(out=CM[0:64, :, :T], in0=CM[0:64, :, :T], in1=p64('b64_mu', 14).unsqueeze(2).to_broadcast([64, 14, T]),
                                              op=ALU.mult), ['CM', 'ppf'], ['CM'])
        yield
        c.op('dve', lambda e: e.tensor_tensor(out=CM[0:64, :, :T], in0=CM[0:64, :, :T], in1=X[0:64, :, 1:1 + T], op=ALU.add), ['CM', kx], ['CM'])
        yield
        c.op('dve', lambda e: e.tensor_tensor(out=gm[:, :T], in0=G[:, 0:T], in1=G[:, 1:1 + T], op=ALU.subtract), [kg], ['gm'])
        yield
        c.op('dve', lambda e: e.scalar_tensor_tensor(out=gm[:, :T], in0=gm[:, :T], scalar=pcol('b_mug'), in1=G[:, 1:1 + T], op0=ALU.mult, op1=ALU.add),
             ['gm', kg, 'ppf'], ['gm'])
        if last:
            c.op('pool', lambda e: e.tensor_copy(out=osm[0:64, OS['b_sh64'] + sq * 14:OS['b_sh64'] + sq * 14 + 14], in_=X[0:64, :, T]), [kx], ['osm'])
            c.op('pool', lambda e: e.tensor_copy(out=osm[:, OS['b_shg'] + sq:OS['b_shg'] + sq + 1], in_=G[:, T:T + 1]), [kg], ['osm'])
        rr, kr, vr = CM[0:64, 0:4, :T], CM[0:64, 4:8, :T], CM[0:64, 8:12, :T]
        yield
        c.op('act', lambda e: e.activation(out=tw[0:64, :T], in_=CM[0:64, 12, :T], func=AF.Tanh), ['CM'], ['tw'])
        yield
        c.op('act', lambda e: e.activation(out=alb[0:64, :T], in_=CM[0:64, 13, :T], func=AF.Copy), ['CM'], ['alb'])
        yield
        c.op('act', lambda e: e.activation(out=gs[:, :T], in_=gm[:, :T], func=AF.Sigmoid), ['gm'], ['gs'])
        yield
        pw, pkw = nextps()
        yield
        for h in range(4):
            c.op('pe', lambda e: e.matmul(pw[0:64, h * 128:h * 128 + T], lhsT=w2b[:, h * 64:(h + 1) * 64], rhs=tw[0:64, :T], start=True, stop=True),
                 ['pmat', 'tw'], [pkw])
        yield
        for h in range(4):
            c.op('act', lambda e: e.activation(out=ld[0:64, h, :T], in_=pw[0:64, h * 128:h * 128 + T], func=AF.Sigmoid, bias=p64('b64_w0', 4)[:, h:h + 1], scale=1.0),
                 [pkw, 'ppf'], ['ld'])
        yield
        c.op('dve', lambda e: e.tensor_scalar(out=ld[0:64, :, :T], in0=ld[0:64, :, :T], scalar1=-EM05, scalar2=None, op0=ALU.mult), ['ld'], ['ld'])
        pa, pka = nextps()
        yield
        for h in range(4):
            c.op('pe', lambda e: e.matmul(pa[0:64, h * 128:h * 128 + T], lhsT=a2b[:, h * 64:(h + 1) * 64], rhs=alb[0:64, :T], start=True, stop=True),
                 ['pmat', 'alb'], [pka])
        yield
        for h in range(4):
            c.op('act', lambda e: e.activation(out=asg[0:64, h, :T], in_=pa[0:64, h * 128:h * 128 + T], func=AF.Sigmoid, bias=p64('b64_a0', 4)[:, h:h + 1], scale=1.0),
                 [pka, 'ppf'], ['asg'])
        pg, pkg = nextps()
        yield
        for h in range(4):
            c.op('pe', lambda e: e.matmul(pg[0:64, h * 128:h * 128 + T], lhsT=g2b[:, h * 64:(h + 1) * 64], rhs=gs[:, :T], start=True, stop=True),
                 ['pmat', 'gs'], [pkg])
        yield
        c.op('act', lambda e: e.activation(out=gg[0:64, :, :T], in_=pg[0:64, :].rearrange("p (h t) -> p h t", h=4)[:, :, :T], func=AF.Copy), [pkg], ['gg'])
        yield
        yield
        c.op('dve', lambda e: e.tensor_tensor(out=kk[0:64, :, :T], in0=kr, in1=p64('b64_kk', 4).unsqueeze(2).to_broadcast([64, 4, T]), op=ALU.mult),
             ['CM', 'ppf'], ['kk'])
        yield
        c.op('dve', lambda e: e.tensor_tensor(out=h16[0:64, :, :T], in0=kk[0:64, :, :T], in1=kk[0:64, :, :T], op=ALU.mult), ['kk'], ['h16'])
        pn, pkn = nextps()
        yield
        for h in range(4):
            c.op('pe', lambda e: e.matmul(pn[0:64, h * 128:h * 128 + T], lhsT=ones64, rhs=h16[0:64, h, :T], start=True, stop=True), ['cbb', 'h16'], [pkn])
        yield
        c.op('act', lambda e: e.activation(out=t1[0:64, :, :T], in_=pn[0:64, :].rearrange("p (h t) -> p h t", h=4)[:, :, :T], func=AF.Ln,
                                           bias=1e-24, scale=1.0), [pkn], ['t1'])
        yield
        c.op('act', lambda e: e.activation(out=t1[0:64, :, :T], in_=t1[0:64, :, :T], func=AF.Exp, scale=-0.5), ['t1'], ['t1'])
        yield
        c.op('dve', lambda e: e.tensor_tensor(out=kk[0:64, :, :T], in0=kk[0:64, :, :T], in1=t1[0:64, :, :T], op=ALU.mult), ['kk', 't1'], ['kk'])
        yield
        c.op('dve', lambda e: e.tensor_tensor(out=t1[0:64, :, :T], in0=asg[0:64, :, :T], in1=p64('b64_ka', 4).unsqueeze(2).to_broadcast([64, 4, T]), op=ALU.mult),
             ['asg', 'ppf'], ['t1'])
        yield
        c.op('dve', lambda e: e.tensor_tensor(out=t1[0:64, :, :T], in0=t1[0:64, :, :T], in1=pder[0:64, 12:16].unsqueeze(2).to_broadcast([64, 4, T]), op=ALU.add),
             ['t1', 'pder'], ['t1'])
        yield
        c.op('dve', lambda e: e.tensor_tensor(out=km[0:64, :, :T], in0=kr, in1=t1[0:64, :, :T], op=ALU.mult), ['CM', 't1'], ['km'])
        yield
        yield
        for h in range(4):
            c.op('dve', lambda e: e.tensor_tensor_scan(out=cs[0:64, h, :T], data0=resetm[0:64, 0:T], data1=ld[0:64, h, :T], initial=0.0,
                                                       op0=ALU.mult, op1=ALU.add), ['ld', 'caf'], ['cs'])
        AR4 = AR[0:64].rearrange("p h c (a t) -> p h c a t", a=2)

        def ch4(ap):
            return ap.rearrange("p h (c t) -> p h c t", c=nck)
        yield
        c.op('dve', lambda e: e.tensor_tensor(out=eb[0:64, :, :T], in0=cs[0:64, :, :T], in1=ld[0:64, :, :T], op=ALU.subtract), ['cs', 'ld'], ['eb'])
        yield
        c.op('act', lambda e: e.activation(out=eb[0:64, :, :T], in_=eb[0:64, :, :T], func=AF.Exp), ['eb'], ['eb'])
        yield
        c.op('dve', lambda e: e.scalar_tensor_tensor(out=AR4[:, :, 0:nck, 1, 0:CL], in0=ch4(kk[0:64, :, :T]), scalar=-1.0, in1=ch4(eb[0:64, :, :T]),
                                                     op0=ALU.mult, op1=ALU.mult), ['kk', 'eb'], ['AR'])
        yield
        c.op('act', lambda e: e.activation(out=eb[0:64, :, :T], in_=cs[0:64, :, :T], func=AF.Exp), ['cs', 'AR'], ['eb'])
        yield
        c.op('dve', lambda e: e.tensor_tensor(out=AR4[:, :, 0:nck, 0, 0:CL], in0=ch4(rr), in1=ch4(eb[0:64, :, :T]), op=ALU.mult), ['CM', 'eb'], ['AR'])
        yield
        c.op('dve', lambda e: e.tensor_copy(out=PC[0:64, :, 0:nck], in_=eb[0:64, :, CL - 1:T:CL]), ['eb'], ['PC'])
        yield
        c.op('act', lambda e: e.activation(out=eb[0:64, :, :T], in_=cs[0:64, :, :T], func=AF.Exp, scale=-1.0), ['cs', 'AR', 'PC'], ['eb'])
        yield
        c.op('dve', lambda e: e.tensor_tensor(out=t1[0:64, :, :T], in0=kk[0:64, :, :T], in1=asg[0:64, :, :T], op=ALU.mult), ['kk', 'asg'], ['t1'])
        yield
        c.op('dve', lambda e: e.tensor_tensor(out=Bt[0:64, :, :T], in0=t1[0:64, :, :T], in1=eb[0:64, :, :T], op=ALU.mult), ['t1', 'eb'], ['Bt'])
        yield
        c.op('dve', lambda e: e.tensor_tensor(out=Kt[0:64, :, :T], in0=km[0:64, :, :T], in1=eb[0:64, :, :T], op=ALU.mult), ['km', 'eb'], ['Kt'])
        yield
        c.op('act', lambda e: e.activation(out=Vt[0:64, :, :T], in_=vr, func=AF.Copy), ['CM'], ['Vt'])
        yield
        for ck in range(nck):
            cs_ = slice(ck * CL, (ck + 1) * CL)
            tm, tmk = tmb[0], ('tmbB', 0)
            it += 1
            for j, src in enumerate((Bt, Kt, Vt)):
                for h in range(4):
                    c.op('pe', lambda e: e.transpose(psb[0:CL, (j * 4 + h) * 64:(j * 4 + h + 1) * 64], src[0:64, h, cs_], identb[0:64, 0:64]),
                         [('Bt', 'Kt', 'Vt')[j], 'cbb'], ['psb'])
            c.op('act', lambda e: e.activation(out=tm[0:CL, :], in_=psb[0:CL, 0:768], func=AF.Copy), ['psb'], [tmk])
            yield

            def Btm(h):
                return tm[0:CL, h * 64:(h + 1) * 64]

            def Ktm(h):
                return tm[0:CL, 256 + h * 64:256 + (h + 1) * 64]

            def Vtm(h):
                return tm[0:CL, 512 + h * 64:512 + (h + 1) * 64]
            P1, pk1 = nextps()
            P2, pk2 = nextps()
            for h in range(4):
                c.op('pe', lambda e: e.matmul(P1[0:CL, h * 128:h * 128 + 128], lhsT=Bt[0:64, h, cs_], rhs=AR[0:64, h, ck, :], start=True, stop=True),
                     ['Bt', 'AR'], [pk1])
            for h in range(4):
                c.op('pe', lambda e: e.matmul(P2[0:CL, h * 128:h * 128 + 128], lhsT=Kt[0:64, h, cs_], rhs=AR[0:64, h, ck, :], start=True, stop=True),
                     ['Kt', 'AR'], [pk2])
            m2b = mask2[0:CL, :, 0:CL].unsqueeze(1).to_broadcast([CL, 4, 2, CL])
            W0, W1 = Wb[0], Wb[1]
            W0v = W0[0:CL].rearrange("p h (a t) -> p h a t", a=3)
            W1v = W1[0:CL].rearrange("p h (a t) -> p h a t", a=3)
            G1v = G1[0:CL].rearrange("p h (a t) -> p h a t", a=2)
            G2v = G2[0:CL].rearrange("p h (a t) -> p h a t", a=2)
            P1v = P1[0:CL, :].rearrange("p (h a t) -> p h a t", h=4, a=2)
            P2v = P2[0:CL, :].rearrange("p (h a t) -> p h a t", h=4, a=2)
            c.op('dve', lambda e: e.tensor_tensor(out=G1v[:, :, :, 0:CL], in0=P1v[:, :, :, 0:CL], in1=m2b, op=ALU.mult), [pk1, 'cbf'], ['G1'])
            c.op('dve', lambda e: e.tensor_tensor(out=G2v[:, :, :, 0:CL], in0=P2v[:, :, :, 0:CL], in1=m2b, op=ALU.mult), [pk2, 'cbf'], ['G2'])
            P3, pk3 = nextps()
            for h in range(4):
                c.op('pe', lambda e: e.matmul(P3[0:CL, h * 64:h * 64 + CL], lhsT=AR[0:64, h, ck, 64:64 + CL], rhs=Bt[0:64, h, cs_], start=True, stop=True),
                     ['Bt', 'AR'], [pk3])
            c.op('act', lambda e: e.activation(out=W0v[:, :, 0, 0:CL], in_=G1v[:, :, 1, 0:CL], func=AF.Copy), ['G1'], ['W0'])
            c.op('act', lambda e: e.activation(out=W0v[:, :, 1, 0:CL], in_=identb[0:CL, 0:CL].unsqueeze(1).to_broadcast([CL, 4, CL]), func=AF.Copy),
                 ['cbb'], ['W0'])
            c.op('dve', lambda e: e.tensor_tensor(out=W0v[:, :, 2, 0:CL], in0=P3[0:CL, 0:256].rearrange("p (h t) -> p h t", h=4)[:, :, 0:CL],
                                                  in1=tri_sl[0:CL, 0:CL].unsqueeze(1).to_broadcast([CL, 4, CL]), op=ALU.mult), [pk3, 'cbf'], ['W0'])
            yield
            cur, nxt, curk, nxtk = W0v, W1v, 'W0', 'W1'
            for r_ in range(nr):
                lastr = (r_ == nr - 1)
                PA, pkA = nextps()
                PAv = PA[0:CL, :].rearrange("p (h a t) -> p h a t", h=4, a=2)
                for h in range(4):
                    for a_ in range(2):
                        if lastr and a_ == 0:
                            continue
                        c.op('pe', lambda e: e.matmul(PAv[:, h, a_, 0:CL], lhsT=cur[:, h, 2, 0:CL], rhs=cur[:, h, a_, 0:CL], start=True, stop=True),
                             [curk], [pkA])
                if not lastr:
                    PB, pkB = nextps()
                    PBv = PB[0:CL, 0:256].rearrange("p (h t) -> p h t", h=4)
                    for h in range(4):
                        c.op('pe', lambda e: e.matmul(PBv[:, h, 0:CL], lhsT=cur[:, h, 0, 0:CL], rhs=cur[:, h, 2, 0:CL], start=True, stop=True), [curk], [pkB])
                    c.op('act', lambda e: e.activation(out=nxt[:, :, 0, 0:CL], in_=PAv[:, :, 0, 0:CL], func=AF.Copy), [pkA], [nxtk])
                    c.op('act', lambda e: e.activation(out=nxt[:, :, 2, 0:CL], in_=PBv[:, :, 0:CL], func=AF.Copy), [pkB], [nxtk])
                c.op('dve', lambda e: e.tensor_tensor(out=nxt[:, :, 1, 0:CL], in0=PAv[:, :, 1, 0:CL], in1=cur[:, :, 1, 0:CL], op=ALU.add), [pkA, curk], [nxtk])
                cur, nxt, curk, nxtk = nxt, cur, nxtk, curk
                yield
            TTv, TTk = cur, curk

            def At(h):
                return AR[0:64, h, ck, 64:64 + CL]

            def Rt(h):
                return AR[0:64, h, ck, 0:CL]
            while tok['next'] != gi:
                yield
            if first:
                if sq == 0:
                    c.op('dve', lambda e: e.memset(ST[0:64], 0.0), [], ['ST'])
                else:
                    c.dma('sp', ST[0:64].rearrange("p h v -> p (h v)"), wkv_in[l][:, (sq - 1) * 256:sq * 256], writes=['ST'])
                c.op('act', lambda e: e.activation(out=STb[0:64], in_=ST[0:64], func=AF.Copy), ['ST'], ['STb'])
            PZ, pkZ = nextps()
            for h in range(4):
                c.op('pe', lambda e: e.matmul(PZ[0:CL, h * 64:(h + 1) * 64], lhsT=At(h), rhs=STb[0:64, h, :], start=True, stop=False), ['AR', 'STb'], [pkZ])
                c.op('pe', lambda e: e.matmul(PZ[0:CL, h * 64:(h + 1) * 64], lhsT=G2v[:, h, 1, 0:CL], rhs=Vtm(h), start=False, stop=True), ['G2', tmk], [pkZ])
            c.op('act', lambda e: e.activation(out=Zs[0:CL], in_=PZ[0:CL, 0:256].rearrange("p (h v) -> p h v", h=4), func=AF.Copy), [pkZ], ['Zs'])
            yield
            PU, pkU = nextps()
            for h in range(4):
                c.op('pe', lambda e: e.matmul(PU[0:CL, h * 64:(h + 1) * 64], lhsT=TTv[:, h, 1, 0:CL], rhs=Zs[0:CL, h, :], start=True, stop=True), [TTk, 'Zs'], [pkU])
            c.op('act', lambda e: e.activation(out=Us[0:CL], in_=PU[0:CL, 0:256].rearrange("p (h v) -> p h v", h=4), func=AF.Copy), [pkU], ['Us'])
            yield
            PY, pkY = nextps()
            for h in range(4):
                c.op('pe', lambda e: e.matmul(PY[0:64, h * 64:h * 64 + CL], lhsT=STb[0:64, h, :], rhs=Rt(h), start=True, stop=False), ['AR', 'STb'], [pkY])
                c.op('pe', lambda e: e.matmul(PY[0:64, h * 64:h * 64 + CL], lhsT=Us[0:CL, h, :], rhs=G1v[:, h, 0, 0:CL], start=False, stop=False), ['Us', 'G1'], [pkY])
                c.op('pe', lambda e: e.matmul(PY[0:64, h * 64:h * 64 + CL], lhsT=Vtm(h), rhs=G2v[:, h, 0, 0:CL], start=False, stop=True), [tmk, 'G2'], [pkY])
            c.op('act', lambda e: e.activation(out=yraw[0:64, :, cs_], in_=PY[0:64, 0:256].rearrange("p (h t) -> p h t", h=4)[:, :, 0:CL], func=AF.Copy),
                 [pkY], ['yraw'])
            PS, pkS = nextps()
            for h in range(4):
                c.op('pe', lambda e: e.matmul(PS[0:64, h * 64:(h + 1) * 64], lhsT=Btm(h), rhs=Us[0:CL, h, :], start=True, stop=False), [tmk, 'Us'], [pkS])
                c.op('pe', lambda e: e.matmul(PS[0:64, h * 64:(h + 1) * 64], lhsT=Ktm(h), rhs=Vtm(h), start=False, stop=True), [tmk], [pkS])
            c.op('dve', lambda e: e.tensor_tensor(out=ST[0:64], in0=PS[0:64, 0:256].rearrange("p (h v) -> p h v", h=4), in1=ST[0:64], op=ALU.add),
                 [pkS, 'ST'], ['ST'])
            c.op('dve', lambda e: e.tensor_tensor(out=ST[0:64], in0=ST[0:64], in1=PC[0:64, :, ck:ck + 1].to_broadcast([64, 4, 64]), op=ALU.mult),
                 ['ST', 'PC'], ['ST'])
            c.op('act', lambda e: e.activation(out=STb[0:64], in_=ST[0:64], func=AF.Copy), ['ST'], ['STb'])
            if last:
                c.dma('pool', o_wkv[l, sq], ST[0:64].rearrange("p h v -> p (h v)"), reads=['ST'])
            tok['next'] = gi + 1
            yield
        yield
        c.op('act', lambda e: e.activation(out=h16[0:64, :, :T], in_=yraw[0:64, :, :T], func=AF.Copy), ['yraw'], ['h16'])
        pm, pkm = nextps()
        yield
        for h in range(4):
            c.op('pe', lambda e: e.matmul(pm[0:64, h * 128:h * 128 + T], lhsT=ones64, rhs=h16[0:64, h, :T], start=True, stop=True), ['cbb', 'h16'], [pkm])
        pmv = pm[0:64, :].rearrange("p (h t) -> p h t", h=4)[:, :, :T]
        yield
        c.op('dve', lambda e: e.scalar_tensor_tensor(out=yraw[0:64, :, :T], in0=pmv, scalar=-1.0 / 64, in1=yraw[0:64, :, :T], op0=ALU.mult, op1=ALU.add),
             [pkm, 'yraw'], ['yraw'])
        yield
        c.op('dve', lambda e: e.tensor_tensor(out=h16[0:64, :, :T], in0=yraw[0:64, :, :T], in1=yraw[0:64, :, :T], op=ALU.mult), ['yraw'], ['h16'])
        pv_, pkv = nextps()
        yield
        for h in range(4):
            c.op('pe', lambda e: e.matmul(pv_[0:64, h * 128:h * 128 + T], lhsT=ones64, rhs=h16[0:64, h, :T], start=True, stop=True), ['cbb', 'h16'], [pkv])
        yield
        c.op('act', lambda e: e.activation(out=t1[0:64, :, :T], in_=pv_[0:64, :].rearrange("p (h t) -> p h t", h=4)[:, :, :T], func=AF.Ln,
                                           bias=64e-5, scale=1.0 / 64), [pkv], ['t1'])
        yield
        c.op('act', lambda e: e.activation(out=t1[0:64, :, :T], in_=t1[0:64, :, :T], func=AF.Exp, scale=-0.5), ['t1'], ['t1'])
        yield
        c.op('dve', lambda e: e.tensor_tensor(out=yraw[0:64, :, :T], in0=yraw[0:64, :, :T], in1=t1[0:64, :, :T], op=ALU.mult), ['yraw', 't1'], ['yraw'])
        yield
        c.op('dve', lambda e: e.tensor_tensor(out=yraw[0:64, :, :T], in0=yraw[0:64, :, :T], in1=p64('b64_lnw', 4).unsqueeze(2).to_broadcast([64, 4, T]), op=ALU.mult),
             ['yraw', 'ppf'], ['yraw'])
        yield
        c.op('dve', lambda e: e.tensor_tensor(out=yraw[0:64, :, :T], in0=yraw[0:64, :, :T], in1=p64('b64_lnb', 4).unsqueeze(2).to_broadcast([64, 4, T]), op=ALU.add),
             ['yraw', 'ppf'], ['yraw'])
        yield
        c.op('dve', lambda e: e.tensor_tensor(out=t1[0:64, :, :T], in0=rr, in1=km[0:64, :, :T], op=ALU.mult), ['CM', 'km'], ['t1'])
        yield
        c.op('dve', lambda e: e.tensor_tensor(out=h16[0:64, :, :T], in0=t1[0:64, :, :T], in1=p64('b64_rk', 4).unsqueeze(2).to_broadcast([64, 4, T]), op=ALU.mult),
             ['t1', 'ppf'], ['h16'])
        pb_, pkb = nextps()
        yield
        for h in range(4):
            c.op('pe', lambda e: e.matmul(pb_[0:64, h * 128:h * 128 + T], lhsT=ones64, rhs=h16[0:64, h, :T], start=True, stop=True), ['cbb', 'h16'], [pkb])
        yield
        c.op('dve', lambda e: e.tensor_tensor(out=t1[0:64, :, :T], in0=pb_[0:64, :].rearrange("p (h t) -> p h t", h=4)[:, :, :T], in1=vr, op=ALU.mult),
             [pkb, 'CM'], ['t1'])
        yield
        c.op('dve', lambda e: e.tensor_tensor(out=yraw[0:64, :, :T], in0=yraw[0:64, :, :T], in1=t1[0:64, :, :T], op=ALU.add), ['yraw', 't1'], ['yraw'])
        yield
        c.op('dve', lambda e: e.tensor_tensor(out=outB[0:64, :, :T], in0=yraw[0:64, :, :T], in1=gg[0:64, :, :T], op=ALU.mult), ['yraw', 'gg'], ['outB'])
        for hh in range(2):
            c.dma('pool', b1[hh * 64:(hh + 1) * 64, 2:4, n0:n0 + T], outB[0:64, hh::2, :T], reads=['outB'], writes=k_b1([2, 3], n0, T))
        yield


WNAMES = ['norm1_g', 'w_in', 'a_conv_w', 'a_conv_b', 'a_gx_w', 'a_gx_b', 'a_ga_w', 'a_ga_b', 'a_lambda', 'b_mu', 'b_w0', 'b_w2',
          'b_a0', 'b_a2', 'b_g2', 'b_k_k', 'b_k_a', 'b_r_k', 'b_ln_w', 'b_ln_b', 'c_lb', 'c_norm_g', 'w_branch', 'w_out',
          'norm2_g', 'ffn_w_gate', 'ffn_w_up', 'ffn_conv_w', 'ffn_conv_b', 'ffn_w_down', 'final_norm_g']
_CACHE = {}


def host_prep(inp):
    f = lambda a: np.ascontiguousarray(np.asarray(a, dtype=np.float32))
    W = {k: np.asarray(inp[k], np.float32) for k in WNAMES}
    cb, ca, mp = host_consts()
    pp = np.stack([host_ppack(W, l) for l in range(2)])
    gt = np.stack([W['norm1_g'][0], W['norm1_g'][1], W['norm2_g'][0], W['norm2_g'][1], W['final_norm_g']]).astype(np.float32)
    shared = {"w_in": f(W['w_in']), "w_branch": f(W['w_branch'].reshape(2, 1024, 1024)), "w_out": f(W['w_out']),
              "ffn_w_gate": f(W['ffn_w_gate']), "ffn_w_up": f(W['ffn_w_up']), "ffn_w_down": f(W['ffn_w_down']),
              "ppack": f(pp), "gtab": f(gt), "cbpack": cb, "capack": ca, "maskP": mp}
    xp = np.asarray(inp['x_prompt'], np.float32)
    xs = np.asarray(inp['x_sample'], np.float32)
    s_ah = np.asarray(inp['state_a_h'], np.float32)
    s_ac = np.asarray(inp['state_a_conv'], np.float32)
    s_wkv = np.asarray(inp['state_b_wkv'], np.float32)
    s_sh = np.asarray(inp['state_b_shift'], np.float32)
    s_cs = np.asarray(inp['state_c_s'], np.float32)
    s_fc = np.asarray(inp['state_ffn_conv'], np.float32)
    ck = np.asarray(inp['cache_d_k'], np.float32)
    cv = np.asarray(inp['cache_d_v'], np.float32)
    maps = []
    for b in range(8):
        sl = slice(4 * b, 4 * b + 4)
        m = dict(shared)
        m["x"] = f(np.concatenate([xp[b], xs[sl].reshape(16, 1024)], 0))
        spk = np.zeros((2, 128, NSP), np.float32)
        for l in range(2):
            spk[l, :, SP['a_h']:SP['a_h'] + 8] = s_ah[l, sl].reshape(4, 2, 128).transpose(2, 0, 1).reshape(128, 8)
            spk[l, :, SP['a_conv']:SP['a_conv'] + 24] = s_ac[l, sl].reshape(4, 3, 2, 128).transpose(3, 0, 2, 1).reshape(128, 24)
            spk[l, :, SP['b_shift']:SP['b_shift'] + 32] = s_sh[l, sl].reshape(4, 8, 128).transpose(2, 0, 1).reshape(128, 32)
            spk[l, :, SP['f_conv']:SP['f_conv'] + 192] = s_fc[l, sl].reshape(4, 2, 24, 128).transpose(3, 0, 2, 1).reshape(128, 192)
            spk[l, :64, SP['b_sh64']:SP['b_sh64'] + 56] = s_sh[l, sl][:, :896].reshape(4, 14, 64).transpose(2, 0, 1).reshape(64, 56)
            spk[l, :, SP['b_shg']:SP['b_shg'] + 4] = s_sh[l, sl][:, 896:].T
        m["spack"] = spk
        m["wkvT"] = f(s_wkv[:, sl].transpose(0, 4, 1, 2, 3).reshape(2, 64, 1024))
        m["cs0"] = f(s_cs[:, sl].transpose(0, 3, 1, 2, 4).reshape(2, 128, 1024))
        m["kcT"] = f(ck[:, sl].reshape(2, 4, 2048, 2, 2, 64).transpose(0, 1, 4, 5, 3, 2).reshape(2, 4, 128, 4096))
        m["vc"] = f(cv[:, sl].reshape(2, 4, 16, 128, 256).transpose(0, 1, 3, 2, 4).reshape(2, 4, 128, 4096))
        maps.append(m)
    return maps


def host_post(results):
    yp = np.zeros((8, 2048, 1024), np.float32)
    ys = np.zeros((32, 4, 1024), np.float32)
    p_a_h = np.zeros((2, 8, 256), np.float32)
    p_a_conv = np.zeros((2, 8, 3, 256), np.float32)
    p_wkv = np.zeros((2, 8, 4, 64, 64), np.float32)
    p_sh = np.zeros((2, 8, 1024), np.float32)
    p_cs = np.zeros((2, 8, 4, 128, 64), np.float32)
    p_dk = np.zeros((2, 8, 2048, 4, 64), np.float32)
    p_dv = np.zeros((2, 8, 2048, 4, 64), np.float32)
    p_fc = np.zeros((2, 8, 2, 3072), np.float32)
    s_a_h = np.zeros((2, 32, 256), np.float32)
    s_a_conv = np.zeros((2, 32, 3, 256), np.float32)
    s_wkv = np.zeros((2, 32, 4, 64, 64), np.float32)
    s_sh = np.zeros((2, 32, 1024), np.float32)
    s_cs = np.zeros((2, 32, 4, 128, 64), np.float32)
    s_dk = np.zeros((2, 32, 4, 4, 64), np.float32)
    s_dv = np.zeros((2, 32, 4, 4, 64), np.float32)
    s_fc = np.zeros((2, 32, 2, 3072), np.float32)
    for b in range(8):
        r = results[b]
        y = r["y"]
        yp[b] = y[:2048]
        ys[4 * b:4 * b + 4] = y[2048:].reshape(4, 4, 1024)
        dk, dv = r["o_dk"], r["o_dv"]
        p_dk[:, b] = dk[:, :2048].reshape(2, 2048, 4, 64)
        p_dv[:, b] = dv[:, :2048].reshape(2, 2048, 4, 64)
        s_dk[:, 4 * b:4 * b + 4] = dk[:, 2048:].reshape(2, 4, 4, 4, 64)
        s_dv[:, 4 * b:4 * b + 4] = dv[:, 2048:].reshape(2, 4, 4, 4, 64)
        sm = r["o_small"]
        ah = sm[:, :, OS['a_h']:OS['a_h'] + 10].reshape(2, 128, 5, 2).transpose(0, 2, 3, 1).reshape(2, 5, 256)
        ac = sm[:, :, OS['a_conv']:OS['a_conv'] + 30].reshape(2, 128, 5, 2, 3).transpose(0, 2, 4, 3, 1).reshape(2, 5, 3, 256)
        sh64 = sm[:, :64, OS['b_sh64']:OS['b_sh64'] + 70].reshape(2, 64, 5, 14).transpose(0, 2, 3, 1).reshape(2, 5, 896)
        shg = sm[:, :, OS['b_shg']:OS['b_shg'] + 5].transpose(0, 2, 1)
        sh = np.concatenate([sh64, shg], axis=2)
        fc = sm[:, :, OS['f_conv']:OS['f_conv'] + 240].reshape(2, 128, 5, 24, 2).transpose(0, 2, 4, 3, 1).reshape(2, 5, 2, 3072)
        p_a_h[:, b], s_a_h[:, 4 * b:4 * b + 4] = ah[:, 0], ah[:, 1:]
        p_a_conv[:, b], s_a_conv[:, 4 * b:4 * b + 4] = ac[:, 0], ac[:, 1:]
        p_sh[:, b], s_sh[:, 4 * b:4 * b + 4] = sh[:, 0], sh[:, 1:]
        p_fc[:, b], s_fc[:, 4 * b:4 * b + 4] = fc[:, 0], fc[:, 1:]
        wk = r["o_wkv"].reshape(2, 5, 64, 4, 64).transpose(0, 1, 3, 4, 2)
        p_wkv[:, b], s_wkv[:, 4 * b:4 * b + 4] = wk[:, 0], wk[:, 1:]
        cs = r["o_cs"].reshape(2, 5, 128, 4, 64).transpose(0, 1, 3, 2, 4)
        p_cs[:, b], s_cs[:, 4 * b:4 * b + 4] = cs[:, 0], cs[:, 1:]
    return (yp, ys, p_a_h, p_a_conv, p_wkv, p_sh, p_cs, p_dk, p_dv, p_fc,
            s_a_h, s_a_conv, s_wkv, s_sh, s_cs, s_dk, s_dv, s_fc)


ENABLE = ('A', 'B', 'C', 'D')


def kernel(**inputs):
    maps = host_prep(inputs)
    key = tuple(ENABLE)
    if key not in _CACHE:
        _CACHE[key] = build_program(ENABLE)
    nc = _CACHE[key]
    res = run_bass_kernel_spmd(nc, maps, core_ids=list(range(8)))
    return host_post(res.results)
```

```python
import numpy as np
import ml_dtypes
import concourse.bass as bass
import concourse.mybir as mybir
from concourse.bass_utils import run_bass_kernel_spmd

F32 = mybir.dt.float32
BF16 = mybir.dt.bfloat16
ALU = mybir.AluOpType
AF = mybir.ActivationFunctionType
AX = mybir.AxisListType

NT = 2064
NTILES = [(0, 512), (512, 512), (1024, 512), (1536, 512), (2048, 16)]
TT128 = [(t * 128, 128) for t in range(16)] + [(2048, 16)]
DEPTH = 2
EM05 = float(np.exp(-0.5))


class Ctx:
    NDS = 8

    def __init__(self, nc):
        self.nc = nc
        self.engs = {'pe': nc.tensor, 'act': nc.scalar, 'dve': nc.vector, 'pool': nc.gpsimd, 'sp': nc.sync}
        self.csem = {e: nc.alloc_semaphore(name="c_" + e) for e in ('pe', 'act', 'dve', 'pool')}
        self.ccnt = {e: 0 for e in self.csem}
        self.dsem = {q: [nc.alloc_semaphore(name="d_%s%d" % (q, i)) for i in range(self.NDS)]
                     for q in ('sp', 'pool', 'act')}
        self.dcnt = {q: 0 for q in self.dsem}
        self.waited = {e: {} for e in self.engs}
        self.last_w = {}
        self.readers = {}
        self.n_inst = 0
        self.n_wait = 0

    def _wait(self, engine, ev):
        sem, val, src = ev
        w = self.waited[engine]
        k = id(sem)
        if w.get(k, 0) >= val:
            return
        if src == 'pe' and engine == 'pe':
            return
        self.engs[engine].wait_ge(sem, val)
        w[k] = val
        self.n_wait += 1

    def _deps(self, engine, reads, writes):
        for k in reads:
            ev = self.last_w.get(k)
            if ev is not None:
                self._wait(engine, ev)
        for k in writes:
            ev = self.last_w.get(k)
            if ev is not None:
                self._wait(engine, ev)
            for ev in self.readers.get(k, ()):
                self._wait(engine, ev)

    def _commit(self, ev, reads, writes):
        for k in writes:
            self.last_w[k] = ev
            self.readers[k] = []
        for k in reads:
            if k not in writes:
                self.readers.setdefault(k, []).append(ev)

    def op(self, engine, fn, reads=(), writes=()):
        pr = [k for k in reads if k == 'psb' or (isinstance(k, tuple) and k[0] == 'ps')]
        if pr:
            reads = [k for k in reads if k not in pr]
            writes = list(writes) + [k for k in pr if k not in writes]
        self._deps(engine, reads, writes)
        inst = fn(self.engs[engine])
        self.ccnt[engine] += 1
        inst.then_inc(self.csem[engine], 1)
        ev = (self.csem[engine], self.ccnt[engine], engine)
        self._commit(ev, reads, writes)
        self.n_inst += 1
        return inst

    def dma(self, q, out, in_, reads=(), writes=(), **kw):
        n = self.dcnt[q]
        sem = self.dsem[q][n % self.NDS]
        target = 16 * (n // self.NDS + 1)
        if n >= self.NDS:
            self._wait(q, (sem, target - 16, 'dma'))
        self._deps(q, reads, writes)
        inst = self.engs[q].dma_start(out=out, in_=in_, **kw)
        inst.then_inc(sem, 16)
        self.dcnt[q] = n + 1
        ev = (sem, target, 'dma')
        self._commit(ev, reads, writes)
        self.n_inst += 1
        return inst

    def _all_events(self):
        evs = []
        for q in self.dsem:
            n = self.dcnt[q]
            for i in range(min(n, self.NDS)):
                cnt = (n - 1 - i) // self.NDS + 1
                evs.append((self.dsem[q][i], 16 * cnt, 'dma'))
        for e in self.csem:
            if self.ccnt[e]:
                evs.append((self.csem[e], self.ccnt[e], 'x'))
        return evs

    def barrier(self):
        evs = self._all_events()
        for e in self.engs:
            for ev in evs:
                self._wait(e, ev)
        self.last_w = {}
        self.readers = {}

    def finish(self):
        for ev in self._all_events():
            self._wait('sp', ev)


class Arena:
    def __init__(self, nc, name, nbytes):
        self.words = nbytes // 4
        self.t = nc.alloc_sbuf_tensor(name, [128, self.words], F32).ap()
        self.off = 0

    def reset(self):
        self.off = 0

    def alloc(self, shape, dtype=F32):
        n = int(np.prod(shape))
        words = n if dtype == F32 else (n + 1) // 2
        words = (words + 7) // 8 * 8
        assert self.off + words <= self.words, ("arena overflow", self.off, words, self.words)
        ap = self.t[:, self.off:self.off + words]
        self.off += words
        if dtype != F32:
            ap = ap.bitcast(dtype)
        ap = ap[:, 0:n]
        if len(shape) == 2:
            return ap.rearrange("p (a b) -> p a b", a=shape[0])
        if len(shape) == 3:
            return ap.rearrange("p (a b c) -> p a b c", a=shape[0], b=shape[1])
        return ap


class CtxTag:
    SHARED = ('ST', 'STb', 'ppf', 'pder', 'spf', 'pmat', 'cbb', 'cbf', 'caf', 'osm', 'psb')

    def __init__(self, c, tag):
        self.c = c
        self.tag = tag

    def _m(self, keys):
        out = []
        for k in keys:
            if k in self.SHARED or (isinstance(k, tuple) and k[0] in ('ps', 'b1', 'cols')):
                out.append(k)
            else:
                out.append((self.tag, k))
        return out

    def op(self, engine, fn, reads=(), writes=()):
        return self.c.op(engine, fn, self._m(reads), self._m(writes))

    def dma(self, q, out, in_, reads=(), writes=(), **kw):
        return self.c.dma(q, out, in_, reads=self._m(reads), writes=self._m(writes), **kw)


def _mk_layout(entries):
    off, d = 0, {}
    for name, n in entries:
        d[name] = off
        off += n
    return d, off


PK, NPK = _mk_layout([
    ('a_cw', 8), ('a_cb', 2), ('a_gxb', 2), ('a_gab', 2), ('a_lam', 2),
    ('b_mu', 8), ('b_w0', 2), ('b_a0', 2), ('b_kk', 2), ('b_ka', 2), ('b_rk', 2), ('b_lnw', 2), ('b_lnb', 2),
    ('c_lb0', 4), ('c_lb1', 4), ('c_ng', 1), ('f_cw', 72), ('f_cb', 24), ('pad', 1),
    ('b64_mu', 14), ('b_mug', 1), ('b64_w0', 4), ('b64_a0', 4), ('b64_kk', 4), ('b64_ka', 4), ('b64_rk', 4), ('b64_lnw', 4),
    ('b64_lnb', 4), ('pad2', 5),
    ('gxw', 256), ('gaw', 256), ('w2', 256), ('a2', 256), ('g2', 256)])
PMAT0 = PK['gxw']
SP, NSP = _mk_layout([('a_h', 8), ('a_conv', 24), ('b_shift', 32), ('f_conv', 192), ('b_sh64', 56), ('b_shg', 4)])
OS, NOS = _mk_layout([('a_h', 10), ('a_conv', 30), ('b_shift', 40), ('f_conv', 240), ('b_sh64', 70), ('b_shg', 5)])
CB, NCB = _mk_layout([('ident', 128), ('tri_incl', 64), ('tri_su', 64), ('tri_sl', 64), ('blk64', 128),
                      ('maskS', 68), ('maskN', 16), ('ones', 64)])
CA, NCA = _mk_layout([('rope', 17 * 64), ('reset', 512)])


def _fm(v, nchunk):
    return np.ascontiguousarray(np.asarray(v, np.float32).reshape(nchunk, 128).T)


def _mult(dist):
    dist = np.asarray(dist)
    m = ((dist >= 0) & (dist <= 128)).astype(np.float32)
    m += ((dist >= 0) & (dist % 4 == 0) & (dist <= 512))
    m += ((dist >= 0) & (dist % 16 == 0) & (dist <= 2048))
    return m.astype(np.float32)


def host_consts():
    cb = np.zeros((128, NCB), np.float32)
    cb[:, CB['ident']:CB['ident'] + 128] = np.eye(128, dtype=np.float32)
    i = np.arange(64)
    cb[:64, CB['tri_incl']:CB['tri_incl'] + 64] = (i[:, None] <= i[None, :])
    cb[:64, CB['tri_su']:CB['tri_su'] + 64] = (i[:, None] < i[None, :])
    cb[:64, CB['tri_sl']:CB['tri_sl'] + 64] = (i[:, None] > i[None, :])
    blk = np.zeros((128, 128), np.float32)
    blk[:64, :64] = 1.0
    blk[64:, 64:] = 1.0
    cb[:, CB['blk64']:CB['blk64'] + 128] = blk
    cb[:, CB['ones']:CB['ones'] + 64] = 1.0
    p = np.arange(128)
    ms = np.zeros((128, 17, 4), np.float32)
    for b in range(16):
        for q in range(4):
            ms[:, b, q] = _mult(2048 + q - (b * 128 + p))
    cb[:, CB['maskS']:CB['maskS'] + 68] = ms.reshape(128, 68)
    mn = np.zeros((16, 4, 4), np.float32)
    for s2 in range(4):
        for i2 in range(4):
            for q in range(4):
                mn[s2 * 4 + i2, s2, q] = _mult(q - i2)
    cb[:16, CB['maskN']:CB['maskN'] + 16] = mn.reshape(16, 16)
    ca = np.zeros((128, NCA), np.float32)
    half = 32
    inv = (10000.0 ** (-np.arange(half, dtype=np.float32) / half)).astype(np.float32)
    rope = np.zeros((128, 17, 64), np.float32)
    for t in range(17):
        if t < 16:
            pos = (t * 128 + np.arange(128)).astype(np.float32)
        else:
            pos = np.zeros(128, np.float32)
            pos[:16] = (8192 + np.tile(np.arange(4), 4)).astype(np.float32)
        ang = (pos[:, None] * inv[None, :]).astype(np.float32)
        rope[:, t, :32] = np.cos(ang)
        rope[:, t, 32:] = np.sin(ang)
    ca[:, CA['rope']:CA['rope'] + 17 * 64] = rope.reshape(128, -1)
    rs = np.ones(512, np.float32)
    rs[::64] = 0.0
    ca[:, CA['reset']:CA['reset'] + 512] = rs[None, :]
    mp = np.zeros((128, 17, 128), np.float32)
    for d in range(17):
        mp[:, d, :] = _mult((d * 128 + p[None, :]) - p[:, None])
    return cb, ca, mp.reshape(128, 17 * 128)


def host_ppack(W, l):
    pk = np.zeros((128, NPK), np.float32)

    def put(name, arr):
        arr = np.asarray(arr, np.float32)
        pk[:arr.shape[0], PK[name]:PK[name] + arr.shape[1]] = arr
    cw = np.asarray(W['a_conv_w'][l], np.float32)
    put('a_cw', np.stack([_fm(cw[j], 2) for j in range(4)], axis=2).reshape(128, 8))
    put('a_cb', _fm(W['a_conv_b'][l], 2))
    put('a_gxb', _fm(W['a_gx_b'][l], 2))
    put('a_gab', _fm(W['a_ga_b'][l], 2))
    put('a_lam', _fm(W['a_lambda'][l], 2))
    put('b_mu', _fm(W['b_mu'][l], 8))
    put('b_w0', _fm(W['b_w0'][l], 2))
    put('b_a0', _fm(W['b_a0'][l], 2))
    put('b_kk', _fm(W['b_k_k'][l], 2))
    put('b_ka', _fm(W['b_k_a'][l], 2))
    put('b_rk', _fm(np.asarray(W['b_r_k'][l]).reshape(256), 2))
    put('b_lnw', _fm(W['b_ln_w'][l], 2))
    put('b_lnb', _fm(W['b_ln_b'][l], 2))
    put('c_lb0', _fm(W['c_lb'][0], 4))
    put('c_lb1', _fm(W['c_lb'][1], 4))
    put('c_ng', np.tile(np.asarray(W['c_norm_g'][l], np.float32), 2)[:, None])
    fw = np.asarray(W['ffn_conv_w'][l], np.float32)
    put('f_cw', np.stack([_fm(fw[j], 24) for j in range(3)], axis=2).reshape(128, 72))
    put('f_cb', _fm(W['ffn_conv_b'][l], 24))
    for nm, key in (('gxw', 'a_gx_w'), ('gaw', 'a_ga_w')):
        g = np.asarray(W[key][l], np.float32)
        m = np.zeros((128, 256), np.float32)
        for c in range(2):
            for hh in range(2):
                m[hh * 64:(hh + 1) * 64, c * 128 + hh * 64:c * 128 + (hh + 1) * 64] = g[c * 2 + hh]
        put(nm, m)
    put('w2', np.asarray(W['b_w2'][l], np.float32))
    put('a2', np.asarray(W['b_a2'][l], np.float32))
    mu = np.asarray(W['b_mu'][l], np.float32)
    put('b64_mu', mu[:896].reshape(14, 64).T)
    put('b_mug', mu[896:].reshape(128, 1))
    for nm, key in (('b64_w0', 'b_w0'), ('b64_a0', 'b_a0'), ('b64_kk', 'b_k_k'), ('b64_ka', 'b_k_a'), ('b64_rk', 'b_r_k'),
                    ('b64_lnw', 'b_ln_w'), ('b64_lnb', 'b_ln_b')):
        put(nm, np.asarray(W[key][l], np.float32).reshape(4, 64).T)
    put('g2', np.asarray(W['b_g2'][l], np.float32))
    return pk


STOP = None


class _Stop(Exception):
    pass


def build_program(enable=('A', 'B', 'C', 'D')):
    nc = bass.Bass("TRN2", target_bir_lowering=False)
    try:
        _build(nc, enable)
    except _Stop:
        pass
    return nc


def _build(nc, enable):

    def din(name, shape):
        return nc.dram_tensor(name, list(shape), F32, kind="ExternalInput").ap()

    def dout(name, shape):
        return nc.dram_tensor(name, list(shape), F32, kind="ExternalOutput").ap()
    x_in = din("x", [NT, 1024])
    w_in = din("w_in", [2, 1024, 7936])
    w_br = din("w_branch", [2, 1024, 1024])
    w_out = din("w_out", [2, 1024, 1024])
    w_g = din("ffn_w_gate", [2, 1024, 3072])
    w_u = din("ffn_w_up", [2, 1024, 3072])
    w_d = din("ffn_w_down", [2, 3072, 1024])
    ppack = din("ppack", [2, 128, NPK])
    gtab = din("gtab", [5, 1024])
    spack = din("spack", [2, 128, NSP])
    wkv_in = din("wkvT", [2, 64, 4 * 4 * 64])
    cs_in = din("cs0", [2, 128, 4 * 4 * 64])
    kcT = din("kcT", [2, 4, 128, 2 * 2048])
    vcd = din("vc", [2, 4, 128, 16 * 256])
    cb_in = din("cbpack", [128, NCB])
    ca_in = din("capack", [128, NCA])
    mp_in = din("maskP", [128, 17 * 128])

    y_out = dout("y", [NT, 1024])
    o_dk = dout("o_dk", [2, NT, 256])
    o_dv = dout("o_dv", [2, NT, 256])
    o_small = dout("o_small", [2, 128, NOS])
    o_wkv = dout("o_wkv", [2, 5, 64, 256])
    o_cs = dout("o_cs", [2, 5, 128, 256])

    xres = nc.dram_tensor("xres", [NT, 1024], F32).ap()
    colsT = nc.dram_tensor("colsT", [7936, NT], F32).ap()
    hmT = nc.dram_tensor("hmT", [3072, NT], BF16).ap()
    gT = nc.dram_tensor("gT", [4096, NT], BF16).ap()

    c = Ctx(nc)
    sb = nc.alloc_sbuf_tensor

    def chk(name):
        if STOP == name:
            c.finish()
            print('STOP at', name, 'instructions', c.n_inst)
            raise _Stop()

    b1 = sb("b1", [128, 8, NT], BF16).ap()
    QT = sb("QT", [128, 2, NT], BF16).ap()
    KT = sb("KT", [128, 2, NT], BF16).ap()
    Vall = sb("Vall", [128, 17, 256], BF16).ap()
    cbf = sb("cbf", [128, NCB], F32).ap()
    cbb = sb("cbb", [128, NCB], BF16).ap()
    caf = sb("caf", [128, NCA], F32).ap()
    mpb = sb("mpb", [128, 17, 128], BF16).ap()
    arena2 = Arena(nc, "arena2", 56 * 1024)
    wslot = [arena2.alloc([6144], BF16) for i in range(2)]
    stg = [arena2.alloc([512]) for i in range(4)]
    xbuf = [arena2.alloc([1024]) for i in range(3)]
    sqj = arena2.alloc([1024])
    ubuf = [arena2.alloc([1024], BF16) for i in range(2)]
    ssb = [sb("ss%d" % i, [128, 4], F32).ap() for i in range(4)]
    gbc = arena2.alloc([1024])
    ppf = sb("ppf", [128, NPK], F32).ap()
    stgb = [sb("stgb%d" % i, [128, 512], BF16).ap() for i in range(2)]
    pmat = sb("pmat", [128, 1280], BF16).ap()
    spf = sb("spf", [128, NSP], F32).ap()
    pder = sb("pder", [128, 32], F32).ap()
    osm = sb("osm", [128, NOS], F32).ap()
    kcbuf = [sb("kcb%d" % i, [128, 2, 512], BF16).ap() for i in range(2)]
    vcbuf = [sb("vcb%d" % i, [128, 4, 256], BF16).ap() for i in range(2)]
    arena = Arena(nc, "arena", 56 * 1024)
    print("sbuf remaining after alloc", nc.sbuf_bytes_remaining)

    psf = [nc.alloc_psum_tensor("ps%d" % i, [128, 512], F32).ap() for i in range(7)]
    psb = nc.alloc_psum_tensor("psb", [128, 1024], BF16).ap()
    pstate = {'i': 0, 'w': 0, 's': 0, 'a': 0}

    def nextps():
        i = pstate['i'] % 5
        pstate['i'] += 1
        return psf[i], ('ps', i)

    def mkps(idx):
        st = {'i': 0}

        def f():
            i = idx[st['i'] % len(idx)]
            st['i'] += 1
            return psf[i], ('ps', i)
        return f

    def nextacc():
        i = 5 + pstate['a'] % 2
        pstate['a'] += 1
        return psf[i], ('ps', i)

    identb = cbb[:, CB['ident']:CB['ident'] + 128]
    identf = cbf[:, CB['ident']:CB['ident'] + 128]
    blk64b = cbb[:, CB['blk64']:CB['blk64'] + 128]
    onesb = cbb[:, CB['ones']:CB['ones'] + 64]
    tri_incl = cbf[0:64, CB['tri_incl']:CB['tri_incl'] + 64]
    tri_su = cbf[0:64, CB['tri_su']:CB['tri_su'] + 64]
    tri_sl = cbf[0:64, CB['tri_sl']:CB['tri_sl'] + 64]
    maskSb = cbb[:, CB['maskS']:CB['maskS'] + 68].rearrange("p (b q) -> p b q", b=17)
    maskNb = cbb[0:16, CB['maskN']:CB['maskN'] + 16].rearrange("p (s q) -> p s q", s=4)
    ropet = caf[:, CA['rope']:CA['rope'] + 17 * 64].rearrange("p (t d) -> p t d", t=17)
    resetm = caf[:, CA['reset']:CA['reset'] + 512]

    def pcol(name, i=0):
        return ppf[:, PK[name] + i:PK[name] + i + 1]

    def k_b1(cs, n0, nn):
        return [('b1', cc, t) for cc in cs for t in range(n0 // 128, (n0 + nn + 127) // 128)]

    c.dma('sp', cbf, cb_in, writes=['cbf'])
    c.dma('pool', cbb, cb_in, writes=['cbb'])
    c.dma('sp', caf, ca_in, writes=['caf'])
    c.dma('pool', mpb.rearrange("p a b -> p (a b)"), mp_in, writes=['mpb'])
    for t, (r0, R) in enumerate(TT128):
        xt = xbuf[t % 3]
        c.dma('sp', xt[:R], x_in[r0:r0 + R, :], writes=[('xt', t % 3)])
        c.dma('sp', xres[r0:r0 + R, :], xt[:R], reads=[('xt', t % 3)], writes=[('xres', t)])

    chk('setup')

    def norm_tile(xt, xk, t, r0, R, final=False):
        ss = ssb[t % 4]
        sk = ('ss', t % 4)
        c.op('pool', lambda e: e.tensor_tensor(out=sqj[:R], in0=xt[:R], in1=xt[:R], op=ALU.mult), [xk], ['sqj'])
        c.op('dve', lambda e: e.reduce_sum(out=ss[:R, 0:1], in_=sqj[:R], axis=AX.X), ['sqj'], [sk])
        c.op('act', lambda e: e.activation(out=ss[:R, 1:2], in_=ss[:R, 0:1], func=AF.Sqrt, bias=1e-6, scale=1.0 / 1024),
             [sk], [sk])
        c.op('dve', lambda e: e.reciprocal(out=ss[:R, 2:3], in_=ss[:R, 1:2]), [sk], [sk])
        if final:
            c.op('dve', lambda e: e.scalar_tensor_tensor(out=sqj[:R], in0=xt[:R], scalar=ss[:R, 2:3], in1=gbc[:R],
                                                         op0=ALU.mult, op1=ALU.mult), [xk, sk, 'gbc'], ['sqj'])
            c.dma('sp', y_out[r0:r0 + R, :], sqj[:R], reads=['sqj'])
            return
        ub = ubuf[t % 2]
        uk = ('ub', t % 2)
        c.op('dve', lambda e: e.scalar_tensor_tensor(out=ub[:R], in0=xt[:R], scalar=ss[:R, 2:3], in1=gbc[:R],
                                                     op0=ALU.mult, op1=ALU.mult), [xk, sk, 'gbc'], [uk])
        for cc in range(8):
            c.op('pe', lambda e: e.transpose(psb[:, cc * 128:cc * 128 + R], ub[:R, cc * 128:(cc + 1) * 128], identb[:R, :R]),
                 [uk, 'cbb'], ['psb'])
        c.op('act', lambda e: e.activation(out=b1[:, :, r0:r0 + R],
                                           in_=psb.rearrange("p (c t) -> p c t", c=8)[:, :, :R], func=AF.Copy),
             ['psb'], [('b1', cc, t) for cc in range(8)])

    def load_gains(gidx):
        c.dma('sp', gbc, gtab[gidx].partition_broadcast(128), writes=['gbc'])

    def norm_phase(gidx, final=False):
        load_gains(gidx)
        for t, (r0, R) in enumerate(TT128):
            xt = xbuf[t % 3]
            xk = ('xt', t % 3)
            c.dma('sp', xt[:R], xres[r0:r0 + R, :], reads=[('xres', t)], writes=[xk])
            norm_tile(xt, xk, t, r0, R, final)

    def getslot():
        i = pstate['w'] % 2
        pstate['w'] += 1
        return wslot[i], ('ws', i)

    def getstg():
        i = pstate['s'] % 4
        pstate['s'] += 1
        return stg[i], ('stg', i)

    def ldcols(dst, row0, nch, n0, T, q='sp'):
        src = colsT[row0:row0 + nch * 128].rearrange("(c p) t -> p c t", p=128)[:, :, n0:n0 + T]
        return src, [('cols', row0 // 128 + cc, n0 // 512) for cc in range(nch)]

    for l in range(DEPTH):
        c.dma('sp', ppf, ppack[l], writes=['ppf'])
        c.dma('pool', pmat, ppack[l][:, PMAT0:PMAT0 + 1280], writes=['pmat'])
        c.dma('sp', spf, spack[l], writes=['spf'])
        c.op('act', lambda e: e.activation(out=pder[:, 0:2], in_=ppf[:, PK['a_lam']:PK['a_lam'] + 2], func=AF.Exp, scale=-1.0),
             ['ppf'], ['pder'])
        c.op('act', lambda e: e.activation(out=pder[:, 0:2], in_=pder[:, 0:2], func=AF.Ln, bias=1.0, scale=1.0), ['pder'], ['pder'])
        c.op('dve', lambda e: e.tensor_scalar(out=pder[:, 2:4], in0=pder[:, 0:2], scalar1=-16.0, scalar2=None, op0=ALU.mult),
             ['pder'], ['pder'])
        c.op('dve', lambda e: e.tensor_scalar(out=pder[:, 0:2], in0=pder[:, 0:2], scalar1=-8.0, scalar2=None, op0=ALU.mult),
             ['pder'], ['pder'])
        if l == 0:
            c.op('dve', lambda e: e.memset(pder[:, 4:8], 0.0), [], ['pder'])
        else:
            c.op('dve', lambda e: e.tensor_tensor(out=pder[:, 4:8], in0=ppf[:, PK['c_lb1']:PK['c_lb1'] + 4],
                                                  in1=ppf[:, PK['c_lb0']:PK['c_lb0'] + 4], op=ALU.subtract), ['ppf'], ['pder'])
            c.op('act', lambda e: e.activation(out=pder[:, 4:8], in_=pder[:, 4:8], func=AF.Sigmoid), ['pder'], ['pder'])
        c.op('dve', lambda e: e.tensor_scalar(out=pder[:, 8:12], in0=pder[:, 4:8], scalar1=-1.0, scalar2=1.0,
                                              op0=ALU.mult, op1=ALU.add), ['pder'], ['pder'])
        c.op('dve', lambda e: e.tensor_scalar(out=pder[:, 12:16], in0=ppf[:, PK['b64_ka']:PK['b64_ka'] + 4], scalar1=-1.0,
                                              scalar2=1.0, op0=ALU.mult, op1=ALU.add), ['ppf'], ['pder'])
        c.op('dve', lambda e: e.tensor_scalar(out=pder[:, 16:20], in0=ppf[:, PK['b64_w0']:PK['b64_w0'] + 4], scalar1=0.5, scalar2=None, op0=ALU.mult),
             ['ppf'], ['pder'])
        c.op('dve', lambda e: e.tensor_scalar(out=pder[:, 20:24], in0=ppf[:, PK['b64_a0']:PK['b64_a0'] + 4], scalar1=0.5, scalar2=None, op0=ALU.mult),
             ['ppf'], ['pder'])
        c.op('dve', lambda e: e.memset(osm, 0.0), [], ['osm'])

        if l == 0:
            norm_phase(0)

        chk('norm1')
        wv = w_in[l].rearrange("(k p) c -> p k c", p=128)
        blocks = [(c0, 768) for c0 in (0, 768, 1536, 2304)] + [(3840 + i * 768, 768) for i in range(5)] + [(7680, 256)]
        ev = 0
        for (c0, ncol) in blocks:
            ws, wk = getslot()
            ws3 = ws.rearrange("p (k c) -> p k c", k=8)
            c.dma('pool', ws3[:, :, :ncol], wv[:, :, c0:c0 + ncol], writes=[wk])
            for j, (n0, nn) in enumerate(NTILES):
                for cc in range(ncol // 128):
                    ps, pk = nextps()
                    for k in range(8):
                        c.op('pe', lambda e: e.matmul(ps[:, :nn], lhsT=ws3[:, k, cc * 128:(cc + 1) * 128], rhs=b1[:, k, n0:n0 + nn],
                                                      start=(k == 0), stop=(k == 7)), [wk] + k_b1([k], n0, nn), [pk])
                    st, sk = getstg()
                    if c0 >= 3840:
                        sgb, sgk = stgb[ev % 2], ('stgb', ev % 2)
                        c.op('act', lambda e: e.activation(out=sgb[:, :nn], in_=ps[:, :nn], func=AF.Sigmoid), [pk], [sgk])
                        ev += 1
                        rb = c0 // 128 + cc
                        c.dma('sp', gT[(rb - 30) * 128:(rb - 29) * 128, n0:n0 + nn], sgb[:, :nn], reads=[sgk], writes=[('cols', rb, j)])
                        continue
                    elif ev % 2 == 0:
                        c.op('act', lambda e: e.activation(out=st[:, :nn], in_=ps[:, :nn], func=AF.Copy), [pk], [sk])
                    else:
                        c.op('dve', lambda e: e.tensor_copy(out=st[:, :nn], in_=ps[:, :nn]), [pk], [sk])
                    ev += 1
                    rb = c0 // 128 + cc
                    c.dma('sp', colsT[rb * 128:(rb + 1) * 128, n0:n0 + nn], st[:, :nn], reads=[sk], writes=[('cols', rb, j)])

        chk('inproj')
        c.barrier()
        arena.reset()
        NB3 = 3
        dqk = [arena.alloc([512]) for _ in range(NB3)]
        dvv = [arena.alloc([256]) for _ in range(NB3)]
        rot = [arena.alloc([512]) for _ in range(NB3)]
        rtmp = [arena.alloc([512]) for _ in range(NB3)]
        rbb = [arena.alloc([512], BF16) for _ in range(NB3)]
        ws, wk = getslot()
        ws3 = ws.rearrange("p (k c) -> p k c", k=8)
        c.dma('pool', ws3, wv[:, :, 3072:3840], writes=[wk])

        def qkvA(t, r0, R):
            i2 = t % NB3
            psA, pkA = nextps()
            psB, pkB = nextps()
            for k in range(8):
                c.op('pe', lambda e: e.matmul(psA[:R, :512], lhsT=b1[:, k, r0:r0 + R], rhs=ws3[:, k, 0:512], start=(k == 0), stop=(k == 7)),
                     [wk, ('b1', k, t)], [pkA])
            for k in range(8):
                c.op('pe', lambda e: e.matmul(psB[:R, :256], lhsT=b1[:, k, r0:r0 + R], rhs=ws3[:, k, 512:768], start=(k == 0), stop=(k == 7)),
                     [wk, ('b1', k, t)], [pkB])
            qk, vv = dqk[i2], dvv[i2]
            kq, kv = ('dqk', i2), ('dvv', i2)
            c.op('act', lambda e: e.activation(out=qk[:R], in_=psA[:R, :512], func=AF.Copy), [pkA], [kq])
            c.op('dve', lambda e: e.tensor_copy(out=vv[:R], in_=psB[:R, :256]), [pkB], [kv])
            c.dma('sp', o_dv[l, r0:r0 + R, :], vv[:R], reads=[kv])
            c.op('pool', lambda e: e.tensor_copy(out=Vall[:R, t, :], in_=vv[:R]), [kv], [('Vall', t)])

        def qkvB(t, r0, R):
            i2 = t % NB3
            qk, ro, rt_, rb_ = dqk[i2], rot[i2], rtmp[i2], rbb[i2]
            kq, kr, kt_, kb_ = ('dqk', i2), ('rot', i2), ('rtmp', i2), ('rbb', i2)
            q4 = qk[:R].rearrange("p (h two d) -> p h two d", h=8, two=2)
            r4 = ro[:R].rearrange("p (h two d) -> p h two d", h=8, two=2)
            t4 = rt_[:R].rearrange("p (h two d) -> p h two d", h=8, two=2)
            cosb = ropet[:R, t, 0:32].unsqueeze(1).to_broadcast([R, 8, 32])
            sinb = ropet[:R, t, 32:64].unsqueeze(1).to_broadcast([R, 8, 32])
            c.op('dve', lambda e: e.tensor_tensor(out=r4[:, :, 0, :], in0=q4[:, :, 0, :], in1=cosb, op=ALU.mult), [kq, 'caf'], [kr])
            c.op('dve', lambda e: e.tensor_tensor(out=t4[:, :, 0, :], in0=q4[:, :, 1, :], in1=sinb, op=ALU.mult), [kq, 'caf'], [kt_])
            c.op('dve', lambda e: e.tensor_tensor(out=r4[:, :, 0, :], in0=r4[:, :, 0, :], in1=t4[:, :, 0, :], op=ALU.subtract), [kr, kt_], [kr])
            c.op('dve', lambda e: e.tensor_tensor(out=r4[:, :, 1, :], in0=q4[:, :, 1, :], in1=cosb, op=ALU.mult), [kq, 'caf'], [kr])
            c.op('dve', lambda e: e.tensor_tensor(out=t4[:, :, 1, :], in0=q4[:, :, 0, :], in1=sinb, op=ALU.mult), [kq, 'caf'], [kt_])
            c.op('dve', lambda e: e.tensor_tensor(out=r4[:, :, 1, :], in0=r4[:, :, 1, :], in1=t4[:, :, 1, :], op=ALU.add), [kr, kt_], [kr])
            c.dma('sp', o_dk[l, r0:r0 + R, :], ro[:R, 256:512], reads=[kr])
            c.op('act', lambda e: e.activation(out=rb_[:R], in_=ro[:R], func=AF.Copy), [kr], [kb_])

        def qkvC(t, r0, R):
            i2 = t % NB3
            rb_, kb_ = rbb[i2], ('rbb', i2)
            for i in range(4):
                c.op('pe', lambda e: e.transpose(psb[:, i * 128:i * 128 + R], rb_[:R, i * 128:(i + 1) * 128], identb[:R, :R]),
                     [kb_, 'cbb'], ['psb'])
            pv = psb[:, 0:512].rearrange("p (i t) -> p i t", i=4)
            c.op('act', lambda e: e.activation(out=QT[:, :, r0:r0 + R], in_=pv[:, 0:2, :R], func=AF.Copy), ['psb'], [('QT', t)])
            c.op('dve', lambda e: e.tensor_copy(out=KT[:, :, r0:r0 + R], in_=pv[:, 2:4, :R]), ['psb'], [('KT', t)])
        ntt = len(TT128)
        for it_ in range(ntt + 2):
            if it_ < ntt:
                qkvA(it_, *TT128[it_])
            if 0 <= it_ - 1 < ntt:
                qkvB(it_ - 1, *TT128[it_ - 1])
            if 0 <= it_ - 2 < ntt:
                qkvC(it_ - 2, *TT128[it_ - 2])

        chk('qkv')
        for bi, nm in enumerate('ABCD'):
            if nm not in enable:
                c.op('pool', lambda e: e.memset(b1[:, 2 * bi:2 * bi + 2, :], 0.0), [],
                     [('b1', cc, t) for cc in (2 * bi, 2 * bi + 1) for t in range(17)])

        def gen_D(arena):
            nextps = cad_ps
            nextacc = mkps([6])
            dcnt = {'i': 0}
            esb = [arena.alloc([512], BF16) for _ in range(3)]
            recb = [arena.alloc([256]) for _ in range(2)]
            for qb in range(16):
                pso, pko = nextacc()
                for kb in range(qb + 1):
                    pss2 = [nextps(), nextps()]
                    for h in range(4):
                        hh, hp = h % 2, h // 2
                        pss, pks = pss2[hh]
                        c.op('pe', lambda e: e.matmul(pss[:, hp * 128:(hp + 1) * 128], lhsT=KT[hh * 64:(hh + 1) * 64, hp, kb * 128:(kb + 1) * 128],
                                                      rhs=QT[hh * 64:(hh + 1) * 64, hp, qb * 128:(qb + 1) * 128], start=True, stop=True),
                             [('KT', kb), ('QT', qb)], [pks])
                    dcnt['i'] += 1
                    ei = dcnt['i'] % 3
                    es, ek = esb[ei], ('esb', ei)
                    e3 = es.rearrange("p (h q) -> p h q", h=4)
                    for hh in range(2):
                        pss, pks = pss2[hh]
                        c.op('act', lambda e: e.activation(out=e3[:, hh::2, :], in_=pss[:, 0:256].rearrange("p (a q) -> p a q", a=2),
                                                           func=AF.Exp, scale=0.125), [pks], [ek])
                    c.op('dve', lambda e: e.tensor_tensor(out=e3, in0=e3, in1=mpb[:, qb - kb, :].unsqueeze(1).to_broadcast([128, 4, 128]),
                                                          op=ALU.mult), [ek, 'mpb'], [ek])
                    for h in range(4):
                        hh, hp = h % 2, h // 2
                        c.op('pe', lambda e: e.matmul(pso[hh * 64:(hh + 1) * 64, hp * 128:(hp + 1) * 128], lhsT=Vall[:, kb, h * 64:(h + 1) * 64],
                                                      rhs=es[:, h * 128:(h + 1) * 128], start=(kb == 0 and hp == 0), stop=(kb == qb)),
                             [('Vall', kb), ek], [pko])
                        c.op('pe', lambda e: e.matmul(pso[hh * 64:(hh + 1) * 64, 256 + hp * 128:256 + (hp + 1) * 128], lhsT=onesb,
                                                      rhs=es[:, h * 128:(h + 1) * 128], start=False, stop=(kb == qb)),
                             ['cbb', ek], [pko])
                    yield
                rc, rk = recb[qb % 2], ('recb', qb % 2)
                c.op('dve', lambda e: e.reciprocal(out=rc, in_=pso[:, 256:512]), [pko], [rk])
                c.op('dve', lambda e: e.tensor_tensor(out=b1[:, 6:8, qb * 128:(qb + 1) * 128],
                                                      in0=pso[:, 0:256].rearrange("p (a t) -> p a t", a=2),
                                                      in1=rc.rearrange("p (a t) -> p a t", a=2), op=ALU.mult),
                     [pko, rk], [('b1', 6, qb), ('b1', 7, qb)])
            for s in range(4):
                q0 = 2048 + 4 * s
                pso, pko = nextacc()
                for g in range(4):
                    kct, kck = kcbuf[g % 2], ('kcb', g % 2)
                    vct, vck = vcbuf[g % 2], ('vcb', g % 2)
                    c.dma('pool', kct, kcT[l, s].rearrange("p (a k) -> p a k", a=2)[:, :, g * 512:(g + 1) * 512], writes=[kck])
                    c.dma('pool', vct, vcd[l, s].rearrange("p (b d) -> p b d", b=16)[:, g * 4:(g + 1) * 4, :], writes=[vck])
                    pss2 = [nextps(), nextps()]
                    for kb in range(4):
                        for h in range(4):
                            hh, hp = h % 2, h // 2
                            pss, pks = pss2[hh]
                            o0 = (kb * 2 + hp) * 4
                            c.op('pe', lambda e: e.matmul(pss[:, o0:o0 + 4], lhsT=kct[hh * 64:(hh + 1) * 64, hp, kb * 128:(kb + 1) * 128],
                                                          rhs=QT[hh * 64:(hh + 1) * 64, hp, q0:q0 + 4], start=True, stop=True),
                                 [kck, ('QT', 16)], [pks])
                    dcnt['i'] += 1
                    ei = dcnt['i'] % 3
                    es, ek = esb[ei], ('esb', ei)
                    e5 = es[:, 0:64].rearrange("p (b a x q) -> p b a x q", b=4, a=2, x=2)
                    for hh in range(2):
                        pss, pks = pss2[hh]
                        c.op('act', lambda e: e.activation(out=e5[:, :, :, hh, :], in_=pss[:, 0:32].rearrange("p (b a q) -> p b a q", b=4, a=2),
                                                           func=AF.Exp, scale=0.125), [pks], [ek])
                    e4 = es[:, 0:64].rearrange("p (b h q) -> p b h q", b=4, h=4)
                    c.op('dve', lambda e: e.tensor_tensor(out=e4, in0=e4, in1=maskSb[:, g * 4:(g + 1) * 4, :].unsqueeze(2).to_broadcast([128, 4, 4, 4]),
                                                          op=ALU.mult), [ek, 'cbb'], [ek])
                    for kb in range(4):
                        for h in range(4):
                            hh, hp = h % 2, h // 2
                            o0 = (kb * 4 + h) * 4
                            first = (g == 0 and kb == 0)
                            c.op('pe', lambda e: e.matmul(pso[hh * 64:(hh + 1) * 64, hp * 4:hp * 4 + 4], lhsT=vct[:, kb, h * 64:(h + 1) * 64],
                                                          rhs=es[:, o0:o0 + 4], start=(first and hp == 0), stop=False), [vck, ek], [pko])
                            c.op('pe', lambda e: e.matmul(pso[hh * 64:(hh + 1) * 64, 8 + hp * 4:8 + hp * 4 + 4], lhsT=onesb,
                                                          rhs=es[:, o0:o0 + 4], start=False, stop=False), ['cbb', ek], [pko])
                    yield
                pss2 = [nextps(), nextps()]
                for h in range(4):
                    hh, hp = h % 2, h // 2
                    pss, pks = pss2[hh]
                    c.op('pe', lambda e: e.matmul(pss[0:16, hp * 4:hp * 4 + 4], lhsT=KT[hh * 64:(hh + 1) * 64, hp, 2048:2064],
                                                  rhs=QT[hh * 64:(hh + 1) * 64, hp, q0:q0 + 4], start=True, stop=True),
                         [('KT', 16), ('QT', 16)], [pks])
                dcnt['i'] += 1
                ei = dcnt['i'] % 3
                es, ek = esb[ei], ('esb', ei)
                e3n = es[0:16, 0:16].rearrange("p (a x q) -> p a x q", a=2, x=2)
                for hh in range(2):
                    pss, pks = pss2[hh]
                    c.op('act', lambda e: e.activation(out=e3n[:, :, hh, :], in_=pss[0:16, 0:8].rearrange("p (a q) -> p a q", a=2),
                                                       func=AF.Exp, scale=0.125), [pks], [ek])
                e3 = es[0:16, 0:16].rearrange("p (h q) -> p h q", h=4)
                c.op('dve', lambda e: e.tensor_tensor(out=e3, in0=e3, in1=maskNb[:, s, :].unsqueeze(1).to_broadcast([16, 4, 4]), op=ALU.mult),
                     [ek, 'cbb'], [ek])
                for h in range(4):
                    hh, hp = h % 2, h // 2
                    c.op('pe', lambda e: e.matmul(pso[hh * 64:(hh + 1) * 64, hp * 4:hp * 4 + 4], lhsT=Vall[0:16, 16, h * 64:(h + 1) * 64],
                                                  rhs=es[0:16, h * 4:h * 4 + 4], start=False, stop=True), [('Vall', 16), ek], [pko])
                    c.op('pe', lambda e: e.matmul(pso[hh * 64:(hh + 1) * 64, 8 + hp * 4:8 + hp * 4 + 4], lhsT=onesb[0:16, :],
                                                  rhs=es[0:16, h * 4:h * 4 + 4], start=False, stop=True), ['cbb', ek], [pko])
                rc, rk = recb[s % 2], ('recb', s % 2)
                c.op('dve', lambda e: e.reciprocal(out=rc[:, 0:8], in_=pso[:, 8:16]), [pko], [rk])
                c.op('dve', lambda e: e.tensor_tensor(out=b1[:, 6:8, q0:q0 + 4], in0=pso[:, 0:8].rearrange("p (a t) -> p a t", a=2),
                                                      in1=rc[:, 0:8].rearrange("p (a t) -> p a t", a=2), op=ALU.mult),
                     [pko, rk], [('b1', 6, 16), ('b1', 7, 16)])

        def gen_A(arena):
            nextps = cad_ps
            TA = 128
            axp = arena.alloc([2, TA + 3])
            agt = arena.alloc([2, TA])
            xc = arena.alloc([2, TA])
            xcb = arena.alloc([2, TA], BF16)
            gx = arena.alloc([2, TA])
            ga = arena.alloc([2, TA])
            av = arena.alloc([2, TA])
            bi_ = arena.alloc([2, TA])
            hh_ = arena.alloc([2, TA])
            hst = arena.alloc([2])
            segs = [(n0, TA, 0, n0 == 0, n0 + TA == 2048) for n0 in range(0, 2048, TA)] + \
                   [(2048 + 4 * s, 4, 1 + s, True, True) for s in range(4)]
            for (n0, T, sq, first, last) in segs:
                if first:
                    if sq == 0:
                        c.op('dve', lambda e: e.memset(axp[:, :, 0:3], 0.0), [], ['axp'])
                        c.op('dve', lambda e: e.memset(hst, 0.0), [], ['hst'])
                    else:
                        s = sq - 1
                        c.op('dve', lambda e: e.tensor_copy(out=axp[:, :, 0:3],
                                                            in_=spf[:, SP['a_conv'] + s * 6:SP['a_conv'] + s * 6 + 6].rearrange("p (c j) -> p c j", c=2)),
                             ['spf'], ['axp'])
                        c.op('dve', lambda e: e.tensor_copy(out=hst, in_=spf[:, SP['a_h'] + s * 2:SP['a_h'] + s * 2 + 2]), ['spf'], ['hst'])
                else:
                    c.op('dve', lambda e: e.tensor_copy(out=axp[:, :, 0:3], in_=axp[:, :, TA:TA + 3]), ['axp'], ['axp'])
                src, ks = ldcols(None, 0, 2, n0, T)
                c.dma('sp', axp[:, :, 3:3 + T], src, reads=ks, writes=['axp'])
                src, ks = ldcols(None, 256, 2, n0, T)
                c.dma('sp', agt[:, :, :T], src, reads=ks, writes=['agt'])
                for cc in range(2):
                    c.op('dve', lambda e: e.tensor_scalar(out=xc[:, cc, :T], in0=axp[:, cc, 0:T], scalar1=pcol('a_cw', cc * 4),
                                                          scalar2=pcol('a_cb', cc), op0=ALU.mult, op1=ALU.add), ['axp', 'ppf'], ['xc'])
                    for j in range(1, 4):
                        c.op('dve', lambda e: e.scalar_tensor_tensor(out=xc[:, cc, :T], in0=axp[:, cc, j:j + T], scalar=pcol('a_cw', cc * 4 + j),
                                                                     in1=xc[:, cc, :T], op0=ALU.mult, op1=ALU.add), ['axp', 'ppf', 'xc'], ['xc'])
                c.op('act', lambda e: e.activation(out=xcb[:, :, :T], in_=xc[:, :, :T], func=AF.Copy), ['xc'], ['xcb'])
                yield
                psx, pkx = nextps()
                for cc in range(2):
                    c.op('pe', lambda e: e.matmul(psx[:, cc * 128:cc * 128 + T], lhsT=pmat[:, cc * 128:(cc + 1) * 128], rhs=xcb[:, cc, :T],
                                                  start=True, stop=True), ['pmat', 'xcb'], [pkx])
                for cc in range(2):
                    c.op('pe', lambda e: e.matmul(psx[:, 256 + cc * 128:256 + cc * 128 + T], lhsT=pmat[:, 256 + cc * 128:256 + (cc + 1) * 128], rhs=xcb[:, cc, :T],
                                                  start=True, stop=True), ['pmat', 'xcb'], [pkx])
                for cc in range(2):
                    c.op('act', lambda e: e.activation(out=gx[:, cc, :T], in_=psx[:, cc * 128:cc * 128 + T], func=AF.Sigmoid, bias=pcol('a_gxb', cc), scale=1.0),
                         [pkx, 'ppf'], ['gx'])
                    c.op('act', lambda e: e.activation(out=ga[:, cc, :T], in_=psx[:, 256 + cc * 128:256 + cc * 128 + T], func=AF.Sigmoid, bias=pcol('a_gab', cc), scale=1.0),
                         [pkx, 'ppf'], ['ga'])
                yield
                for cc in range(2):
                    c.op('act', lambda e: e.activation(out=av[:, cc, :T], in_=ga[:, cc, :T], func=AF.Exp, scale=pder[:, cc:cc + 1]), ['ga', 'pder'], ['av'])
                    c.op('act', lambda e: e.activation(out=bi_[:, cc, :T], in_=ga[:, cc, :T], func=AF.Exp, scale=pder[:, 2 + cc:3 + cc]), ['ga', 'pder'], ['bi'])
                c.op('act', lambda e: e.activation(out=bi_[:, :, :T], in_=bi_[:, :, :T], func=AF.Sqrt, bias=1.0, scale=-1.0), ['bi'], ['bi'])
                c.op('dve', lambda e: e.tensor_tensor(out=bi_[:, :, :T], in0=bi_[:, :, :T], in1=gx[:, :, :T], op=ALU.mult), ['bi', 'gx'], ['bi'])
                c.op('dve', lambda e: e.tensor_tensor(out=bi_[:, :, :T], in0=bi_[:, :, :T], in1=xc[:, :, :T], op=ALU.mult), ['bi', 'xc'], ['bi'])
                for cc in range(2):
                    c.op('dve', lambda e: e.tensor_tensor_scan(out=hh_[:, cc, :T], data0=av[:, cc, :T], data1=bi_[:, cc, :T], initial=hst[:, cc:cc + 1],
                                                               op0=ALU.mult, op1=ALU.add), ['av', 'bi', 'hst'], ['hh'])
                c.op('dve', lambda e: e.tensor_copy(out=hst, in_=hh_[:, :, T - 1]), ['hh'], ['hst'])
                yield
                c.op('act', lambda e: e.activation(out=agt[:, :, :T], in_=agt[:, :, :T], func=AF.Gelu_apprx_tanh), ['agt'], ['agt'])
                c.op('dve', lambda e: e.tensor_tensor(out=b1[:, 0:2, n0:n0 + T], in0=hh_[:, :, :T], in1=agt[:, :, :T], op=ALU.mult),
                     ['hh', 'agt'], k_b1([0, 1], n0, T))
                if last:
                    c.op('dve', lambda e: e.tensor_copy(out=osm[:, OS['a_h'] + sq * 2:OS['a_h'] + sq * 2 + 2], in_=hst), ['hst'], ['osm'])
                    c.op('dve', lambda e: e.tensor_copy(out=osm[:, OS['a_conv'] + sq * 6:OS['a_conv'] + sq * 6 + 6].rearrange("p (c j) -> p c j", c=2),
                                                        in_=axp[:, :, T:T + 3]), ['axp'], ['osm'])

        c.barrier()
        arena.reset()
        arena2.reset()
        gens = []
        cad_ps = mkps([4, 5])
        if 'B' in enable:
            ST_ = arena2.alloc([4, 64])
            STb_ = arena2.alloc([4, 64], BF16)
            tok = {'next': 0}
            for tid in range(2):
                bps = mkps([0, 1] if tid == 0 else [2, 3])
                gens.append(build_B(c, nc, l, arena, bps, psb, b1, colsT, ppf, pder, spf, pmat, wkv_in, o_wkv, osm, cbf, cbb, caf, k_b1, ldcols, pcol,
                                    tid, ST_, STb_, tok, arena2))
        if 'C' in enable:
            gens.append(build_C(c, nc, l, arena2, cad_ps, psb, b1, colsT, ppf, pder, spf, cs_in, o_cs, cbf, cbb, caf, k_b1, ldcols, pcol))
        if 'D' in enable:
            gens.append(gen_D(arena2))
        if 'A' in enable:
            gens.append(gen_A(arena2))
        while gens:
            for g in list(gens):
                try:
                    next(g)
                except StopIteration:
                    gens.remove(g)
        chk('B')
        c.barrier()
        arena.reset()
        wo = arena.alloc([8, 1024], BF16)
        gtb = [arena.alloc([4, 512], BF16) for _ in range(3)]
        prodb = [arena.alloc([4, 512]) for _ in range(2)]
        mg = arena.alloc([8, 512], BF16)
        wbr = []
        for i in range(2):
            ws, wk = getslot()
            w4 = ws[:, 0:4096].rearrange("p (k c) -> p k c", k=4)
            c.dma('pool', w4, w_br[l].rearrange("(k p) c -> p k c", p=128)[:, i * 4:(i + 1) * 4, :], writes=[wk])
            wbr.append((w4, wk))
        c.dma('pool', wo, w_out[l].rearrange("(k p) c -> p k c", p=128), writes=['wo'])
        gview = gT.rearrange("(n c p) t -> p n c t", n=4, c=8, p=128)
        gi = 0
        load_gains(2 + l)
        pendN = [None]
        for j, (n0, nn) in enumerate(NTILES):
            for dj in range(8):
                gt, gk = gtb[gi % 3], ('gtb', gi % 3)
                pr, prk = prodb[gi % 2], ('prodb', gi % 2)
                gi += 1
                c.dma('sp', gt[:, :, :nn], gview[:, :, dj, n0:n0 + nn], reads=[('cols', 30 + n * 8 + dj, j) for n in range(4)], writes=[gk])
                for n in range(4):
                    ps, pk = nextps()
                    w4, wk = wbr[n // 2]
                    for kc in range(2):
                        c.op('pe', lambda e: e.matmul(ps[:, :nn], lhsT=w4[:, (n % 2) * 2 + kc, dj * 128:(dj + 1) * 128], rhs=b1[:, 2 * n + kc, n0:n0 + nn],
                                                      start=(kc == 0), stop=(kc == 1)), [wk] + k_b1([2 * n + kc], n0, nn), [pk])
                    c.op('dve', lambda e: e.tensor_tensor(out=pr[:, n, :nn], in0=ps[:, :nn], in1=gt[:, n, :nn], op=ALU.mult), [pk, gk], [prk])
                c.op('pool', lambda e: e.tensor_tensor(out=pr[:, 0, :nn], in0=pr[:, 0, :nn], in1=pr[:, 1, :nn], op=ALU.add), [prk], [prk])
                c.op('pool', lambda e: e.tensor_tensor(out=pr[:, 2, :nn], in0=pr[:, 2, :nn], in1=pr[:, 3, :nn], op=ALU.add), [prk], [prk])
                c.op('pool', lambda e: e.tensor_tensor(out=mg[:, dj, :nn], in0=pr[:, 0, :nn], in1=pr[:, 2, :nn], op=ALU.add), [prk], [('mg', dj)])
            for st in range((nn + 127) // 128):
                R = min(128, nn - st * 128)
                t = n0 // 128 + st
                xt, xk = xbuf[t % 3], ('xt', t % 3)
                c.dma('sp', xt[:R], xres[n0 + st * 128:n0 + st * 128 + R, :], reads=[('xres', t)], writes=[xk])
                for half in range(2):
                    ps, pk = nextps()
                    for kc in range(8):
                        c.op('pe', lambda e: e.matmul(ps[:R, :512], lhsT=mg[:, kc, st * 128:st * 128 + R], rhs=wo[:, kc, half * 512:(half + 1) * 512],
                                                      start=(kc == 0), stop=(kc == 7)), [('mg', kc), 'wo'], [pk])
                    c.op('dve', lambda e: e.tensor_tensor(out=xt[:R, half * 512:(half + 1) * 512], in0=ps[:R, :512],
                                                          in1=xt[:R, half * 512:(half + 1) * 512], op=ALU.add), [pk, xk], [xk])
                c.dma('sp', xres[n0 + st * 128:n0 + st * 128 + R, :], xt[:R], reads=[xk], writes=[('xres', t)])
                if pendN[0] is not None:
                    pendN[0]()
                pendN[0] = (lambda xt=xt, xk=xk, t=t, r0=n0 + st * 128, R=R: norm_tile(xt, xk, t, r0, R))

        if pendN[0] is not None:
            pendN[0]()
        chk('merge')
        c.barrier()
        arena.reset()
        hgp = [arena.alloc([516]) for _ in range(2)]
        hgs = [arena.alloc([4, 6]) for _ in range(2)]
        cvb = [arena.alloc([512]) for _ in range(3)]
        hmo = [arena.alloc([512], BF16) for _ in range(3)]
        hub = [arena.alloc([512], BF16) for _ in range(3)]
        pendB = [None]
        wgv = w_g[l].rearrange("(k p) c -> p k c", p=128)
        wuv = w_u[l].rearrange("(k p) c -> p k c", p=128)
        hi = 0
        def load_fb(fb):
            ws, wk = getslot()
            w5 = ws[:, 0:4096].rearrange("p (g k c) -> p g k c", g=2, k=8)
            c.dma('pool', w5[:, 0], wgv[:, :, fb * 256:(fb + 1) * 256], writes=[wk])
            c.dma('pool', w5[:, 1], wuv[:, :, fb * 256:(fb + 1) * 256], writes=[wk])
            return w5, wk
        nxtw = load_fb(0)
        for fb in range(12):
            w5, wk = nxtw
            if fb + 1 < 12:
                nxtw = load_fb(fb + 1)
            for fc in range(2):
                f = fb * 2 + fc
                hg, hk = hgp[fc], ('hgp', fc)
                hs, hsk = hgs[fc], ('hgs', fc)
                c.op('pool', lambda e: e.memset(hg[:, 0:2], 0.0), [], [hk])
                for j, (n0, nn) in enumerate(NTILES):
                    psg, pkg = nextps()
                    psu, pku = nextps()
                    for k in range(8):
                        c.op('pe', lambda e: e.matmul(psg[:, :nn], lhsT=w5[:, 0, k, fc * 128:(fc + 1) * 128], rhs=b1[:, k, n0:n0 + nn],
                                                      start=(k == 0), stop=(k == 7)), [wk] + k_b1([k], n0, nn), [pkg])
                    for k in range(8):
                        c.op('pe', lambda e: e.matmul(psu[:, :nn], lhsT=w5[:, 1, k, fc * 128:(fc + 1) * 128], rhs=b1[:, k, n0:n0 + nn],
                                                      start=(k == 0), stop=(k == 7)), [wk] + k_b1([k], n0, nn), [pku])
                    cv, ck = cvb[hi % 3], ('cvb', hi % 3)
                    ho, hok = hmo[hi % 3], ('hmo', hi % 3)
                    hi += 1
                    w0, w1, w2_, bb = pcol('f_cw', f * 3), pcol('f_cw', f * 3 + 1), pcol('f_cw', f * 3 + 2), pcol('f_cb', f)
                    hu, huk = hub[hi % 3], ('hub', hi % 3)
                    c.op('act', lambda e: e.activation(out=hu[:, :nn], in_=psu[:, :nn], func=AF.Copy), [pku], [huk])
                    if j < 4:
                        c.op('act', lambda e: e.activation(out=hg[:, 2:2 + nn], in_=psg[:, :nn], func=AF.Copy), [pkg], [hk])
                        c.op('dve', lambda e: e.tensor_scalar(out=cv[:, :nn], in0=hg[:, 0:nn], scalar1=w0, scalar2=bb, op0=ALU.mult, op1=ALU.add),
                             [hk, 'ppf'], [ck])
                        c.op('dve', lambda e: e.scalar_tensor_tensor(out=cv[:, :nn], in0=hg[:, 1:1 + nn], scalar=w1, in1=cv[:, :nn], op0=ALU.mult, op1=ALU.add),
                             [hk, 'ppf', ck], [ck])
                        c.op('dve', lambda e: e.scalar_tensor_tensor(out=cv[:, :nn], in0=hg[:, 2:2 + nn], scalar=w2_, in1=cv[:, :nn], op0=ALU.mult, op1=ALU.add),
                             [hk, 'ppf', ck], [ck])
                        if j == 3:
                            c.op('pool', lambda e: e.tensor_copy(out=osm[:, OS['f_conv'] + f * 2:OS['f_conv'] + f * 2 + 2], in_=hg[:, nn:nn + 2]), [hk], ['osm'])
                        else:
                            c.op('pool', lambda e: e.tensor_copy(out=hg[:, 0:2], in_=hg[:, nn:nn + 2]), [hk], [hk])
                    else:
                        c.op('pool', lambda e: e.tensor_copy(out=hs[:, :, 0:2],
                                                             in_=spf[:, SP['f_conv']:SP['f_conv'] + 192].rearrange("p (s f j) -> p s f j", s=4, f=24)[:, :, f, :]),
                             ['spf'], [hsk])
                        c.op('act', lambda e: e.activation(out=hs[:, :, 2:6], in_=psg[:, 0:16].rearrange("p (s i) -> p s i", s=4), func=AF.Copy), [pkg], [hsk])
                        cv3 = cv[:, 0:16].rearrange("p (s i) -> p s i", s=4)
                        c.op('dve', lambda e: e.tensor_scalar(out=cv3, in0=hs[:, :, 0:4], scalar1=w0, scalar2=bb, op0=ALU.mult, op1=ALU.add),
                             [hsk, 'ppf'], [ck])
                        c.op('dve', lambda e: e.scalar_tensor_tensor(out=cv3, in0=hs[:, :, 1:5], scalar=w1, in1=cv3, op0=ALU.mult, op1=ALU.add),
                             [hsk, 'ppf', ck], [ck])
                        c.op('dve', lambda e: e.scalar_tensor_tensor(out=cv3, in0=hs[:, :, 2:6], scalar=w2_, in1=cv3, op0=ALU.mult, op1=ALU.add),
                             [hsk, 'ppf', ck], [ck])
                        for s in range(4):
                            o0 = OS['f_conv'] + (1 + s) * 48 + f * 2
                            c.op('pool', lambda e: e.tensor_copy(out=osm[:, o0:o0 + 2], in_=hs[:, s, 4:6]), [hsk], ['osm'])
                    def stageB(cv=cv, ck=ck, ho=ho, hok=hok, hu=hu, huk=huk, f=f, j=j, n0=n0, nn=nn):
                        c.op('act', lambda e: e.activation(out=cv[:, :nn], in_=cv[:, :nn], func=AF.Gelu_apprx_tanh), [ck], [ck])
                        c.op('dve', lambda e: e.tensor_tensor(out=ho[:, :nn], in0=hu[:, :nn], in1=cv[:, :nn], op=ALU.mult), [huk, ck], [hok])
                        c.dma('sp', hmT[f * 128:(f + 1) * 128, n0:n0 + nn], ho[:, :nn], reads=[hok], writes=[('hmT', f, j)])
                    if pendB[0] is not None:
                        pendB[0]()
                    pendB[0] = stageB
        if pendB[0] is not None:
            pendB[0]()
        chk('ffn1')
        c.barrier()
        arena.reset()
        xt4 = [arena.alloc([1024]) for _ in range(4)]
        hm = arena.alloc([24, 512], BF16)
        wdv = w_d[l].rearrange("(f p) c -> p f c", p=128)
        hmv = hmT.rearrange("(f p) t -> p f t", p=128)
        lastl = (l == DEPTH - 1)
        pendF = [None]
        load_gains(4 if lastl else l + 1)
        wseq = [(j_, h_, k_) for j_ in range(len(NTILES)) for h_ in range(2) for k_ in range(2)]

        def load_wd(q):
            j_, h_, k_ = wseq[q]
            ws, wk = getslot()
            w3 = ws.rearrange("p (f c) -> p f c", f=12)
            c.dma('pool', w3, wdv[:, k_ * 12:(k_ + 1) * 12, h_ * 512:(h_ + 1) * 512], writes=[wk])
            return w3, wk
        wq = 0
        nxtw = load_wd(0)
        for j, (n0, nn) in enumerate(NTILES):
            c.dma('sp', hm[:, :, :nn], hmv[:, :, n0:n0 + nn], reads=[('hmT', f, j) for f in range(24)], writes=['hm'])
            nst = (nn + 127) // 128
            for st in range(nst):
                R = min(128, nn - st * 128)
                t = n0 // 128 + st
                c.dma('sp', xt4[st][:R], xres[n0 + st * 128:n0 + st * 128 + R, :], reads=[('xres', t)], writes=[('xt4', st)])
            for half in range(2):
                banks = [nextps() for _ in range(nst)]
                for kg in range(2):
                    w3, wk = nxtw
                    wq += 1
                    if wq < len(wseq):
                        nxtw = load_wd(wq)
                    for fk in range(12):
                        for st in range(nst):
                            R = min(128, nn - st * 128)
                            ps, pk = banks[st]
                            c.op('pe', lambda e: e.matmul(ps[:R, :512], lhsT=hm[:, kg * 12 + fk, st * 128:st * 128 + R], rhs=w3[:, fk, :],
                                                          start=(kg == 0 and fk == 0), stop=(kg == 1 and fk == 11)), ['hm', wk], [pk])
                for st in range(nst):
                    R = min(128, nn - st * 128)
                    ps, pk = banks[st]
                    c.op('dve', lambda e: e.tensor_tensor(out=xt4[st][:R, half * 512:(half + 1) * 512], in0=ps[:R, :512],
                                                          in1=xt4[st][:R, half * 512:(half + 1) * 512], op=ALU.add), [pk, ('xt4', st)], [('xt4', st)])
            for st in range(nst):
                R = min(128, nn - st * 128)
                t = n0 // 128 + st
                if not lastl:
                    c.dma('sp', xres[n0 + st * 128:n0 + st * 128 + R, :], xt4[st][:R], reads=[('xt4', st)], writes=[('xres', t)])
                if pendF[0] is not None:
                    pendF[0]()
                pendF[0] = (lambda st=st, t=t, r0=n0 + st * 128, R=R: norm_tile(xt4[st], ('xt4', st), t, r0, R, final=lastl))
            if pendF[0] is not None:
                pendF[0]()
                pendF[0] = None
        chk('ffn2')
        c.dma('sp', o_small[l], osm, reads=['osm'])
        c.barrier()

    c.finish()
    print("instructions", c.n_inst, "waits", c.n_wait)


def build_C(c, nc, l, arena, nextps, psb, b1, colsT, ppf, pder, spf, cs_in, o_cs, cbf, cbb, caf, k_b1, ldcols, pcol):
    TC = 128
    identb = cbb[:, CB['ident']:CB['ident'] + 128]
    blk64b = cbb[:, CB['blk64']:CB['blk64'] + 128]
    tri_incl = cbf[0:64, CB['tri_incl']:CB['tri_incl'] + 64]
    resetm = caf[:, CA['reset']:CA['reset'] + 512]
    qf = arena.alloc([4, TC])
    ff = arena.alloc([4, TC])
    kf = arena.alloc([4, TC])
    bc = arena.alloc([4, TC])
    ep = arena.alloc([4, TC])
    vf = arena.alloc([2, TC])
    gf = arena.alloc([2, TC])
    oraw = arena.alloc([2, TC])
    rs = arena.alloc([2, TC])
    qt = arena.alloc([4, TC], BF16)
    kt = arena.alloc([4, TC], BF16)
    vb = arena.alloc([2, TC], BF16)
    sqb = arena.alloc([2, TC], BF16)
    elb = arena.alloc([4, 4])
    attb = [arena.alloc([4, 64], BF16) for _ in range(2)]
    tmb = [arena.alloc([768], BF16) for _ in range(2)]
    S = arena.alloc([4, 64])
    Sb = arena.alloc([4, 64], BF16)
    segs = [(n0, TC, 0, n0 == 0, n0 + TC == 2048) for n0 in range(0, 2048, TC)] + \
           [(2048 + 4 * s, 4, 1 + s, True, True) for s in range(4)]
    it = 0
    for (n0, T, sq, first, last) in segs:
        CL = 64 if T >= 64 else T
        nck = T // CL
        if first:
            if sq == 0:
                c.op('dve', lambda e: e.memset(S, 0.0), [], ['S'])
            else:
                c.dma('sp', S.rearrange("p h v -> p (h v)"), cs_in[l][:, (sq - 1) * 256:sq * 256], writes=['S'])
            c.op('act', lambda e: e.activation(out=Sb, in_=S, func=AF.Copy), ['S'], ['Sb'])
        src, ks = ldcols(None, 1536, 4, n0, T)
        c.dma('sp', qf[:, :, :T], src, reads=ks, writes=['qf'])
        src, ks = ldcols(None, 2048, 4, n0, T)
        c.dma('sp', ff[:, :, :T], src, reads=ks, writes=['ff'])
        src, ks = ldcols(None, 2560, 2, n0, T)
        c.dma('sp', vf[:, :, :T], src, reads=ks, writes=['vf'])
        src, ks = ldcols(None, 2816, 2, n0, T)
        c.dma('sp', gf[:, :, :T], src, reads=ks, writes=['gf'])
        yield
        c.op('act', lambda e: e.activation(out=ff[:, :, :T], in_=ff[:, :, :T], func=AF.Sigmoid), ['ff'], ['ff'])
        yield
        for h in range(4):
            c.op('dve', lambda e: e.tensor_scalar(out=ff[:, h, :T], in0=ff[:, h, :T], scalar1=pder[:, 8 + h:9 + h], scalar2=pder[:, 4 + h:5 + h],
                                                  op0=ALU.mult, op1=ALU.add), ['ff', 'pder'], ['ff'])
        yield
        c.op('dve', lambda e: e.tensor_scalar(out=kf[:, :, :T], in0=ff[:, :, :T], scalar1=-1.0, scalar2=1.0, op0=ALU.mult, op1=ALU.add), ['ff'], ['kf'])
        yield
        c.op('act', lambda e: e.activation(out=ff[:, :, :T], in_=ff[:, :, :T], func=AF.Ln), ['ff'], ['ff'])
        yield
        for h in range(4):
            c.op('dve', lambda e: e.tensor_tensor_scan(out=bc[:, h, :T], data0=resetm[:, 0:T], data1=ff[:, h, :T], initial=0.0,
                                                       op0=ALU.mult, op1=ALU.add), ['ff', 'caf'], ['bc'])
        yield
        c.op('act', lambda e: e.activation(out=ep[:, :, :T], in_=bc[:, :, :T], func=AF.Exp), ['bc'], ['ep'])
        yield
        c.op('dve', lambda e: e.tensor_tensor(out=qt[:, :, :T], in0=qf[:, :, :T], in1=ep[:, :, :T], op=ALU.mult), ['qf', 'ep'], ['qt'])
        yield
        c.op('dve', lambda e: e.tensor_copy(out=elb[:, :, 0:nck], in_=ep[:, :, CL - 1:T:CL]), ['ep'], ['elb'])
        yield
        c.op('act', lambda e: e.activation(out=bc[:, :, :T], in_=bc[:, :, :T], func=AF.Exp, scale=-1.0), ['bc'], ['bc'])
        yield
        c.op('dve', lambda e: e.tensor_tensor(out=kt[:, :, :T], in0=kf[:, :, :T], in1=bc[:, :, :T], op=ALU.mult), ['kf', 'bc'], ['kt'])
        yield
        c.op('act', lambda e: e.activation(out=vb[:, :, :T], in_=vf[:, :, :T], func=AF.Copy), ['vf'], ['vb'])
        yield
        for ck in range(nck):
            cs_ = slice(ck * CL, (ck + 1) * CL)
            at, atk = attb[it % 2], ('attb', it % 2)
            tm, tmk = tmb[it % 2], ('tmb', it % 2)
            it += 1
            pa, pka = nextps()
            for h in range(4):
                c.op('pe', lambda e: e.matmul(pa[0:CL, h * 64:h * 64 + CL], lhsT=kt[:, h, cs_], rhs=qt[:, h, cs_], start=True, stop=True),
                     ['kt', 'qt'], [pka])
            c.op('dve', lambda e: e.tensor_tensor(out=at[0:CL, :, 0:CL], in0=pa[0:CL, 0:256].rearrange("p (h t) -> p h t", h=4)[:, :, 0:CL],
                                                  in1=tri_incl[0:CL, 0:CL].unsqueeze(1).to_broadcast([CL, 4, CL]), op=ALU.mult), [pka, 'cbf'], [atk])
            for h in range(4):
                c.op('pe', lambda e: e.transpose(psb[0:CL, h * 128:(h + 1) * 128], kt[:, h, cs_], identb), ['kt', 'cbb'], ['psb'])
            for hp in range(2):
                c.op('pe', lambda e: e.transpose(psb[0:CL, 512 + hp * 128:512 + (hp + 1) * 128], vb[:, hp, cs_], identb), ['vb', 'cbb'], ['psb'])
            c.op('act', lambda e: e.activation(out=tm[0:CL, :], in_=psb[0:CL, 0:768], func=AF.Copy), ['psb'], [tmk])
            yield
            po, pko = nextps()
            for h in range(4):
                hh, hp = h % 2, h // 2
                c.op('pe', lambda e: e.matmul(po[hh * 64:(hh + 1) * 64, hp * 64:hp * 64 + CL], lhsT=Sb[:, h, :], rhs=qt[:, h, cs_], start=True, stop=False),
                     ['Sb', 'qt'], [pko])
                c.op('pe', lambda e: e.matmul(po[hh * 64:(hh + 1) * 64, hp * 64:hp * 64 + CL], lhsT=tm[0:CL, 512 + h * 64:512 + (h + 1) * 64],
                                              rhs=at[0:CL, h, 0:CL], start=False, stop=True), [tmk, atk], [pko])
            c.op('act', lambda e: e.activation(out=oraw[:, :, cs_], in_=po[:, 0:128].rearrange("p (a t) -> p a t", a=2)[:, :, 0:CL], func=AF.Copy),
                 [pko], ['oraw'])
            pS, pkS = nextps()
            for h in range(4):
                c.op('pe', lambda e: e.matmul(pS[:, h * 64:(h + 1) * 64], lhsT=tm[0:CL, h * 128:(h + 1) * 128], rhs=tm[0:CL, 512 + h * 64:512 + (h + 1) * 64],
                                              start=True, stop=True), [tmk], [pkS])
            c.op('dve', lambda e: e.tensor_tensor(out=S, in0=pS[:, 0:256].rearrange("p (h v) -> p h v", h=4), in1=S, op=ALU.add), [pkS, 'S'], ['S'])
            c.op('dve', lambda e: e.tensor_tensor(out=S, in0=S, in1=elb[:, :, ck:ck + 1].to_broadcast([128, 4, 64]), op=ALU.mult), ['S', 'elb'], ['S'])
            c.op('act', lambda e: e.activation(out=Sb, in_=S, func=AF.Copy), ['S'], ['Sb'])
            yield
        c.op('pool', lambda e: e.tensor_tensor(out=sqb[:, :, :T], in0=oraw[:, :, :T], in1=oraw[:, :, :T], op=ALU.mult), ['oraw'], ['sqb'])
        pn, pkn = nextps()
        for hp in range(2):
            c.op('pe', lambda e: e.matmul(pn[:, hp * 256:hp * 256 + T], lhsT=blk64b, rhs=sqb[:, hp, :T], start=True, stop=True), ['sqb', 'cbb'], [pkn])
        c.op('act', lambda e: e.activation(out=rs[:, :, :T], in_=pn.rearrange("p (a t) -> p a t", a=2)[:, :, :T], func=AF.Ln, bias=1e-6, scale=1.0 / 64),
             [pkn], ['rs'])
        c.op('act', lambda e: e.activation(out=rs[:, :, :T], in_=rs[:, :, :T], func=AF.Exp, scale=-0.5), ['rs'], ['rs'])
        c.op('dve', lambda e: e.tensor_tensor(out=rs[:, :, :T], in0=rs[:, :, :T], in1=oraw[:, :, :T], op=ALU.mult), ['rs', 'oraw'], ['rs'])
        c.op('act', lambda e: e.activation(out=gf[:, :, :T], in_=gf[:, :, :T], func=AF.Silu), ['gf'], ['gf'])
        c.op('dve', lambda e: e.scalar_tensor_tensor(out=b1[:, 4:6, n0:n0 + T], in0=rs[:, :, :T], scalar=pcol('c_ng'), in1=gf[:, :, :T],
                                                     op0=ALU.mult, op1=ALU.mult), ['rs', 'gf', 'ppf'], k_b1([4, 5], n0, T))
        if last:
            c.dma('pool', o_cs[l, sq], S.rearrange("p h v -> p (h v)"), reads=['S'])
        yield


def build_B(c, nc, l, arena, nextps, psb, b1, colsT, ppf, pder, spf, pmat, wkv_in, o_wkv, osm, cbf, cbb, caf, k_b1, ldcols, pcol,
            tid, ST, STb, tok, xarena):
    c = CtxTag(c, 'B%d' % tid)
    TB = 64
    identb = cbb[:, CB['ident']:CB['ident'] + 128]
    identf = cbf[:, CB['ident']:CB['ident'] + 128]
    ones64 = cbb[0:64, CB['ones']:CB['ones'] + 64]
    mask2 = cbf[0:64, CB['tri_incl']:CB['tri_incl'] + 128].rearrange("p (a t) -> p a t", a=2)
    tri_sl = cbf[0:64, CB['tri_sl']:CB['tri_sl'] + 64]
    resetm = caf[:, CA['reset']:CA['reset'] + 512]
    w2b = pmat[0:64, 512:768]
    a2b = pmat[0:64, 768:1024]
    g2b = pmat[:, 1024:1280]

    def p64(name, n):
        return ppf[0:64, PK[name]:PK[name] + n]
    Xb = [xarena.alloc([14, TB + 1]) for _ in range(2)]
    Gb = [xarena.alloc([TB + 1]) for _ in range(2)]
    CM = arena.alloc([14, TB])
    gm = arena.alloc([TB])
    tw = arena.alloc([TB], BF16)
    alb = arena.alloc([TB], BF16)
    gs = arena.alloc([TB], BF16)
    ld = arena.alloc([4, TB])
    asg = arena.alloc([4, TB])
    gg = arena.alloc([4, TB])
    kk = arena.alloc([4, TB])
    km = arena.alloc([4, TB])
    cs = arena.alloc([4, TB])
    eb = arena.alloc([4, TB])
    t1 = arena.alloc([4, TB])
    yraw = arena.alloc([4, TB])
    AR = arena.alloc([4, 1, 128], BF16)
    Bt = arena.alloc([4, TB], BF16)
    Kt = arena.alloc([4, TB], BF16)
    Vt = arena.alloc([4, TB], BF16)
    h16 = arena.alloc([4, TB], BF16)
    outB = arena.alloc([4, TB], BF16)
    PC = arena.alloc([4, 2])
    tmb = [arena.alloc([768], BF16) for _ in range(1)]
    G1 = arena.alloc([4, 128], BF16)
    G2 = arena.alloc([4, 128], BF16)
    Wb = [arena.alloc([4, 192], BF16) for _ in range(2)]
    Zs = arena.alloc([4, 64], BF16)
    Us = arena.alloc([4, 64], BF16)
    allsegs = [(n0, TB, 0, n0 == 0, n0 + TB == 2048) for n0 in range(0, 2048, TB)] + \
              [(2048 + 4 * s, 4, 1 + s, True, True) for s in range(4)]
    it = 0
    mysegs = [(gi, sg) for gi, sg in enumerate(allsegs) if gi % 2 == tid]
    xsrc = colsT[512:1408].rearrange("(j k) t -> k j t", k=64)

    def issue_load(k):
        gi, (n0, T, sq, first, last) = mysegs[k]
        X, G, kx, kg = Xb[k % 2], Gb[k % 2], ('X', k % 2), ('G', k % 2)
        if first:
            if sq == 0:
                c.op('pool', lambda e: e.memset(X[0:64, :, 0:1], 0.0), [], [kx])
                c.op('pool', lambda e: e.memset(G[:, 0:1], 0.0), [], [kg])
            else:
                s_ = sq - 1
                c.op('pool', lambda e: e.tensor_copy(out=X[0:64, :, 0], in_=spf[0:64, SP['b_sh64'] + s_ * 14:SP['b_sh64'] + s_ * 14 + 14]), ['spf'], [kx])
                c.op('pool', lambda e: e.tensor_copy(out=G[:, 0:1], in_=spf[:, SP['b_shg'] + s_:SP['b_shg'] + s_ + 1]), ['spf'], [kg])
            c.dma('sp', X[0:64, :, 1:1 + T], xsrc[:, :, n0:n0 + T], reads=[('cols', rb, n0 // 512) for rb in range(4, 11)], writes=[kx])
            c.dma('sp', G[:, 1:1 + T], colsT[1408:1536, n0:n0 + T], reads=[('cols', 11, n0 // 512)], writes=[kg])
        else:
            jj = sorted(set([(n0 - 1) // 512, n0 // 512]))
            c.dma('sp', X[0:64, :, 0:1 + T], xsrc[:, :, n0 - 1:n0 + T], reads=[('cols', rb, j_) for rb in range(4, 11) for j_ in jj], writes=[kx])
            c.dma('sp', G[:, 0:1 + T], colsT[1408:1536, n0 - 1:n0 + T], reads=[('cols', 11, j_) for j_ in jj], writes=[kg])
    issue_load(0)
    for k, (gi, (n0, T, sq, first, last)) in enumerate(mysegs):
        if k + 1 < len(mysegs):
            issue_load(k + 1)
        X, G, kx, kg = Xb[k % 2], Gb[k % 2], ('X', k % 2), ('G', k % 2)
        CL = 64 if T >= 64 else T
        nck = T // CL
        nr = 6 if CL == 64 else 2
        yield
        c.op('dve', lambda e: e.tensor_tensor(out=CM[0:64, :, :T], in0=X[0:64, :, 0:T], in1=X[0:64, :, 1:1 + T], op=ALU.subtract), [kx], ['CM'])
        yield
        c.op('dve', lambda e: e.tensor_tensor(out=CM[0:64, :, :T], in0=CM[0:64, :, :T], in1=p64('b64_mu', 14).unsqueeze(2).to_broadcast([64, 14, T]),
                                              op=ALU.mult), ['CM', 'ppf'], ['CM'])
        yield
        c.op('dve', lambda e: e.tensor_tensor(out=CM[0:64, :, :T], in0=CM[0:64, :, :T], in1=X[0:64, :, 1:1 + T], op=ALU.add), ['CM', kx], ['CM'])
        yield
        c.op('dve', lambda e: e.tensor_tensor(out=gm[:, :T], in0=G[:, 0:T], in1=G[:, 1:1 + T], op=ALU.subtract), [kg], ['gm'])
        yield
        c.op('dve', lambda e: e.scalar_tensor_tensor(out=gm[:, :T], in0=gm[:, :T], scalar=pcol('b_mug'), in1=G[:, 1:1 + T], op0=ALU.mult, op1=ALU.add),
             ['gm', kg, 'ppf'], ['gm'])
        if last:
            c.op('pool', lambda e: e.tensor_copy(out=osm[0:64, OS['b_sh64'] + sq * 14:OS['b_sh64'] + sq * 14 + 14], in_=X[0:64, :, T]), [kx], ['osm'])
            c.op('pool', lambda e: e.tensor_copy(out=osm[:, OS['b_shg'] + sq:OS['b_shg'] + sq + 1], in_=G[:, T:T + 1]), [kg], ['osm'])
        rr, kr, vr = CM[0:64, 0:4, :T], CM[0:64, 4:8, :T], CM[0:64, 8:12, :T]
        yield
        c.op('act', lambda e: e.activation(out=tw[0:64, :T], in_=CM[0:64, 12, :T], func=AF.Tanh), ['CM'], ['tw'])
        yield
        c.op('act', lambda e: e.activation(out=alb[0:64, :T], in_=CM[0:64, 13, :T], func=AF.Copy), ['CM'], ['alb'])
        yield
        c.op('act', lambda e: e.activation(out=gm[:, :T], in_=gm[:, :T], func=AF.Tanh, scale=0.5), ['gm'], ['gm'])
        c.op('act', lambda e: e.activation(out=gs[:, :T], in_=gm[:, :T], func=AF.Copy, bias=0.5, scale=0.5), ['gm'], ['gs'])
        yield
        pw, pkw = nextps()
        yield
        for h in range(4):
            c.op('pe', lambda e: e.matmul(pw[0:64, h * 128:h * 128 + T], lhsT=w2b[:, h * 64:(h + 1) * 64], rhs=tw[0:64, :T], start=True, stop=True),
                 ['pmat', 'tw'], [pkw])
        yield
        for h in range(4):
            c.op('act', lambda e: e.activation(out=ld[0:64, h, :T], in_=pw[0:64, h * 128:h * 128 + T], func=AF.Tanh, bias=pder[0:64, 16 + h:17 + h], scale=0.5),
                 [pkw, 'pder'], ['ld'])
        yield
        c.op('dve', lambda e: e.tensor_scalar(out=ld[0:64, :, :T], in0=ld[0:64, :, :T], scalar1=-0.5 * EM05, scalar2=-0.5 * EM05, op0=ALU.mult, op1=ALU.add),
             ['ld'], ['ld'])
        pa, pka = nextps()
        yield
        for h in range(4):
            c.op('pe', lambda e: e.matmul(pa[0:64, h * 128:h * 128 + T], lhsT=a2b[:, h * 64:(h + 1) * 64], rhs=alb[0:64, :T], start=True, stop=True),
                 ['pmat', 'alb'], [pka])
        yield
        for h in range(4):
            c.op('act', lambda e: e.activation(out=asg[0:64, h, :T], in_=pa[0:64, h * 128:h * 128 + T], func=AF.Tanh, bias=pder[0:64, 20 + h:21 + h], scale=0.5),
                 [pka, 'pder'], ['asg'])
        yield
        c.op('dve', lambda e: e.tensor_scalar(out=asg[0:64, :, :T], in0=asg[0:64, :, :T], scalar1=0.5, scalar2=0.5, op0=ALU.mult, op1=ALU.add), ['asg'], ['asg'])
        pg, pkg = nextps()
        yield
        for h in range(4):
            c.op('pe', lambda e: e.matmul(pg[0:64, h * 128:h * 128 + T], lhsT=g2b[:, h * 64:(h + 1) * 64], rhs=gs[:, :T], start=True, stop=True),
                 ['pmat', 'gs'], [pkg])
        yield
        c.op('act', lambda e: e.activation(out=gg[0:64, :, :T], in_=pg[0:64, :].rearrange("p (h t) -> p h t", h=4)[:, :, :T], func=AF.Copy), [pkg], ['gg'])
        yield
        yield
        c.op('dve', lambda e: e.tensor_tensor(out=kk[0:64, :, :T], in0=kr, in1=p64('b64_kk', 4).unsqueeze(2).to_broadcast([64, 4, T]), op=ALU.mult),
             ['CM', 'ppf'], ['kk'])
        yield
        c.op('dve', lambda e: e.tensor_tensor(out=h16[0:64, :, :T], in0=kk[0:64, :, :T], in1=kk[0:64, :, :T], op=ALU.mult), ['kk'], ['h16'])
        pn, pkn = nextps()
        yield
        for h in range(4):
            c.op('pe', lambda e: e.matmul(pn[0:64, h * 128:h * 128 + T], lhsT=ones64, rhs=h16[0:64, h, :T], start=True, stop=True), ['cbb', 'h16'], [pkn])
        yield
        c.op('act', lambda e: e.activation(out=t1[0:64, :, :T], in_=pn[0:64, :].rearrange("p (h t) -> p h t", h=4)[:, :, :T], func=AF.Ln,
                                           bias=1e-24, scale=1.0), [pkn], ['t1'])
        yield
        c.op('act', lambda e: e.activation(out=t1[0:64, :, :T], in_=t1[0:64, :, :T], func=AF.Exp, scale=-0.5), ['t1'], ['t1'])
        yield
        c.op('dve', lambda e: e.tensor_tensor(out=kk[0:64, :, :T], in0=kk[0:64, :, :T], in1=t1[0:64, :, :T], op=ALU.mult), ['kk', 't1'], ['kk'])
        yield
        c.op('dve', lambda e: e.tensor_tensor(out=t1[0:64, :, :T], in0=asg[0:64, :, :T], in1=p64('b64_ka', 4).unsqueeze(2).to_broadcast([64, 4, T]), op=ALU.mult),
             ['asg', 'ppf'], ['t1'])
        yield
        c.op('dve', lambda e: e.tensor_tensor(out=t1[0:64, :, :T], in0=t1[0:64, :, :T], in1=pder[0:64, 12:16].unsqueeze(2).to_broadcast([64, 4, T]), op=ALU.add),
             ['t1', 'pder'], ['t1'])
        yield
        c.op('dve', lambda e: e.tensor_tensor(out=km[0:64, :, :T], in0=kr, in1=t1[0:64, :, :T], op=ALU.mult), ['CM', 't1'], ['km'])
        yield
        yield
        for h in range(4):
            c.op('dve', lambda e: e.tensor_tensor_scan(out=cs[0:64, h, :T], data0=resetm[0:64, 0:T], data1=ld[0:64, h, :T], initial=0.0,
                                                       op0=ALU.mult, op1=ALU.add), ['ld', 'caf'], ['cs'])
        AR4 = AR[0:64].rearrange("p h c (a t) -> p h c a t", a=2)

        def ch4(ap):
            return ap.rearrange("p h (c t) -> p h c t", c=nck)
        yield
        c.op('dve', lambda e: e.tensor_tensor(out=eb[0:64, :, :T], in0=cs[0:64, :, :T], in1=ld[0:64, :, :T], op=ALU.subtract), ['cs', 'ld'], ['eb'])
        yield
        c.op('act', lambda e: e.activation(out=eb[0:64, :, :T], in_=eb[0:64, :, :T], func=AF.Exp), ['eb'], ['eb'])
        yield
        c.op('dve', lambda e: e.scalar_tensor_tensor(out=AR4[:, :, 0:nck, 1, 0:CL], in0=ch4(kk[0:64, :, :T]), scalar=-1.0, in1=ch4(eb[0:64, :, :T]),
                                                     op0=ALU.mult, op1=ALU.mult), ['kk', 'eb'], ['AR'])
        yield
        c.op('act', lambda e: e.activation(out=eb[0:64, :, :T], in_=cs[0:64, :, :T], func=AF.Exp), ['cs', 'AR'], ['eb'])
        yield
        c.op('dve', lambda e: e.tensor_tensor(out=AR4[:, :, 0:nck, 0, 0:CL], in0=ch4(rr), in1=ch4(eb[0:64, :, :T]), op=ALU.mult), ['CM', 'eb'], ['AR'])
        yield
        c.op('dve', lambda e: e.tensor_copy(out=PC[0:64, :, 0:nck], in_=eb[0:64, :, CL - 1:T:CL]), ['eb'], ['PC'])
        yield
        c.op('act', lambda e: e.activation(out=eb[0:64, :, :T], in_=cs[0:64, :, :T], func=AF.Exp, scale=-1.0), ['cs', 'AR', 'PC'], ['eb'])
        yield
        c.op('dve', lambda e: e.tensor_tensor(out=t1[0:64, :, :T], in0=kk[0:64, :, :T], in1=asg[0:64, :, :T], op=ALU.mult), ['kk', 'asg'], ['t1'])
        yield
        c.op('dve', lambda e: e.tensor_tensor(out=Bt[0:64, :, :T], in0=t1[0:64, :, :T], in1=eb[0:64, :, :T], op=ALU.mult), ['t1', 'eb'], ['Bt'])
        yield
        c.op('dve', lambda e: e.tensor_tensor(out=Kt[0:64, :, :T], in0=km[0:64, :, :T], in1=eb[0:64, :, :T], op=ALU.mult), ['km', 'eb'], ['Kt'])
        yield
        c.op('act', lambda e: e.activation(out=Vt[0:64, :, :T], in_=vr, func=AF.Copy), ['CM'], ['Vt'])
        yield
        for ck in range(nck):
            cs_ = slice(ck * CL, (ck + 1) * CL)
            tm, tmk = tmb[0], ('tmbB', 0)
            it += 1
            for j, src in enumerate((Bt, Kt, Vt)):
                for h in range(4):
                    c.op('pe', lambda e: e.transpose(psb[0:CL, (j * 4 + h) * 64:(j * 4 + h + 1) * 64], src[0:64, h, cs_], identb[0:64, 0:64]),
                         [('Bt', 'Kt', 'Vt')[j], 'cbb'], ['psb'])
            c.op('act', lambda e: e.activation(out=tm[0:CL, :], in_=psb[0:CL, 0:768], func=AF.Copy), ['psb'], [tmk])
            yield

            def Btm(h):
                return tm[0:CL, h * 64:(h + 1) * 64]

            def Ktm(h):
                return tm[0:CL, 256 + h * 64:256 + (h + 1) * 64]

            def Vtm(h):
                return tm[0:CL, 512 + h * 64:512 + (h + 1) * 64]
            P1, pk1 = nextps()
            P2, pk2 = nextps()
            for h in range(4):
                c.op('pe', lambda e: e.matmul(P1[0:CL, h * 128:h * 128 + 128], lhsT=Bt[0:64, h, cs_], rhs=AR[0:64, h, ck, :], start=True, stop=True),
                     ['Bt', 'AR'], [pk1])
            for h in range(4):
                c.op('pe', lambda e: e.matmul(P2[0:CL, h * 128:h * 128 + 128], lhsT=Kt[0:64, h, cs_], rhs=AR[0:64, h, ck, :], start=True, stop=True),
                     ['Kt', 'AR'], [pk2])
            m2b = mask2[0:CL, :, 0:CL].unsqueeze(1).to_broadcast([CL, 4, 2, CL])
            W0, W1 = Wb[0], Wb[1]
            W0v = W0[0:CL].rearrange("p h (a t) -> p h a t", a=3)
            W1v = W1[0:CL].rearrange("p h (a t) -> p h a t", a=3)
            G1v = G1[0:CL].rearrange("p h (a t) -> p h a t", a=2)
            G2v = G2[0:CL].rearrange("p h (a t) -> p h a t", a=2)
            P1v = P1[0:CL, :].rearrange("p (h a t) -> p h a t", h=4, a=2)
            P2v = P2[0:CL, :].rearrange("p (h a t) -> p h a t", h=4, a=2)
            c.op('dve', lambda e: e.tensor_tensor(out=G1v[:, :, :, 0:CL], in0=P1v[:, :, :, 0:CL], in1=m2b, op=ALU.mult), [pk1, 'cbf'], ['G1'])
            c.op('dve', lambda e: e.tensor_tensor(out=G2v[:, :, :, 0:CL], in0=P2v[:, :, :, 0:CL], in1=m2b, op=ALU.mult), [pk2, 'cbf'], ['G2'])
            P3, pk3 = nextps()
            for h in range(4):
                c.op('pe', lambda e: e.matmul(P3[0:CL, h * 64:h * 64 + CL], lhsT=AR[0:64, h, ck, 64:64 + CL], rhs=Bt[0:64, h, cs_], start=True, stop=True),
                     ['Bt', 'AR'], [pk3])
            c.op('act', lambda e: e.activation(out=W0v[:, :, 0, 0:CL], in_=G1v[:, :, 1, 0:CL], func=AF.Copy), ['G1'], ['W0'])
            c.op('act', lambda e: e.activation(out=W0v[:, :, 1, 0:CL], in_=identb[0:CL, 0:CL].unsqueeze(1).to_broadcast([CL, 4, CL]), func=AF.Copy),
                 ['cbb'], ['W0'])
            c.op('dve', lambda e: e.tensor_tensor(out=W0v[:, :, 2, 0:CL], in0=P3[0:CL, 0:256].rearrange("p (h t) -> p h t", h=4)[:, :, 0:CL],
                                                  in1=tri_sl[0:CL, 0:CL].unsqueeze(1).to_broadcast([CL, 4, CL]), op=ALU.mult), [pk3, 'cbf'], ['W0'])
            yield
            cur, nxt, curk, nxtk = W0v, W1v, 'W0', 'W1'
            for r_ in range(nr):
                lastr = (r_ == nr - 1)
                PA, pkA = nextps()
                PAv = PA[0:CL, :].rearrange("p (h a t) -> p h a t", h=4, a=2)
                for h in range(4):
                    for a_ in range(2):
                        if lastr and a_ == 0:
                            continue
                        c.op('pe', lambda e: e.matmul(PAv[:, h, a_, 0:CL], lhsT=cur[:, h, 2, 0:CL], rhs=cur[:, h, a_, 0:CL], start=True, stop=True),
                             [curk], [pkA])
                if not lastr:
                    PB, pkB = nextps()
                    PBv = PB[0:CL, 0:256].rearrange("p (h t) -> p h t", h=4)
                    for h in range(4):
                        c.op('pe', lambda e: e.matmul(PBv[:, h, 0:CL], lhsT=cur[:, h, 0, 0:CL], rhs=cur[:, h, 2, 0:CL], start=True, stop=True), [curk], [pkB])
                    c.op('act', lambda e: e.activation(out=nxt[:, :, 0, 0:CL], in_=PAv[:, :, 0, 0:CL], func=AF.Copy), [pkA], [nxtk])
                    c.op('act', lambda e: e.activation(out=nxt[:, :, 2, 0:CL], in_=PBv[:, :, 0:CL], func=AF.Copy), [pkB], [nxtk])
                c.op('dve', lambda e: e.tensor_tensor(out=nxt[:, :, 1, 0:CL], in0=PAv[:, :, 1, 0:CL], in1=cur[:, :, 1, 0:CL], op=ALU.add), [pkA, curk], [nxtk])
                cur, nxt, curk, nxtk = nxt, cur, nxtk, curk
                yield
            TTv, TTk = cur, curk

            def At(h):
                return AR[0:64, h, ck, 64:64 + CL]

            def Rt(h):
                return AR[0:64, h, ck, 0:CL]
            while tok['next'] != gi:
                yield
            if first:
                if sq == 0:
                    c.op('dve', lambda e: e.memset(ST[0:64], 0.0), [], ['ST'])
                else:
                    c.dma('sp', ST[0:64].rearrange("p h v -> p (h v)"), wkv_in[l][:, (sq - 1) * 256:sq * 256], writes=['ST'])
                c.op('act', lambda e: e.activation(out=STb[0:64], in_=ST[0:64], func=AF.Copy), ['ST'], ['STb'])
            PZ, pkZ = nextps()
            for h in range(4):
                c.op('pe', lambda e: e.matmul(PZ[0:CL, h * 64:(h + 1) * 64], lhsT=At(h), rhs=STb[0:64, h, :], start=True, stop=False), ['AR', 'STb'], [pkZ])
                c.op('pe', lambda e: e.matmul(PZ[0:CL, h * 64:(h + 1) * 64], lhsT=G2v[:, h, 1, 0:CL], rhs=Vtm(h), start=False, stop=True), ['G2', tmk], [pkZ])
            c.op('act', lambda e: e.activation(out=Zs[0:CL], in_=PZ[0:CL, 0:256].rearrange("p (h v) -> p h v", h=4), func=AF.Copy), [pkZ], ['Zs'])
            yield
            PU, pkU = nextps()
            for h in range(4):
                c.op('pe', lambda e: e.matmul(PU[0:CL, h * 64:(h + 1) * 64], lhsT=TTv[:, h, 1, 0:CL], rhs=Zs[0:CL, h, :], start=True, stop=True), [TTk, 'Zs'], [pkU])
            c.op('act', lambda e: e.activation(out=Us[0:CL], in_=PU[0:CL, 0:256].rearrange("p (h v) -> p h v", h=4), func=AF.Copy), [pkU], ['Us'])
            yield
            PY, pkY = nextps()
            for h in range(4):
                c.op('pe', lambda e: e.matmul(PY[0:64, h * 64:h * 64 + CL], lhsT=STb[0:64, h, :], rhs=Rt(h), start=True, stop=False), ['AR', 'STb'], [pkY])
                c.op('pe', lambda e: e.matmul(PY[0:64, h * 64:h * 64 + CL], lhsT=Us[0:CL, h, :], rhs=G1v[:, h, 0, 0:CL], start=False, stop=False), ['Us', 'G1'], [pkY])
                c.op('pe', lambda e: e.matmul(PY[0:64, h * 64:h * 64 + CL], lhsT=Vtm(h), rhs=G2v[:, h, 0, 0:CL], start=False, stop=True), [tmk, 'G2'], [pkY])
            c.op('act', lambda e: e.activation(out=yraw[0:64, :, cs_], in_=PY[0:64, 0:256].rearrange("p (h t) -> p h t", h=4)[:, :, 0:CL], func=AF.Copy),
                 [pkY], ['yraw'])
            PS, pkS = nextps()
            for h in range(4):
                c.op('pe', lambda e: e.matmul(PS[0:64, h * 64:(h + 1) * 64], lhsT=Btm(h), rhs=Us[0:CL, h, :], start=True, stop=False), [tmk, 'Us'], [pkS])
                c.op('pe', lambda e: e.matmul(PS[0:64, h * 64:(h + 1) * 64], lhsT=Ktm(h), rhs=Vtm(h), start=False, stop=True), [tmk], [pkS])
            c.op('dve', lambda e: e.tensor_tensor(out=ST[0:64], in0=PS[0:64, 0:256].rearrange("p (h v) -> p h v", h=4), in1=ST[0:64], op=ALU.add),
                 [pkS, 'ST'], ['ST'])
            c.op('dve', lambda e: e.tensor_tensor(out=ST[0:64], in0=ST[0:64], in1=PC[0:64, :, ck:ck + 1].to_broadcast([64, 4, 64]), op=ALU.mult),
                 ['ST', 'PC'], ['ST'])
            c.op('act', lambda e: e.activation(out=STb[0:64], in_=ST[0:64], func=AF.Copy), ['ST'], ['STb'])
            if last:
                c.dma('pool', o_wkv[l, sq], ST[0:64].rearrange("p h v -> p (h v)"), reads=['ST'])
            tok['next'] = gi + 1
            yield
        yield
        c.op('act', lambda e: e.activation(out=h16[0:64, :, :T], in_=yraw[0:64, :, :T], func=AF.Copy), ['yraw'], ['h16'])
        pm, pkm = nextps()
        yield
        for h in range(4):
            c.op('pe', lambda e: e.matmul(pm[0:64, h * 128:h * 128 + T], lhsT=ones64, rhs=h16[0:64, h, :T], start=True, stop=True), ['cbb', 'h16'], [pkm])
        pmv = pm[0:64, :].rearrange("p (h t) -> p h t", h=4)[:, :, :T]
        yield
        c.op('dve', lambda e: e.scalar_tensor_tensor(out=yraw[0:64, :, :T], in0=pmv, scalar=-1.0 / 64, in1=yraw[0:64, :, :T], op0=ALU.mult, op1=ALU.add),
             [pkm, 'yraw'], ['yraw'])
        yield
        c.op('dve', lambda e: e.tensor_tensor(out=h16[0:64, :, :T], in0=yraw[0:64, :, :T], in1=yraw[0:64, :, :T], op=ALU.mult), ['yraw'], ['h16'])
        pv_, pkv = nextps()
        yield
        for h in range(4):
            c.op('pe', lambda e: e.matmul(pv_[0:64, h * 128:h * 128 + T], lhsT=ones64, rhs=h16[0:64, h, :T], start=True, stop=True), ['cbb', 'h16'], [pkv])
        yield
        c.op('act', lambda e: e.activation(out=t1[0:64, :, :T], in_=pv_[0:64, :].rearrange("p (h t) -> p h t", h=4)[:, :, :T], func=AF.Ln,
                                           bias=64e-5, scale=1.0 / 64), [pkv], ['t1'])
        yield
        c.op('act', lambda e: e.activation(out=t1[0:64, :, :T], in_=t1[0:64, :, :T], func=AF.Exp, scale=-0.5), ['t1'], ['t1'])
        yield
        c.op('dve', lambda e: e.tensor_tensor(out=yraw[0:64, :, :T], in0=yraw[0:64, :, :T], in1=t1[0:64, :, :T], op=ALU.mult), ['yraw', 't1'], ['yraw'])
        yield
        c.op('dve', lambda e: e.tensor_tensor(out=yraw[0:64, :, :T], in0=yraw[0:64, :, :T], in1=p64('b64_lnw', 4).unsqueeze(2).to_broadcast([64, 4, T]), op=ALU.mult),
             ['yraw', 'ppf'], ['yraw'])
        yield
        c.op('dve', lambda e: e.tensor_tensor(out=yraw[0:64, :, :T], in0=yraw[0:64, :, :T], in1=p64('b64_lnb', 4).unsqueeze(2).to_broadcast([64, 4, T]), op=ALU.add),
             ['yraw', 'ppf'], ['yraw'])
        yield
        c.op('dve', lambda e: e.tensor_tensor(out=t1[0:64, :, :T], in0=rr, in1=km[0:64, :, :T], op=ALU.mult), ['CM', 'km'], ['t1'])
        yield
        c.op('dve', lambda e: e.tensor_tensor(out=h16[0:64, :, :T], in0=t1[0:64, :, :T], in1=p64('b64_rk', 4).unsqueeze(2).to_broadcast([64, 4, T]), op=ALU.mult),
             ['t1', 'ppf'], ['h16'])
        pb_, pkb = nextps()
        yield
        for h in range(4):
            c.op('pe', lambda e: e.matmul(pb_[0:64, h * 128:h * 128 + T], lhsT=ones64, rhs=h16[0:64, h, :T], start=True, stop=True), ['cbb', 'h16'], [pkb])
        yield
        c.op('dve', lambda e: e.tensor_tensor(out=t1[0:64, :, :T], in0=pb_[0:64, :].rearrange("p (h t) -> p h t", h=4)[:, :, :T], in1=vr, op=ALU.mult),
             [pkb, 'CM'], ['t1'])
        yield
        c.op('dve', lambda e: e.tensor_tensor(out=yraw[0:64, :, :T], in0=yraw[0:64, :, :T], in1=t1[0:64, :, :T], op=ALU.add), ['yraw', 't1'], ['yraw'])
        yield
        c.op('dve', lambda e: e.tensor_tensor(out=outB[0:64, :, :T], in0=yraw[0:64, :, :T], in1=gg[0:64, :, :T], op=ALU.mult), ['yraw', 'gg'], ['outB'])
        for hh in range(2):
            c.dma('pool', b1[hh * 64:(hh + 1) * 64, 2:4, n0:n0 + T], outB[0:64, hh::2, :T], reads=['outB'], writes=k_b1([2, 3], n0, T))
        yield


WNAMES = ['norm1_g', 'w_in', 'a_conv_w', 'a_conv_b', 'a_gx_w', 'a_gx_b', 'a_ga_w', 'a_ga_b', 'a_lambda', 'b_mu', 'b_w0', 'b_w2',
          'b_a0', 'b_a2', 'b_g2', 'b_k_k', 'b_k_a', 'b_r_k', 'b_ln_w', 'b_ln_b', 'c_lb', 'c_norm_g', 'w_branch', 'w_out',
          'norm2_g', 'ffn_w_gate', 'ffn_w_up', 'ffn_conv_w', 'ffn_conv_b', 'ffn_w_down', 'final_norm_g']
_CACHE = {}


def host_prep(inp):
    f = lambda a: np.ascontiguousarray(np.asarray(a, dtype=np.float32))
    W = {k: np.asarray(inp[k], np.float32) for k in WNAMES}
    cb, ca, mp = host_consts()
    pp = np.stack([host_ppack(W, l) for l in range(2)])
    gt = np.stack([W['norm1_g'][0], W['norm1_g'][1], W['norm2_g'][0], W['norm2_g'][1], W['final_norm_g']]).astype(np.float32)
    shared = {"w_in": f(W['w_in']), "w_branch": f(W['w_branch'].reshape(2, 1024, 1024)), "w_out": f(W['w_out']),
              "ffn_w_gate": f(W['ffn_w_gate']), "ffn_w_up": f(W['ffn_w_up']), "ffn_w_down": f(W['ffn_w_down']),
              "ppack": f(pp), "gtab": f(gt), "cbpack": cb, "capack": ca, "maskP": mp}
    xp = np.asarray(inp['x_prompt'], np.float32)
    xs = np.asarray(inp['x_sample'], np.float32)
    s_ah = np.asarray(inp['state_a_h'], np.float32)
    s_ac = np.asarray(inp['state_a_conv'], np.float32)
    s_wkv = np.asarray(inp['state_b_wkv'], np.float32)
    s_sh = np.asarray(inp['state_b_shift'], np.float32)
    s_cs = np.asarray(inp['state_c_s'], np.float32)
    s_fc = np.asarray(inp['state_ffn_conv'], np.float32)
    ck = np.asarray(inp['cache_d_k'], np.float32)
    cv = np.asarray(inp['cache_d_v'], np.float32)
    maps = []
    for b in range(8):
        sl = slice(4 * b, 4 * b + 4)
        m = dict(shared)
        m["x"] = f(np.concatenate([xp[b], xs[sl].reshape(16, 1024)], 0))
        spk = np.zeros((2, 128, NSP), np.float32)
        for l in range(2):
            spk[l, :, SP['a_h']:SP['a_h'] + 8] = s_ah[l, sl].reshape(4, 2, 128).transpose(2, 0, 1).reshape(128, 8)
            spk[l, :, SP['a_conv']:SP['a_conv'] + 24] = s_ac[l, sl].reshape(4, 3, 2, 128).transpose(3, 0, 2, 1).reshape(128, 24)
            spk[l, :, SP['b_shift']:SP['b_shift'] + 32] = s_sh[l, sl].reshape(4, 8, 128).transpose(2, 0, 1).reshape(128, 32)
            spk[l, :, SP['f_conv']:SP['f_conv'] + 192] = s_fc[l, sl].reshape(4, 2, 24, 128).transpose(3, 0, 2, 1).reshape(128, 192)
            spk[l, :64, SP['b_sh64']:SP['b_sh64'] + 56] = s_sh[l, sl][:, :896].reshape(4, 14, 64).transpose(2, 0, 1).reshape(64, 56)
            spk[l, :, SP['b_shg']:SP['b_shg'] + 4] = s_sh[l, sl][:, 896:].T
        m["spack"] = spk
        m["wkvT"] = f(s_wkv[:, sl].transpose(0, 4, 1, 2, 3).reshape(2, 64, 1024))
        m["cs0"] = f(s_cs[:, sl].transpose(0, 3, 1, 2, 4).reshape(2, 128, 1024))
        m["kcT"] = f(ck[:, sl].reshape(2, 4, 2048, 2, 2, 64).transpose(0, 1, 4, 5, 3, 2).reshape(2, 4, 128, 4096))
        m["vc"] = f(cv[:, sl].reshape(2, 4, 16, 128, 256).transpose(0, 1, 3, 2, 4).reshape(2, 4, 128, 4096))
        maps.append(m)
    return maps


def host_post(results):
    yp = np.zeros((8, 2048, 1024), np.float32)
    ys = np.zeros((32, 4, 1024), np.float32)
    p_a_h = np.zeros((2, 8, 256), np.float32)
    p_a_conv = np.zeros((2, 8, 3, 256), np.float32)
    p_wkv = np.zeros((2, 8, 4, 64, 64), np.float32)
    p_sh = np.zeros((2, 8, 1024), np.float32)
    p_cs = np.zeros((2, 8, 4, 128, 64), np.float32)
    p_dk = np.zeros((2, 8, 2048, 4, 64), np.float32)
    p_dv = np.zeros((2, 8, 2048, 4, 64), np.float32)
    p_fc = np.zeros((2, 8, 2, 3072), np.float32)
    s_a_h = np.zeros((2, 32, 256), np.float32)
    s_a_conv = np.zeros((2, 32, 3, 256), np.float32)
    s_wkv = np.zeros((2, 32, 4, 64, 64), np.float32)
    s_sh = np.zeros((2, 32, 1024), np.float32)
    s_cs = np.zeros((2, 32, 4, 128, 64), np.float32)
    s_dk = np.zeros((2, 32, 4, 4, 64), np.float32)
    s_dv = np.zeros((2, 32, 4, 4, 64), np.float32)
    s_fc = np.zeros((2, 32, 2, 3072), np.float32)
    for b in range(8):
        r = results[b]
        y = r["y"]
        yp[b] = y[:2048]
        ys[4 * b:4 * b + 4] = y[2048:].reshape(4, 4, 1024)
        dk, dv = r["o_dk"], r["o_dv"]
        p_dk[:, b] = dk[:, :2048].reshape(2, 2048, 4, 64)
        p_dv[:, b] = dv[:, :2048].reshape(2, 2048, 4, 64)
        s_dk[:, 4 * b:4 * b + 4] = dk[:, 2048:].reshape(2, 4, 4, 4, 64)
        s_dv[:, 4 * b:4 * b + 4] = dv[:, 2048:].reshape(2, 4, 4, 4, 64)
        sm = r["o_small"]
        ah = sm[:, :, OS['a_h']:OS['a_h'] + 10].reshape(2, 128, 5, 2).transpose(0, 2, 3, 1).reshape(2, 5, 256)
        ac = sm[:, :, OS['a_conv']:OS['a_conv'] + 30].reshape(2, 128, 5, 2, 3).transpose(0, 2, 4, 3, 1).reshape(2, 5, 3, 256)
        sh64 = sm[:, :64, OS['b_sh64']:OS['b_sh64'] + 70].reshape(2, 64, 5, 14).transpose(0, 2, 3, 1).reshape(2, 5, 896)
        shg = sm[:, :, OS['b_shg']:OS['b_shg'] + 5].transpose(0, 2, 1)
        sh = np.concatenate([sh64, shg], axis=2)
        fc = sm[:, :, OS['f_conv']:OS['f_conv'] + 240].reshape(2, 128, 5, 24, 2).transpose(0, 2, 4, 3, 1).reshape(2, 5, 2, 3072)
        p_a_h[:, b], s_a_h[:, 4 * b:4 * b + 4] = ah[:, 0], ah[:, 1:]
        p_a_conv[:, b], s_a_conv[:, 4 * b:4 * b + 4] = ac[:, 0], ac[:, 1:]
        p_sh[:, b], s_sh[:, 4 * b:4 * b + 4] = sh[:, 0], sh[:, 1:]
        p_fc[:, b], s_fc[:, 4 * b:4 * b + 4] = fc[:, 0], fc[:, 1:]
        wk = r["o_wkv"].reshape(2, 5, 64, 4, 64).transpose(0, 1, 3, 4, 2)
        p_wkv[:, b], s_wkv[:, 4 * b:4 * b + 4] = wk[:, 0], wk[:, 1:]
        cs = r["o_cs"].reshape(2, 5, 128, 4, 64).transpose(0, 1, 3, 2, 4)
        p_cs[:, b], s_cs[:, 4 * b:4 * b + 4] = cs[:, 0], cs[:, 1:]
    return (yp, ys, p_a_h, p_a_conv, p_wkv, p_sh, p_cs, p_dk, p_dv, p_fc,
            s_a_h, s_a_conv, s_wkv, s_sh, s_cs, s_dk, s_dv, s_fc)


ENABLE = ('A', 'B', 'C', 'D')


def kernel(**inputs):
    maps = host_prep(inputs)
    key = tuple(ENABLE)
    if key not in _CACHE:
        _CACHE[key] = build_program(ENABLE)
    nc = _CACHE[key]
    res = run_bass_kernel_spmd(nc, maps, core_ids=list(range(8)))
    return host_post(res.results)
```

```python
import numpy as np
import ml_dtypes
import concourse.bass as bass
import concourse.mybir as mybir
from concourse.bass_utils import run_bass_kernel_spmd

F32 = mybir.dt.float32
BF16 = mybir.dt.bfloat16
ALU = mybir.AluOpType
AF = mybir.ActivationFunctionType
AX = mybir.AxisListType

NT = 2064
NTILES = [(0, 512), (512, 512), (1024, 512), (1536, 512), (2048, 16)]
TT128 = [(t * 128, 128) for t in range(16)] + [(2048, 16)]
DEPTH = 2
EM05 = float(np.exp(-0.5))


class Ctx:
    NDS = 8

    def __init__(self, nc):
        self.nc = nc
        self.engs = {'pe': nc.tensor, 'act': nc.scalar, 'dve': nc.vector, 'pool': nc.gpsimd, 'sp': nc.sync}
        self.csem = {e: nc.alloc_semaphore(name="c_" + e) for e in ('pe', 'act', 'dve', 'pool')}
        self.ccnt = {e: 0 for e in self.csem}
        self.dsem = {q: [nc.alloc_semaphore(name="d_%s%d" % (q, i)) for i in range(self.NDS)]
                     for q in ('sp', 'pool', 'act')}
        self.dcnt = {q: 0 for q in self.dsem}
        self.waited = {e: {} for e in self.engs}
        self.last_w = {}
        self.readers = {}
        self.n_inst = 0
        self.n_wait = 0

    def _wait(self, engine, ev):
        sem, val, src = ev
        w = self.waited[engine]
        k = id(sem)
        if w.get(k, 0) >= val:
            return
        if src == 'pe' and engine == 'pe':
            return
        self.engs[engine].wait_ge(sem, val)
        w[k] = val
        self.n_wait += 1

    def _deps(self, engine, reads, writes):
        for k in reads:
            ev = self.last_w.get(k)
            if ev is not None:
                self._wait(engine, ev)
        for k in writes:
            ev = self.last_w.get(k)
            if ev is not None:
                self._wait(engine, ev)
            for ev in self.readers.get(k, ()):
                self._wait(engine, ev)

    def _commit(self, ev, reads, writes):
        for k in writes:
            self.last_w[k] = ev
            self.readers[k] = []
        for k in reads:
            if k not in writes:
                self.readers.setdefault(k, []).append(ev)

    def op(self, engine, fn, reads=(), writes=()):
        pr = [k for k in reads if k == 'psb' or (isinstance(k, tuple) and k[0] == 'ps')]
        if pr:
            reads = [k for k in reads if k not in pr]
            writes = list(writes) + [k for k in pr if k not in writes]
        self._deps(engine, reads, writes)
        inst = fn(self.engs[engine])
        self.ccnt[engine] += 1
        inst.then_inc(self.csem[engine], 1)
        ev = (self.csem[engine], self.ccnt[engine], engine)
        self._commit(ev, reads, writes)
        self.n_inst += 1
        return inst

    def dma(self, q, out, in_, reads=(), writes=(), **kw):
        n = self.dcnt[q]
        sem = self.dsem[q][n % self.NDS]
        target = 16 * (n // self.NDS + 1)
        if n >= self.NDS:
            self._wait(q, (sem, target - 16, 'dma'))
        self._deps(q, reads, writes)
        inst = self.engs[q].dma_start(out=out, in_=in_, **kw)
        inst.then_inc(sem, 16)
        self.dcnt[q] = n + 1
        ev = (sem, target, 'dma')
        self._commit(ev, reads, writes)
        self.n_inst += 1
        return inst

    def _all_events(self):
        evs = []
        for q in self.dsem:
            n = self.dcnt[q]
            for i in range(min(n, self.NDS)):
                cnt = (n - 1 - i) // self.NDS + 1
                evs.append((self.dsem[q][i], 16 * cnt, 'dma'))
        for e in self.csem:
            if self.ccnt[e]:
                evs.append((self.csem[e], self.ccnt[e], 'x'))
        return evs

    def barrier(self):
        evs = self._all_events()
        for e in self.engs:
            for ev in evs:
                self._wait(e, ev)
        self.last_w = {}
        self.readers = {}

    def finish(self):
        for ev in self._all_events():
            self._wait('sp', ev)


class Arena:
    def __init__(self, nc, name, nbytes):
        self.words = nbytes // 4
        self.t = nc.alloc_sbuf_tensor(name, [128, self.words], F32).ap()
        self.off = 0

    def reset(self):
        self.off = 0

    def alloc(self, shape, dtype=F32):
        n = int(np.prod(shape))
        words = n if dtype == F32 else (n + 1) // 2
        words = (words + 7) // 8 * 8
        assert self.off + words <= self.words, ("arena overflow", self.off, words, self.words)
        ap = self.t[:, self.off:self.off + words]
        self.off += words
        if dtype != F32:
            ap = ap.bitcast(dtype)
        ap = ap[:, 0:n]
        if len(shape) == 2:
            return ap.rearrange("p (a b) -> p a b", a=shape[0])
        if len(shape) == 3:
            return ap.rearrange("p (a b c) -> p a b c", a=shape[0], b=shape[1])
        return ap


class CtxTag:
    SHARED = ('ST', 'STb', 'ppf', 'pder', 'spf', 'pmat', 'cbb', 'cbf', 'caf', 'osm', 'psb')

    def __init__(self, c, tag):
        self.c = c
        self.tag = tag

    def _m(self, keys):
        out = []
        for k in keys:
            if k in self.SHARED or (isinstance(k, tuple) and k[0] in ('ps', 'b1', 'cols')):
                out.append(k)
            else:
                out.append((self.tag, k))
        return out

    def op(self, engine, fn, reads=(), writes=()):
        return self.c.op(engine, fn, self._m(reads), self._m(writes))

    def dma(self, q, out, in_, reads=(), writes=(), **kw):
        return self.c.dma(q, out, in_, reads=self._m(reads), writes=self._m(writes), **kw)


def _mk_layout(entries):
    off, d = 0, {}
    for name, n in entries:
        d[name] = off
        off += n
    return d, off


PK, NPK = _mk_layout([
    ('a_cw', 8), ('a_cb', 2), ('a_gxb', 2), ('a_gab', 2), ('a_lam', 2),
    ('b_mu', 8), ('b_w0', 2), ('b_a0', 2), ('b_kk', 2), ('b_ka', 2), ('b_rk', 2), ('b_lnw', 2), ('b_lnb', 2),
    ('c_lb0', 4), ('c_lb1', 4), ('c_ng', 1), ('f_cw', 72), ('f_cb', 24), ('pad', 1),
    ('b64_mu', 14), ('b_mug', 1), ('b64_w0', 4), ('b64_a0', 4), ('b64_kk', 4), ('b64_ka', 4), ('b64_rk', 4), ('b64_lnw', 4),
    ('b64_lnb', 4), ('pad2', 5),
    ('gxw', 256), ('gaw', 256), ('w2', 256), ('a2', 256), ('g2', 256)])
PMAT0 = PK['gxw']
SP, NSP = _mk_layout([('a_h', 8), ('a_conv', 24), ('b_shift', 32), ('f_conv', 192), ('b_sh64', 56), ('b_shg', 4)])
OS, NOS = _mk_layout([('a_h', 10), ('a_conv', 30), ('b_shift', 40), ('f_conv', 240), ('b_sh64', 70), ('b_shg', 5)])
CB, NCB = _mk_layout([('ident', 128), ('tri_incl', 64), ('tri_su', 64), ('tri_sl', 64), ('blk64', 128),
                      ('maskS', 68), ('maskN', 16), ('ones', 64)])
CA, NCA = _mk_layout([('rope', 17 * 64), ('reset', 512)])


def _fm(v, nchunk):
    return np.ascontiguousarray(np.asarray(v, np.float32).reshape(nchunk, 128).T)


def _mult(dist):
    dist = np.asarray(dist)
    m = ((dist >= 0) & (dist <= 128)).astype(np.float32)
    m += ((dist >= 0) & (dist % 4 == 0) & (dist <= 512))
    m += ((dist >= 0) & (dist % 16 == 0) & (dist <= 2048))
    return m.astype(np.float32)


def host_consts():
    cb = np.zeros((128, NCB), np.float32)
    cb[:, CB['ident']:CB['ident'] + 128] = np.eye(128, dtype=np.float32)
    i = np.arange(64)
    cb[:64, CB['tri_incl']:CB['tri_incl'] + 64] = (i[:, None] <= i[None, :])
    cb[:64, CB['tri_su']:CB['tri_su'] + 64] = (i[:, None] < i[None, :])
    cb[:64, CB['tri_sl']:CB['tri_sl'] + 64] = (i[:, None] > i[None, :])
    blk = np.zeros((128, 128), np.float32)
    blk[:64, :64] = 1.0
    blk[64:, 64:] = 1.0
    cb[:, CB['blk64']:CB['blk64'] + 128] = blk
    cb[:, CB['ones']:CB['ones'] + 64] = 1.0
    p = np.arange(128)
    ms = np.zeros((128, 17, 4), np.float32)
    for b in range(16):
        for q in range(4):
            ms[:, b, q] = _mult(2048 + q - (b * 128 + p))
    cb[:, CB['maskS']:CB['maskS'] + 68] = ms.reshape(128, 68)
    mn = np.zeros((16, 4, 4), np.float32)
    for s2 in range(4):
        for i2 in range(4):
            for q in range(4):
                mn[s2 * 4 + i2, s2, q] = _mult(q - i2)
    cb[:16, CB['maskN']:CB['maskN'] + 16] = mn.reshape(16, 16)
    ca = np.zeros((128, NCA), np.float32)
    half = 32
    inv = (10000.0 ** (-np.arange(half, dtype=np.float32) / half)).astype(np.float32)
    rope = np.zeros((128, 17, 64), np.float32)
    for t in range(17):
        if t < 16:
            pos = (t * 128 + np.arange(128)).astype(np.float32)
        else:
            pos = np.zeros(128, np.float32)
            pos[:16] = (8192 + np.tile(np.arange(4), 4)).astype(np.float32)
        ang = (pos[:, None] * inv[None, :]).astype(np.float32)
        rope[:, t, :32] = np.cos(ang)
        rope[:, t, 32:] = np.sin(ang)
    ca[:, CA['rope']:CA['rope'] + 17 * 64] = rope.reshape(128, -1)
    rs = np.ones(512, np.float32)
    rs[::64] = 0.0
    ca[:, CA['reset']:CA['reset'] + 512] = rs[None, :]
    mp = np.zeros((128, 17, 128), np.float32)
    for d in range(17):
        mp[:, d, :] = _mult((d * 128 + p[None, :]) - p[:, None])
    return cb, ca, mp.reshape(128, 17 * 128)


def host_ppack(W, l):
    pk = np.zeros((128, NPK), np.float32)

    def put(name, arr):
        arr = np.asarray(arr, np.float32)
        pk[:arr.shape[0], PK[name]:PK[name] + arr.shape[1]] = arr
    cw = np.asarray(W['a_conv_w'][l], np.float32)
    put('a_cw', np.stack([_fm(cw[j], 2) for j in range(4)], axis=2).reshape(128, 8))
    put('a_cb', _fm(W['a_conv_b'][l], 2))
    put('a_gxb', _fm(W['a_gx_b'][l], 2))
    put('a_gab', _fm(W['a_ga_b'][l], 2))
    put('a_lam', _fm(W['a_lambda'][l], 2))
    put('b_mu', _fm(W['b_mu'][l], 8))
    put('b_w0', _fm(W['b_w0'][l], 2))
    put('b_a0', _fm(W['b_a0'][l], 2))
    put('b_kk', _fm(W['b_k_k'][l], 2))
    put('b_ka', _fm(W['b_k_a'][l], 2))
    put('b_rk', _fm(np.asarray(W['b_r_k'][l]).reshape(256), 2))
    put('b_lnw', _fm(W['b_ln_w'][l], 2))
    put('b_lnb', _fm(W['b_ln_b'][l], 2))
    put('c_lb0', _fm(W['c_lb'][0], 4))
    put('c_lb1', _fm(W['c_lb'][1], 4))
    put('c_ng', np.tile(np.asarray(W['c_norm_g'][l], np.float32), 2)[:, None])
    fw = np.asarray(W['ffn_conv_w'][l], np.float32)
    put('f_cw', np.stack([_fm(fw[j], 24) for j in range(3)], axis=2).reshape(128, 72))
    put('f_cb', _fm(W['ffn_conv_b'][l], 24))
    for nm, key in (('gxw', 'a_gx_w'), ('gaw', 'a_ga_w')):
        g = np.asarray(W[key][l], np.float32)
        m = np.zeros((128, 256), np.float32)
        for c in range(2):
            for hh in range(2):
                m[hh * 64:(hh + 1) * 64, c * 128 + hh * 64:c * 128 + (hh + 1) * 64] = g[c * 2 + hh]
        put(nm, m)
    put('w2', np.asarray(W['b_w2'][l], np.float32))
    put('a2', np.asarray(W['b_a2'][l], np.float32))
    mu = np.asarray(W['b_mu'][l], np.float32)
    put('b64_mu', mu[:896].reshape(14, 64).T)
    put('b_mug', mu[896:].reshape(128, 1))
    for nm, key in (('b64_w0', 'b_w0'), ('b64_a0', 'b_a0'), ('b64_kk', 'b_k_k'), ('b64_ka', 'b_k_a'), ('b64_rk', 'b_r_k'),
                    ('b64_lnw', 'b_ln_w'), ('b64_lnb', 'b_ln_b')):
        put(nm, np.asarray(W[key][l], np.float32).reshape(4, 64).T)
    put('g2', np.asarray(W['b_g2'][l], np.float32))
    return pk


STOP = None
SLOW_EVERY = 4


class _Stop(Exception):
    pass


def build_program(enable=('A', 'B', 'C', 'D')):
    nc = bass.Bass("TRN2", target_bir_lowering=False)
    try:
        _build(nc, enable)
    except _Stop:
        pass
    return nc


def _build(nc, enable):

    def din(name, shape):
        return nc.dram_tensor(name, list(shape), F32, kind="ExternalInput").ap()

    def dout(name, shape):
        return nc.dram_tensor(name, list(shape), F32, kind="ExternalOutput").ap()
    x_in = din("x", [NT, 1024])
    w_in = din("w_in", [2, 1024, 7936])
    w_br = din("w_branch", [2, 1024, 1024])
    w_out = din("w_out", [2, 1024, 1024])
    w_g = din("ffn_w_gate", [2, 1024, 3072])
    w_u = din("ffn_w_up", [2, 1024, 3072])
    w_d = din("ffn_w_down", [2, 3072, 1024])
    ppack = din("ppack", [2, 128, NPK])
    gtab = din("gtab", [5, 1024])
    spack = din("spack", [2, 128, NSP])
    wkv_in = din("wkvT", [2, 64, 4 * 4 * 64])
    cs_in = din("cs0", [2, 128, 4 * 4 * 64])
    kcT = din("kcT", [2, 4, 128, 2 * 2048])
    vcd = din("vc", [2, 4, 128, 16 * 256])
    cb_in = din("cbpack", [128, NCB])
    ca_in = din("capack", [128, NCA])
    mp_in = din("maskP", [128, 17 * 128])

    y_out = dout("y", [NT, 1024])
    o_dk = dout("o_dk", [2, NT, 256])
    o_dv = dout("o_dv", [2, NT, 256])
    o_small = dout("o_small", [2, 128, NOS])
    o_wkv = dout("o_wkv", [2, 5, 64, 256])
    o_cs = dout("o_cs", [2, 5, 128, 256])

    xres = nc.dram_tensor("xres", [NT, 1024], F32).ap()
    colsT = nc.dram_tensor("colsT", [7936, NT], F32).ap()
    hmT = nc.dram_tensor("hmT", [3072, NT], BF16).ap()
    gT = nc.dram_tensor("gT", [4096, NT], BF16).ap()

    c = Ctx(nc)
    sb = nc.alloc_sbuf_tensor

    def chk(name):
        if STOP == name:
            c.finish()
            print('STOP at', name, 'instructions', c.n_inst)
            raise _Stop()

    b1 = sb("b1", [128, 8, NT], BF16).ap()
    QT = sb("QT", [128, 2, NT], BF16).ap()
    KT = sb("KT", [128, 2, NT], BF16).ap()
    Vall = sb("Vall", [128, 17, 256], BF16).ap()
    cbf = sb("cbf", [128, NCB], F32).ap()
    cbb = sb("cbb", [128, NCB], BF16).ap()
    caf = sb("caf", [128, NCA], F32).ap()
    mpb = sb("mpb", [128, 17, 128], BF16).ap()
    arena2 = Arena(nc, "arena2", 56 * 1024)
    wslot = [arena2.alloc([6144], BF16) for i in range(2)]
    stg = [arena2.alloc([512]) for i in range(4)]
    xbuf = [arena2.alloc([1024]) for i in range(3)]
    sqj = arena2.alloc([1024])
    ubuf = [arena2.alloc([1024], BF16) for i in range(2)]
    ssb = [sb("ss%d" % i, [128, 4], F32).ap() for i in range(4)]
    gbc = arena2.alloc([1024])
    ppf = sb("ppf", [128, NPK], F32).ap()
    stgb = [sb("stgb%d" % i, [128, 512], BF16).ap() for i in range(2)]
    pmat = sb("pmat", [128, 1280], BF16).ap()
    spf = sb("spf", [128, NSP], F32).ap()
    pder = sb("pder", [128, 16], F32).ap()
    osm = sb("osm", [128, NOS], F32).ap()
    kcbuf = [sb("kcb%d" % i, [128, 2, 512], BF16).ap() for i in range(2)]
    vcbuf = [sb("vcb%d" % i, [128, 4, 256], BF16).ap() for i in range(2)]
    arena = Arena(nc, "arena", 56 * 1024)
    print("sbuf remaining after alloc", nc.sbuf_bytes_remaining)

    psf = [nc.alloc_psum_tensor("ps%d" % i, [128, 512], F32).ap() for i in range(7)]
    psb = nc.alloc_psum_tensor("psb", [128, 1024], BF16).ap()
    pstate = {'i': 0, 'w': 0, 's': 0, 'a': 0}

    def nextps():
        i = pstate['i'] % 5
        pstate['i'] += 1
        return psf[i], ('ps', i)

    def mkps(idx):
        st = {'i': 0}

        def f():
            i = idx[st['i'] % len(idx)]
            st['i'] += 1
            return psf[i], ('ps', i)
        return f

    def nextacc():
        i = 5 + pstate['a'] % 2
        pstate['a'] += 1
        return psf[i], ('ps', i)

    identb = cbb[:, CB['ident']:CB['ident'] + 128]
    identf = cbf[:, CB['ident']:CB['ident'] + 128]
    blk64b = cbb[:, CB['blk64']:CB['blk64'] + 128]
    onesb = cbb[:, CB['ones']:CB['ones'] + 64]
    tri_incl = cbf[0:64, CB['tri_incl']:CB['tri_incl'] + 64]
    tri_su = cbf[0:64, CB['tri_su']:CB['tri_su'] + 64]
    tri_sl = cbf[0:64, CB['tri_sl']:CB['tri_sl'] + 64]
    maskSb = cbb[:, CB['maskS']:CB['maskS'] + 68].rearrange("p (b q) -> p b q", b=17)
    maskNb = cbb[0:16, CB['maskN']:CB['maskN'] + 16].rearrange("p (s q) -> p s q", s=4)
    ropet = caf[:, CA['rope']:CA['rope'] + 17 * 64].rearrange("p (t d) -> p t d", t=17)
    resetm = caf[:, CA['reset']:CA['reset'] + 512]

    def pcol(name, i=0):
        return ppf[:, PK[name] + i:PK[name] + i + 1]

    def k_b1(cs, n0, nn):
        return [('b1', cc, t) for cc in cs for t in range(n0 // 128, (n0 + nn + 127) // 128)]

    c.dma('sp', cbf, cb_in, writes=['cbf'])
    c.dma('pool', cbb, cb_in, writes=['cbb'])
    c.dma('sp', caf, ca_in, writes=['caf'])
    c.dma('pool', mpb.rearrange("p a b -> p (a b)"), mp_in, writes=['mpb'])
    for t, (r0, R) in enumerate(TT128):
        xt = xbuf[t % 3]
        c.dma('sp', xt[:R], x_in[r0:r0 + R, :], writes=[('xt', t % 3)])
        c.dma('sp', xres[r0:r0 + R, :], xt[:R], reads=[('xt', t % 3)], writes=[('xres', t)])

    chk('setup')

    def norm_tile(xt, xk, t, r0, R, final=False):
        ss = ssb[t % 4]
        sk = ('ss', t % 4)
        c.op('pool', lambda e: e.tensor_tensor(out=sqj[:R], in0=xt[:R], in1=xt[:R], op=ALU.mult), [xk], ['sqj'])
        c.op('dve', lambda e: e.reduce_sum(out=ss[:R, 0:1], in_=sqj[:R], axis=AX.X), ['sqj'], [sk])
        c.op('act', lambda e: e.activation(out=ss[:R, 1:2], in_=ss[:R, 0:1], func=AF.Sqrt, bias=1e-6, scale=1.0 / 1024),
             [sk], [sk])
        c.op('dve', lambda e: e.reciprocal(out=ss[:R, 2:3], in_=ss[:R, 1:2]), [sk], [sk])
        if final:
            c.op('dve', lambda e: e.scalar_tensor_tensor(out=sqj[:R], in0=xt[:R], scalar=ss[:R, 2:3], in1=gbc[:R],
                                                         op0=ALU.mult, op1=ALU.mult), [xk, sk, 'gbc'], ['sqj'])
            c.dma('sp', y_out[r0:r0 + R, :], sqj[:R], reads=['sqj'])
            return
        ub = ubuf[t % 2]
        uk = ('ub', t % 2)
        c.op('dve', lambda e: e.scalar_tensor_tensor(out=ub[:R], in0=xt[:R], scalar=ss[:R, 2:3], in1=gbc[:R],
                                                     op0=ALU.mult, op1=ALU.mult), [xk, sk, 'gbc'], [uk])
        for cc in range(8):
            c.op('pe', lambda e: e.transpose(psb[:, cc * 128:cc * 128 + R], ub[:R, cc * 128:(cc + 1) * 128], identb[:R, :R]),
                 [uk, 'cbb'], ['psb'])
        c.op('act', lambda e: e.activation(out=b1[:, :, r0:r0 + R],
                                           in_=psb.rearrange("p (c t) -> p c t", c=8)[:, :, :R], func=AF.Copy),
             ['psb'], [('b1', cc, t) for cc in range(8)])

    def load_gains(gidx):
        c.dma('sp', gbc, gtab[gidx].partition_broadcast(128), writes=['gbc'])

    def norm_phase(gidx, final=False):
        load_gains(gidx)
        for t, (r0, R) in enumerate(TT128):
            xt = xbuf[t % 3]
            xk = ('xt', t % 3)
            c.dma('sp', xt[:R], xres[r0:r0 + R, :], reads=[('xres', t)], writes=[xk])
            norm_tile(xt, xk, t, r0, R, final)

    def getslot():
        i = pstate['w'] % 2
        pstate['w'] += 1
        return wslot[i], ('ws', i)

    def getstg():
        i = pstate['s'] % 4
        pstate['s'] += 1
        return stg[i], ('stg', i)

    def ldcols(dst, row0, nch, n0, T, q='sp'):
        src = colsT[row0:row0 + nch * 128].rearrange("(c p) t -> p c t", p=128)[:, :, n0:n0 + T]
        return src, [('cols', row0 // 128 + cc, n0 // 512) for cc in range(nch)]

    for l in range(DEPTH):
        c.dma('sp', ppf, ppack[l], writes=['ppf'])
        c.dma('pool', pmat, ppack[l][:, PMAT0:PMAT0 + 1280], writes=['pmat'])
        c.dma('sp', spf, spack[l], writes=['spf'])
        c.op('act', lambda e: e.activation(out=pder[:, 0:2], in_=ppf[:, PK['a_lam']:PK['a_lam'] + 2], func=AF.Exp, scale=-1.0),
             ['ppf'], ['pder'])
        c.op('act', lambda e: e.activation(out=pder[:, 0:2], in_=pder[:, 0:2], func=AF.Ln, bias=1.0, scale=1.0), ['pder'], ['pder'])
        c.op('dve', lambda e: e.tensor_scalar(out=pder[:, 2:4], in0=pder[:, 0:2], scalar1=-16.0, scalar2=None, op0=ALU.mult),
             ['pder'], ['pder'])
        c.op('dve', lambda e: e.tensor_scalar(out=pder[:, 0:2], in0=pder[:, 0:2], scalar1=-8.0, scalar2=None, op0=ALU.mult),
             ['pder'], ['pder'])
        if l == 0:
            c.op('dve', lambda e: e.memset(pder[:, 4:8], 0.0), [], ['pder'])
        else:
            c.op('dve', lambda e: e.tensor_tensor(out=pder[:, 4:8], in0=ppf[:, PK['c_lb1']:PK['c_lb1'] + 4],
                                                  in1=ppf[:, PK['c_lb0']:PK['c_lb0'] + 4], op=ALU.subtract), ['ppf'], ['pder'])
            c.op('act', lambda e: e.activation(out=pder[:, 4:8], in_=pder[:, 4:8], func=AF.Sigmoid), ['pder'], ['pder'])
        c.op('dve', lambda e: e.tensor_scalar(out=pder[:, 8:12], in0=pder[:, 4:8], scalar1=-1.0, scalar2=1.0,
                                              op0=ALU.mult, op1=ALU.add), ['pder'], ['pder'])
        c.op('dve', lambda e: e.tensor_scalar(out=pder[:, 12:16], in0=ppf[:, PK['b64_ka']:PK['b64_ka'] + 4], scalar1=-1.0,
                                              scalar2=1.0, op0=ALU.mult, op1=ALU.add), ['ppf'], ['pder'])
        c.op('dve', lambda e: e.memset(osm, 0.0), [], ['osm'])

        if l == 0:
            norm_phase(0)

        chk('norm1')
        wv = w_in[l].rearrange("(k p) c -> p k c", p=128)
        blocks = [(c0, 768) for c0 in (0, 768, 1536, 2304)] + [(3840 + i * 768, 768) for i in range(5)] + [(7680, 256)]
        ev = 0
        for (c0, ncol) in blocks:
            ws, wk = getslot()
            ws3 = ws.rearrange("p (k c) -> p k c", k=8)
            c.dma('pool', ws3[:, :, :ncol], wv[:, :, c0:c0 + ncol], writes=[wk])
            for j, (n0, nn) in enumerate(NTILES):
                for cc in range(ncol // 128):
                    ps, pk = nextps()
                    for k in range(8):
                        c.op('pe', lambda e: e.matmul(ps[:, :nn], lhsT=ws3[:, k, cc * 128:(cc + 1) * 128], rhs=b1[:, k, n0:n0 + nn],
                                                      start=(k == 0), stop=(k == 7)), [wk] + k_b1([k], n0, nn), [pk])
                    st, sk = getstg()
                    if c0 >= 3840:
                        sgb, sgk = stgb[ev % 2], ('stgb', ev % 2)
                        c.op('act', lambda e: e.activation(out=sgb[:, :nn], in_=ps[:, :nn], func=AF.Sigmoid), [pk], [sgk])
                        ev += 1
                        rb = c0 // 128 + cc
                        c.dma('sp', gT[(rb - 30) * 128:(rb - 29) * 128, n0:n0 + nn], sgb[:, :nn], reads=[sgk], writes=[('cols', rb, j)])
                        continue
                    elif ev % 2 == 0:
                        c.op('act', lambda e: e.activation(out=st[:, :nn], in_=ps[:, :nn], func=AF.Copy), [pk], [sk])
                    else:
                        c.op('dve', lambda e: e.tensor_copy(out=st[:, :nn], in_=ps[:, :nn]), [pk], [sk])
                    ev += 1
                    rb = c0 // 128 + cc
                    c.dma('sp', colsT[rb * 128:(rb + 1) * 128, n0:n0 + nn], st[:, :nn], reads=[sk], writes=[('cols', rb, j)])

        chk('inproj')
        c.barrier()
        arena.reset()
        NB3 = 3
        dqk = [arena.alloc([512]) for _ in range(NB3)]
        dvv = [arena.alloc([256]) for _ in range(NB3)]
        rot = [arena.alloc([512]) for _ in range(NB3)]
        rtmp = [arena.alloc([512]) for _ in range(NB3)]
        rbb = [arena.alloc([512], BF16) for _ in range(NB3)]
        ws, wk = getslot()
        ws3 = ws.rearrange("p (k c) -> p k c", k=8)
        c.dma('pool', ws3, wv[:, :, 3072:3840], writes=[wk])

        def qkvA(t, r0, R):
            i2 = t % NB3
            psA, pkA = nextps()
            psB, pkB = nextps()
            for k in range(8):
                c.op('pe', lambda e: e.matmul(psA[:R, :512], lhsT=b1[:, k, r0:r0 + R], rhs=ws3[:, k, 0:512], start=(k == 0), stop=(k == 7)),
                     [wk, ('b1', k, t)], [pkA])
            for k in range(8):
                c.op('pe', lambda e: e.matmul(psB[:R, :256], lhsT=b1[:, k, r0:r0 + R], rhs=ws3[:, k, 512:768], start=(k == 0), stop=(k == 7)),
                     [wk, ('b1', k, t)], [pkB])
            qk, vv = dqk[i2], dvv[i2]
            kq, kv = ('dqk', i2), ('dvv', i2)
            c.op('act', lambda e: e.activation(out=qk[:R], in_=psA[:R, :512], func=AF.Copy), [pkA], [kq])
            c.op('dve', lambda e: e.tensor_copy(out=vv[:R], in_=psB[:R, :256]), [pkB], [kv])
            c.dma('sp', o_dv[l, r0:r0 + R, :], vv[:R], reads=[kv])
            c.op('pool', lambda e: e.tensor_copy(out=Vall[:R, t, :], in_=vv[:R]), [kv], [('Vall', t)])

        def qkvB(t, r0, R):
            i2 = t % NB3
            qk, ro, rt_, rb_ = dqk[i2], rot[i2], rtmp[i2], rbb[i2]
            kq, kr, kt_, kb_ = ('dqk', i2), ('rot', i2), ('rtmp', i2), ('rbb', i2)
            q4 = qk[:R].rearrange("p (h two d) -> p h two d", h=8, two=2)
            r4 = ro[:R].rearrange("p (h two d) -> p h two d", h=8, two=2)
            t4 = rt_[:R].rearrange("p (h two d) -> p h two d", h=8, two=2)
            cosb = ropet[:R, t, 0:32].unsqueeze(1).to_broadcast([R, 8, 32])
            sinb = ropet[:R, t, 32:64].unsqueeze(1).to_broadcast([R, 8, 32])
            c.op('dve', lambda e: e.tensor_tensor(out=r4[:, :, 0, :], in0=q4[:, :, 0, :], in1=cosb, op=ALU.mult), [kq, 'caf'], [kr])
            c.op('dve', lambda e: e.tensor_tensor(out=t4[:, :, 0, :], in0=q4[:, :, 1, :], in1=sinb, op=ALU.mult), [kq, 'caf'], [kt_])
            c.op('dve', lambda e: e.tensor_tensor(out=r4[:, :, 0, :], in0=r4[:, :, 0, :], in1=t4[:, :, 0, :], op=ALU.subtract), [kr, kt_], [kr])
            c.op('dve', lambda e: e.tensor_tensor(out=r4[:, :, 1, :], in0=q4[:, :, 1, :], in1=cosb, op=ALU.mult), [kq, 'caf'], [kr])
            c.op('dve', lambda e: e.tensor_tensor(out=t4[:, :, 1, :], in0=q4[:, :, 0, :], in1=sinb, op=ALU.mult), [kq, 'caf'], [kt_])
            c.op('dve', lambda e: e.tensor_tensor(out=r4[:, :, 1, :], in0=r4[:, :, 1, :], in1=t4[:, :, 1, :], op=ALU.add), [kr, kt_], [kr])
            c.dma('sp', o_dk[l, r0:r0 + R, :], ro[:R, 256:512], reads=[kr])
            c.op('act', lambda e: e.activation(out=rb_[:R], in_=ro[:R], func=AF.Copy), [kr], [kb_])

        def qkvC(t, r0, R):
            i2 = t % NB3
            rb_, kb_ = rbb[i2], ('rbb', i2)
            for i in range(4):
                c.op('pe', lambda e: e.transpose(psb[:, i * 128:i * 128 + R], rb_[:R, i * 128:(i + 1) * 128], identb[:R, :R]),
                     [kb_, 'cbb'], ['psb'])
            pv = psb[:, 0:512].rearrange("p (i t) -> p i t", i=4)
            c.op('act', lambda e: e.activation(out=QT[:, :, r0:r0 + R], in_=pv[:, 0:2, :R], func=AF.Copy), ['psb'], [('QT', t)])
            c.op('dve', lambda e: e.tensor_copy(out=KT[:, :, r0:r0 + R], in_=pv[:, 2:4, :R]), ['psb'], [('KT', t)])
        ntt = len(TT128)
        for it_ in range(ntt + 2):
            if it_ < ntt:
                qkvA(it_, *TT128[it_])
            if 0 <= it_ - 1 < ntt:
                qkvB(it_ - 1, *TT128[it_ - 1])
            if 0 <= it_ - 2 < ntt:
                qkvC(it_ - 2, *TT128[it_ - 2])

        chk('qkv')
        for bi, nm in enumerate('ABCD'):
            if nm not in enable:
                c.op('pool', lambda e: e.memset(b1[:, 2 * bi:2 * bi + 2, :], 0.0), [],
                     [('b1', cc, t) for cc in (2 * bi, 2 * bi + 1) for t in range(17)])

        def gen_D(arena):
            nextps = cad_ps
            nextacc = mkps([6])
            dcnt = {'i': 0}
            esb = [arena.alloc([512], BF16) for _ in range(3)]
            recb = [arena.alloc([256]) for _ in range(2)]
            for qb in range(16):
                pso, pko = nextacc()
                for kb in range(qb + 1):
                    pss2 = [nextps(), nextps()]
                    for h in range(4):
                        hh, hp = h % 2, h // 2
                        pss, pks = pss2[hh]
                        c.op('pe', lambda e: e.matmul(pss[:, hp * 128:(hp + 1) * 128], lhsT=KT[hh * 64:(hh + 1) * 64, hp, kb * 128:(kb + 1) * 128],
                                                      rhs=QT[hh * 64:(hh + 1) * 64, hp, qb * 128:(qb + 1) * 128], start=True, stop=True),
                             [('KT', kb), ('QT', qb)], [pks])
                    dcnt['i'] += 1
                    ei = dcnt['i'] % 3
                    es, ek = esb[ei], ('esb', ei)
                    e3 = es.rearrange("p (h q) -> p h q", h=4)
                    for hh in range(2):
                        pss, pks = pss2[hh]
                        c.op('act', lambda e: e.activation(out=e3[:, hh::2, :], in_=pss[:, 0:256].rearrange("p (a q) -> p a q", a=2),
                                                           func=AF.Exp, scale=0.125), [pks], [ek])
                    c.op('dve', lambda e: e.tensor_tensor(out=e3, in0=e3, in1=mpb[:, qb - kb, :].unsqueeze(1).to_broadcast([128, 4, 128]),
                                                          op=ALU.mult), [ek, 'mpb'], [ek])
                    for h in range(4):
                        hh, hp = h % 2, h // 2
                        c.op('pe', lambda e: e.matmul(pso[hh * 64:(hh + 1) * 64, hp * 128:(hp + 1) * 128], lhsT=Vall[:, kb, h * 64:(h + 1) * 64],
                                                      rhs=es[:, h * 128:(h + 1) * 128], start=(kb == 0 and hp == 0), stop=(kb == qb)),
                             [('Vall', kb), ek], [pko])
                        c.op('pe', lambda e: e.matmul(pso[hh * 64:(hh + 1) * 64, 256 + hp * 128:256 + (hp + 1) * 128], lhsT=onesb,
                                                      rhs=es[:, h * 128:(h + 1) * 128], start=False, stop=(kb == qb)),
                             ['cbb', ek], [pko])
                    yield
                rc, rk = recb[qb % 2], ('recb', qb % 2)
                c.op('dve', lambda e: e.reciprocal(out=rc, in_=pso[:, 256:512]), [pko], [rk])
                c.op('dve', lambda e: e.tensor_tensor(out=b1[:, 6:8, qb * 128:(qb + 1) * 128],
                                                      in0=pso[:, 0:256].rearrange("p (a t) -> p a t", a=2),
                                                      in1=rc.rearrange("p (a t) -> p a t", a=2), op=ALU.mult),
                     [pko, rk], [('b1', 6, qb), ('b1', 7, qb)])
            for s in range(4):
                q0 = 2048 + 4 * s
                pso, pko = nextacc()
                for g in range(4):
                    kct, kck = kcbuf[g % 2], ('kcb', g % 2)
                    vct, vck = vcbuf[g % 2], ('vcb', g % 2)
                    c.dma('pool', kct, kcT[l, s].rearrange("p (a k) -> p a k", a=2)[:, :, g * 512:(g + 1) * 512], writes=[kck])
                    c.dma('pool', vct, vcd[l, s].rearrange("p (b d) -> p b d", b=16)[:, g * 4:(g + 1) * 4, :], writes=[vck])
                    pss2 = [nextps(), nextps()]
                    for kb in range(4):
                        for h in range(4):
                            hh, hp = h % 2, h // 2
                            pss, pks = pss2[hh]
                            o0 = (kb * 2 + hp) * 4
                            c.op('pe', lambda e: e.matmul(pss[:, o0:o0 + 4], lhsT=kct[hh * 64:(hh + 1) * 64, hp, kb * 128:(kb + 1) * 128],
                                                          rhs=QT[hh * 64:(hh + 1) * 64, hp, q0:q0 + 4], start=True, stop=True),
                                 [kck, ('QT', 16)], [pks])
                    dcnt['i'] += 1
                    ei = dcnt['i'] % 3
                    es, ek = esb[ei], ('esb', ei)
                    e5 = es[:, 0:64].rearrange("p (b a x q) -> p b a x q", b=4, a=2, x=2)
                    for hh in range(2):
                        pss, pks = pss2[hh]
                        c.op('act', lambda e: e.activation(out=e5[:, :, :, hh, :], in_=pss[:, 0:32].rearrange("p (b a q) -> p b a q", b=4, a=2),
                                                           func=AF.Exp, scale=0.125), [pks], [ek])
                    e4 = es[:, 0:64].rearrange("p (b h q) -> p b h q", b=4, h=4)
                    c.op('dve', lambda e: e.tensor_tensor(out=e4, in0=e4, in1=maskSb[:, g * 4:(g + 1) * 4, :].unsqueeze(2).to_broadcast([128, 4, 4, 4]),
                                                          op=ALU.mult), [ek, 'cbb'], [ek])
                    for kb in range(4):
                        for h in range(4):
                            hh, hp = h % 2, h // 2
                            o0 = (kb * 4 + h) * 4
                            first = (g == 0 and kb == 0)
                            c.op('pe', lambda e: e.matmul(pso[hh * 64:(hh + 1) * 64, hp * 4:hp * 4 + 4], lhsT=vct[:, kb, h * 64:(h + 1) * 64],
                                                          rhs=es[:, o0:o0 + 4], start=(first and hp == 0), stop=False), [vck, ek], [pko])
                            c.op('pe', lambda e: e.matmul(pso[hh * 64:(hh + 1) * 64, 8 + hp * 4:8 + hp * 4 + 4], lhsT=onesb,
                                                          rhs=es[:, o0:o0 + 4], start=False, stop=False), ['cbb', ek], [pko])
                    yield
                pss2 = [nextps(), nextps()]
                for h in range(4):
                    hh, hp = h % 2, h // 2
                    pss, pks = pss2[hh]
                    c.op('pe', lambda e: e.matmul(pss[0:16, hp * 4:hp * 4 + 4], lhsT=KT[hh * 64:(hh + 1) * 64, hp, 2048:2064],
                                                  rhs=QT[hh * 64:(hh + 1) * 64, hp, q0:q0 + 4], start=True, stop=True),
                         [('KT', 16), ('QT', 16)], [pks])
                dcnt['i'] += 1
                ei = dcnt['i'] % 3
                es, ek = esb[ei], ('esb', ei)
                e3n = es[0:16, 0:16].rearrange("p (a x q) -> p a x q", a=2, x=2)
                for hh in range(2):
                    pss, pks = pss2[hh]
                    c.op('act', lambda e: e.activation(out=e3n[:, :, hh, :], in_=pss[0:16, 0:8].rearrange("p (a q) -> p a q", a=2),
                                                       func=AF.Exp, scale=0.125), [pks], [ek])
                e3 = es[0:16, 0:16].rearrange("p (h q) -> p h q", h=4)
                c.op('dve', lambda e: e.tensor_tensor(out=e3, in0=e3, in1=maskNb[:, s, :].unsqueeze(1).to_broadcast([16, 4, 4]), op=ALU.mult),
                     [ek, 'cbb'], [ek])
                for h in range(4):
                    hh, hp = h % 2, h // 2
                    c.op('pe', lambda e: e.matmul(pso[hh * 64:(hh + 1) * 64, hp * 4:hp * 4 + 4], lhsT=Vall[0:16, 16, h * 64:(h + 1) * 64],
                                                  rhs=es[0:16, h * 4:h * 4 + 4], start=False, stop=True), [('Vall', 16), ek], [pko])
                    c.op('pe', lambda e: e.matmul(pso[hh * 64:(hh + 1) * 64, 8 + hp * 4:8 + hp * 4 + 4], lhsT=onesb[0:16, :],
                                                  rhs=es[0:16, h * 4:h * 4 + 4], start=False, stop=True), ['cbb', ek], [pko])
                rc, rk = recb[s % 2], ('recb', s % 2)
                c.op('dve', lambda e: e.reciprocal(out=rc[:, 0:8], in_=pso[:, 8:16]), [pko], [rk])
                c.op('dve', lambda e: e.tensor_tensor(out=b1[:, 6:8, q0:q0 + 4], in0=pso[:, 0:8].rearrange("p (a t) -> p a t", a=2),
                                                      in1=rc[:, 0:8].rearrange("p (a t) -> p a t", a=2), op=ALU.mult),
                     [pko, rk], [('b1', 6, 16), ('b1', 7, 16)])

        def gen_A(arena):
            nextps = cad_ps
            TA = 128
            axp = arena.alloc([2, TA + 3])
            agt = arena.alloc([2, TA])
            xc = arena.alloc([2, TA])
            xcb = arena.alloc([2, TA], BF16)
            gx = arena.alloc([2, TA])
            ga = arena.alloc([2, TA])
            av = arena.alloc([2, TA])
            bi_ = arena.alloc([2, TA])
            hh_ = arena.alloc([2, TA])
            hst = arena.alloc([2])
            segs = [(n0, TA, 0, n0 == 0, n0 + TA == 2048) for n0 in range(0, 2048, TA)] + \
                   [(2048 + 4 * s, 4, 1 + s, True, True) for s in range(4)]
            for (n0, T, sq, first, last) in segs:
                if first:
                    if sq == 0:
                        c.op('dve', lambda e: e.memset(axp[:, :, 0:3], 0.0), [], ['axp'])
                        c.op('dve', lambda e: e.memset(hst, 0.0), [], ['hst'])
                    else:
                        s = sq - 1
                        c.op('dve', lambda e: e.tensor_copy(out=axp[:, :, 0:3],
                                                            in_=spf[:, SP['a_conv'] + s * 6:SP['a_conv'] + s * 6 + 6].rearrange("p (c j) -> p c j", c=2)),
                             ['spf'], ['axp'])
                        c.op('dve', lambda e: e.tensor_copy(out=hst, in_=spf[:, SP['a_h'] + s * 2:SP['a_h'] + s * 2 + 2]), ['spf'], ['hst'])
                else:
                    c.op('dve', lambda e: e.tensor_copy(out=axp[:, :, 0:3], in_=axp[:, :, TA:TA + 3]), ['axp'], ['axp'])
                src, ks = ldcols(None, 0, 2, n0, T)
                c.dma('sp', axp[:, :, 3:3 + T], src, reads=ks, writes=['axp'])
                src, ks = ldcols(None, 256, 2, n0, T)
                c.dma('sp', agt[:, :, :T], src, reads=ks, writes=['agt'])
                for cc in range(2):
                    c.op('dve', lambda e: e.tensor_scalar(out=xc[:, cc, :T], in0=axp[:, cc, 0:T], scalar1=pcol('a_cw', cc * 4),
                                                          scalar2=pcol('a_cb', cc), op0=ALU.mult, op1=ALU.add), ['axp', 'ppf'], ['xc'])
                    for j in range(1, 4):
                        c.op('dve', lambda e: e.scalar_tensor_tensor(out=xc[:, cc, :T], in0=axp[:, cc, j:j + T], scalar=pcol('a_cw', cc * 4 + j),
                                                                     in1=xc[:, cc, :T], op0=ALU.mult, op1=ALU.add), ['axp', 'ppf', 'xc'], ['xc'])
                c.op('act', lambda e: e.activation(out=xcb[:, :, :T], in_=xc[:, :, :T], func=AF.Copy), ['xc'], ['xcb'])
                yield
                psx, pkx = nextps()
                for cc in range(2):
                    c.op('pe', lambda e: e.matmul(psx[:, cc * 128:cc * 128 + T], lhsT=pmat[:, cc * 128:(cc + 1) * 128], rhs=xcb[:, cc, :T],
                                                  start=True, stop=True), ['pmat', 'xcb'], [pkx])
                for cc in range(2):
                    c.op('pe', lambda e: e.matmul(psx[:, 256 + cc * 128:256 + cc * 128 + T], lhsT=pmat[:, 256 + cc * 128:256 + (cc + 1) * 128], rhs=xcb[:, cc, :T],
                                                  start=True, stop=True), ['pmat', 'xcb'], [pkx])
                for cc in range(2):
                    c.op('act', lambda e: e.activation(out=gx[:, cc, :T], in_=psx[:, cc * 128:cc * 128 + T], func=AF.Sigmoid, bias=pcol('a_gxb', cc), scale=1.0),
                         [pkx, 'ppf'], ['gx'])
                    c.op('act', lambda e: e.activation(out=ga[:, cc, :T], in_=psx[:, 256 + cc * 128:256 + cc * 128 + T], func=AF.Sigmoid, bias=pcol('a_gab', cc), scale=1.0),
                         [pkx, 'ppf'], ['ga'])
                yield
                for cc in range(2):
                    c.op('act', lambda e: e.activation(out=av[:, cc, :T], in_=ga[:, cc, :T], func=AF.Exp, scale=pder[:, cc:cc + 1]), ['ga', 'pder'], ['av'])
                    c.op('act', lambda e: e.activation(out=bi_[:, cc, :T], in_=ga[:, cc, :T], func=AF.Exp, scale=pder[:, 2 + cc:3 + cc]), ['ga', 'pder'], ['bi'])
                c.op('act', lambda e: e.activation(out=bi_[:, :, :T], in_=bi_[:, :, :T], func=AF.Sqrt, bias=1.0, scale=-1.0), ['bi'], ['bi'])
                c.op('dve', lambda e: e.tensor_tensor(out=bi_[:, :, :T], in0=bi_[:, :, :T], in1=gx[:, :, :T], op=ALU.mult), ['bi', 'gx'], ['bi'])
                c.op('dve', lambda e: e.tensor_tensor(out=bi_[:, :, :T], in0=bi_[:, :, :T], in1=xc[:, :, :T], op=ALU.mult), ['bi', 'xc'], ['bi'])
                for cc in range(2):
                    c.op('dve', lambda e: e.tensor_tensor_scan(out=hh_[:, cc, :T], data0=av[:, cc, :T], data1=bi_[:, cc, :T], initial=hst[:, cc:cc + 1],
                                                               op0=ALU.mult, op1=ALU.add), ['av', 'bi', 'hst'], ['hh'])
                c.op('dve', lambda e: e.tensor_copy(out=hst, in_=hh_[:, :, T - 1]), ['hh'], ['hst'])
                yield
                c.op('act', lambda e: e.activation(out=agt[:, :, :T], in_=agt[:, :, :T], func=AF.Gelu_apprx_tanh), ['agt'], ['agt'])
                c.op('dve', lambda e: e.tensor_tensor(out=b1[:, 0:2, n0:n0 + T], in0=hh_[:, :, :T], in1=agt[:, :, :T], op=ALU.mult),
                     ['hh', 'agt'], k_b1([0, 1], n0, T))
                if last:
                    c.op('dve', lambda e: e.tensor_copy(out=osm[:, OS['a_h'] + sq * 2:OS['a_h'] + sq * 2 + 2], in_=hst), ['hst'], ['osm'])
                    c.op('dve', lambda e: e.tensor_copy(out=osm[:, OS['a_conv'] + sq * 6:OS['a_conv'] + sq * 6 + 6].rearrange("p (c j) -> p c j", c=2),
                                                        in_=axp[:, :, T:T + 3]), ['axp'], ['osm'])

        c.barrier()
        arena.reset()
        arena2.reset()
        gens = []
        cad_ps = mkps([4, 5])
        if 'B' in enable:
            ST_ = arena2.alloc([4, 64])
            STb_ = arena2.alloc([4, 64], BF16)
            tok = {'next': 0}
            for tid in range(2):
                bps = mkps([0, 1] if tid == 0 else [2, 3])
                gens.append(build_B(c, nc, l, arena, bps, psb, b1, colsT, ppf, pder, spf, pmat, wkv_in, o_wkv, osm, cbf, cbb, caf, k_b1, ldcols, pcol,
                                    tid, ST_, STb_, tok, arena2))
        if 'C' in enable:
            gens.append(build_C(c, nc, l, arena2, cad_ps, psb, b1, colsT, ppf, pder, spf, cs_in, o_cs, cbf, cbb, caf, k_b1, ldcols, pcol))
        if 'D' in enable:
            gens.append(gen_D(arena2))
        if 'A' in enable:
            gens.append(gen_A(arena2))
        nB = 2 if 'B' in enable else 0
        bset = set(id(g) for g in gens[:nB])
        rnd = 0
        while gens:
            for g in list(gens):
                if id(g) not in bset and SLOW_EVERY > 1 and (rnd % SLOW_EVERY) != 0 and any(id(x) in bset for x in gens):
                    continue
                try:
                    next(g)
                except StopIteration:
                    gens.remove(g)
            rnd += 1
        chk('B')
        c.barrier()
        arena.reset()
        wo = arena.alloc([8, 1024], BF16)
        gtb = [arena.alloc([4, 512], BF16) for _ in range(3)]
        prodb = [arena.alloc([4, 512]) for _ in range(2)]
        mg = arena.alloc([8, 512], BF16)
        wbr = []
        for i in range(2):
            ws, wk = getslot()
            w4 = ws[:, 0:4096].rearrange("p (k c) -> p k c", k=4)
            c.dma('pool', w4, w_br[l].rearrange("(k p) c -> p k c", p=128)[:, i * 4:(i + 1) * 4, :], writes=[wk])
            wbr.append((w4, wk))
        c.dma('pool', wo, w_out[l].rearrange("(k p) c -> p k c", p=128), writes=['wo'])
        gview = gT.rearrange("(n c p) t -> p n c t", n=4, c=8, p=128)
        gi = 0
        load_gains(2 + l)
        pendN = [None]
        for j, (n0, nn) in enumerate(NTILES):
            for dj in range(8):
                gt, gk = gtb[gi % 3], ('gtb', gi % 3)
                pr, prk = prodb[gi % 2], ('prodb', gi % 2)
                gi += 1
                c.dma('sp', gt[:, :, :nn], gview[:, :, dj, n0:n0 + nn], reads=[('cols', 30 + n * 8 + dj, j) for n in range(4)], writes=[gk])
                for n in range(4):
                    ps, pk = nextps()
                    w4, wk = wbr[n // 2]
                    for kc in range(2):
                        c.op('pe', lambda e: e.matmul(ps[:, :nn], lhsT=w4[:, (n % 2) * 2 + kc, dj * 128:(dj + 1) * 128], rhs=b1[:, 2 * n + kc, n0:n0 + nn],
                                                      start=(kc == 0), stop=(kc == 1)), [wk] + k_b1([2 * n + kc], n0, nn), [pk])
                    c.op('dve', lambda e: e.tensor_tensor(out=pr[:, n, :nn], in0=ps[:, :nn], in1=gt[:, n, :nn], op=ALU.mult), [pk, gk], [prk])
                c.op('pool', lambda e: e.tensor_tensor(out=pr[:, 0, :nn], in0=pr[:, 0, :nn], in1=pr[:, 1, :nn], op=ALU.add), [prk], [prk])
                c.op('pool', lambda e: e.tensor_tensor(out=pr[:, 2, :nn], in0=pr[:, 2, :nn], in1=pr[:, 3, :nn], op=ALU.add), [prk], [prk])
                c.op('pool', lambda e: e.tensor_tensor(out=mg[:, dj, :nn], in0=pr[:, 0, :nn], in1=pr[:, 2, :nn], op=ALU.add), [prk], [('mg', dj)])
            for st in range((nn + 127) // 128):
                R = min(128, nn - st * 128)
                t = n0 // 128 + st
                xt, xk = xbuf[t % 3], ('xt', t % 3)
                c.dma('sp', xt[:R], xres[n0 + st * 128:n0 + st * 128 + R, :], reads=[('xres', t)], writes=[xk])
                for half in range(2):
                    ps, pk = nextps()
                    for kc in range(8):
                        c.op('pe', lambda e: e.matmul(ps[:R, :512], lhsT=mg[:, kc, st * 128:st * 128 + R], rhs=wo[:, kc, half * 512:(half + 1) * 512],
                                                      start=(kc == 0), stop=(kc == 7)), [('mg', kc), 'wo'], [pk])
                    c.op('dve', lambda e: e.tensor_tensor(out=xt[:R, half * 512:(half + 1) * 512], in0=ps[:R, :512],
                                                          in1=xt[:R, half * 512:(half + 1) * 512], op=ALU.add), [pk, xk], [xk])
                c.dma('sp', xres[n0 + st * 128:n0 + st * 128 + R, :], xt[:R], reads=[xk], writes=[('xres', t)])
                if pendN[0] is not None:
                    pendN[0]()
                pendN[0] = (lambda xt=xt, xk=xk, t=t, r0=n0 + st * 128, R=R: norm_tile(xt, xk, t, r0, R))

        if pendN[0] is not None:
            pendN[0]()
        chk('merge')
        c.barrier()
        arena.reset()
        hgp = [arena.alloc([516]) for _ in range(2)]
        hgs = [arena.alloc([4, 6]) for _ in range(2)]
        cvb = [arena.alloc([512]) for _ in range(3)]
        hmo = [arena.alloc([512], BF16) for _ in range(3)]
        hub = [arena.alloc([512], BF16) for _ in range(3)]
        pendB = [None]
        wgv = w_g[l].rearrange("(k p) c -> p k c", p=128)
        wuv = w_u[l].rearrange("(k p) c -> p k c", p=128)
        hi = 0
        def load_fb(fb):
            ws, wk = getslot()
            w5 = ws[:, 0:4096].rearrange("p (g k c) -> p g k c", g=2, k=8)
            c.dma('pool', w5[:, 0], wgv[:, :, fb * 256:(fb + 1) * 256], writes=[wk])
            c.dma('pool', w5[:, 1], wuv[:, :, fb * 256:(fb + 1) * 256], writes=[wk])
            return w5, wk
        nxtw = load_fb(0)
        for fb in range(12):
            w5, wk = nxtw
            if fb + 1 < 12:
                nxtw = load_fb(fb + 1)
            for fc in range(2):
                f = fb * 2 + fc
                hg, hk = hgp[fc], ('hgp', fc)
                hs, hsk = hgs[fc], ('hgs', fc)
                c.op('pool', lambda e: e.memset(hg[:, 0:2], 0.0), [], [hk])
                for j, (n0, nn) in enumerate(NTILES):
                    psg, pkg = nextps()
                    psu, pku = nextps()
                    for k in range(8):
                        c.op('pe', lambda e: e.matmul(psg[:, :nn], lhsT=w5[:, 0, k, fc * 128:(fc + 1) * 128], rhs=b1[:, k, n0:n0 + nn],
                                                      start=(k == 0), stop=(k == 7)), [wk] + k_b1([k], n0, nn), [pkg])
                    for k in range(8):
                        c.op('pe', lambda e: e.matmul(psu[:, :nn], lhsT=w5[:, 1, k, fc * 128:(fc + 1) * 128], rhs=b1[:, k, n0:n0 + nn],
                                                      start=(k == 0), stop=(k == 7)), [wk] + k_b1([k], n0, nn), [pku])
                    cv, ck = cvb[hi % 3], ('cvb', hi % 3)
                    ho, hok = hmo[hi % 3], ('hmo', hi % 3)
                    hi += 1
                    w0, w1, w2_, bb = pcol('f_cw', f * 3), pcol('f_cw', f * 3 + 1), pcol('f_cw', f * 3 + 2), pcol('f_cb', f)
                    hu, huk = hub[hi % 3], ('hub', hi % 3)
                    c.op('act', lambda e: e.activation(out=hu[:, :nn], in_=psu[:, :nn], func=AF.Copy), [pku], [huk])
                    if j < 4:
                        c.op('act', lambda e: e.activation(out=hg[:, 2:2 + nn], in_=psg[:, :nn], func=AF.Copy), [pkg], [hk])
                        c.op('dve', lambda e: e.tensor_scalar(out=cv[:, :nn], in0=hg[:, 0:nn], scalar1=w0, scalar2=bb, op0=ALU.mult, op1=ALU.add),
                             [hk, 'ppf'], [ck])
                        c.op('dve', lambda e: e.scalar_tensor_tensor(out=cv[:, :nn], in0=hg[:, 1:1 + nn], scalar=w1, in1=cv[:, :nn], op0=ALU.mult, op1=ALU.add),
                             [hk, 'ppf', ck], [ck])
                        c.op('dve', lambda e: e.scalar_tensor_tensor(out=cv[:, :nn], in0=hg[:, 2:2 + nn], scalar=w2_, in1=cv[:, :nn], op0=ALU.mult, op1=ALU.add),
                             [hk, 'ppf', ck], [ck])
                        if j == 3:
                            c.op('pool', lambda e: e.tensor_copy(out=osm[:, OS['f_conv'] + f * 2:OS['f_conv'] + f * 2 + 2], in_=hg[:, nn:nn + 2]), [hk], ['osm'])
                        else:
                            c.op('pool', lambda e: e.tensor_copy(out=hg[:, 0:2], in_=hg[:, nn:nn + 2]), [hk], [hk])
                    else:
                        c.op('pool', lambda e: e.tensor_copy(out=hs[:, :, 0:2],
                                                             in_=spf[:, SP['f_conv']:SP['f_conv'] + 192].rearrange("p (s f j) -> p s f j", s=4, f=24)[:, :, f, :]),
                             ['spf'], [hsk])
                        c.op('act', lambda e: e.activation(out=hs[:, :, 2:6], in_=psg[:, 0:16].rearrange("p (s i) -> p s i", s=4), func=AF.Copy), [pkg], [hsk])
                        cv3 = cv[:, 0:16].rearrange("p (s i) -> p s i", s=4)
                        c.op('dve', lambda e: e.tensor_scalar(out=cv3, in0=hs[:, :, 0:4], scalar1=w0, scalar2=bb, op0=ALU.mult, op1=ALU.add),
                             [hsk, 'ppf'], [ck])
                        c.op('dve', lambda e: e.scalar_tensor_tensor(out=cv3, in0=hs[:, :, 1:5], scalar=w1, in1=cv3, op0=ALU.mult, op1=ALU.add),
                             [hsk, 'ppf', ck], [ck])
                        c.op('dve', lambda e: e.scalar_tensor_tensor(out=cv3, in0=hs[:, :, 2:6], scalar=w2_, in1=cv3, op0=ALU.mult, op1=ALU.add),
                             [hsk, 'ppf', ck], [ck])
                        for s in range(4):
                            o0 = OS['f_conv'] + (1 + s) * 48 + f * 2
                            c.op('pool', lambda e: e.tensor_copy(out=osm[:, o0:o0 + 2], in_=hs[:, s, 4:6]), [hsk], ['osm'])
                    def stageB(cv=cv, ck=ck, ho=ho, hok=hok, hu=hu, huk=huk, f=f, j=j, n0=n0, nn=nn):
                        c.op('act', lambda e: e.activation(out=cv[:, :nn], in_=cv[:, :nn], func=AF.Gelu_apprx_tanh), [ck], [ck])
                        c.op('dve', lambda e: e.tensor_tensor(out=ho[:, :nn], in0=hu[:, :nn], in1=cv[:, :nn], op=ALU.mult), [huk, ck], [hok])
                        c.dma('sp', hmT[f * 128:(f + 1) * 128, n0:n0 + nn], ho[:, :nn], reads=[hok], writes=[('hmT', f, j)])
                    if pendB[0] is not None:
                        pendB[0]()
                    pendB[0] = stageB
        if pendB[0] is not None:
            pendB[0]()
        chk('ffn1')
        c.barrier()
        arena.reset()
        xt4 = [arena.alloc([1024]) for _ in range(4)]
        hm = arena.alloc([24, 512], BF16)
        wdv = w_d[l].rearrange("(f p) c -> p f c", p=128)
        hmv = hmT.rearrange("(f p) t -> p f t", p=128)
        lastl = (l == DEPTH - 1)
        pendF = [None]
        load_gains(4 if lastl else l + 1)
        wseq = [(j_, h_, k_) for j_ in range(len(NTILES)) for h_ in range(2) for k_ in range(2)]

        def load_wd(q):
            j_, h_, k_ = wseq[q]
            ws, wk = getslot()
            w3 = ws.rearrange("p (f c) -> p f c", f=12)
            c.dma('pool', w3, wdv[:, k_ * 12:(k_ + 1) * 12, h_ * 512:(h_ + 1) * 512], writes=[wk])
            return w3, wk
        wq = 0
        nxtw = load_wd(0)
        for j, (n0, nn) in enumerate(NTILES):
            c.dma('sp', hm[:, :, :nn], hmv[:, :, n0:n0 + nn], reads=[('hmT', f, j) for f in range(24)], writes=['hm'])
            nst = (nn + 127) // 128
            for st in range(nst):
                R = min(128, nn - st * 128)
                t = n0 // 128 + st
                c.dma('sp', xt4[st][:R], xres[n0 + st * 128:n0 + st * 128 + R, :], reads=[('xres', t)], writes=[('xt4', st)])
            for half in range(2):
                banks = [nextps() for _ in range(nst)]
                for kg in range(2):
                    w3, wk = nxtw
                    wq += 1
                    if wq < len(wseq):
                        nxtw = load_wd(wq)
                    for fk in range(12):
                        for st in range(nst):
                            R = min(128, nn - st * 128)
                            ps, pk = banks[st]
                            c.op('pe', lambda e: e.matmul(ps[:R, :512], lhsT=hm[:, kg * 12 + fk, st * 128:st * 128 + R], rhs=w3[:, fk, :],
                                                          start=(kg == 0 and fk == 0), stop=(kg == 1 and fk == 11)), ['hm', wk], [pk])
                for st in range(nst):
                    R = min(128, nn - st * 128)
                    ps, pk = banks[st]
                    c.op('dve', lambda e: e.tensor_tensor(out=xt4[st][:R, half * 512:(half + 1) * 512], in0=ps[:R, :512],
                                                          in1=xt4[st][:R, half * 512:(half + 1) * 512], op=ALU.add), [pk, ('xt4', st)], [('xt4', st)])
            for st in range(nst):
                R = min(128, nn - st * 128)
                t = n0 // 128 + st
                if not lastl:
                    c.dma('sp', xres[n0 + st * 128:n0 + st * 128 + R, :], xt4[st][:R], reads=[('xt4', st)], writes=[('xres', t)])
                if pendF[0] is not None:
                    pendF[0]()
                pendF[0] = (lambda st=st, t=t, r0=n0 + st * 128, R=R: norm_tile(xt4[st], ('xt4', st), t, r0, R, final=lastl))
            if pendF[0] is not None:
                pendF[0]()
                pendF[0] = None
        chk('ffn2')
        c.dma('sp', o_small[l], osm, reads=['osm'])
        c.barrier()

    c.finish()
    print("instructions", c.n_inst, "waits", c.n_wait)


def build_C(c, nc, l, arena, nextps, psb, b1, colsT, ppf, pder, spf, cs_in, o_cs, cbf, cbb, caf, k_b1, ldcols, pcol):
    TC = 128
    identb = cbb[:, CB['ident']:CB['ident'] + 128]
    blk64b = cbb[:, CB['blk64']:CB['blk64'] + 128]
    tri_incl = cbf[0:64, CB['tri_incl']:CB['tri_incl'] + 64]
    resetm = caf[:, CA['reset']:CA['reset'] + 512]
    qf = arena.alloc([4, TC])
    ff = arena.alloc([4, TC])
    kf = arena.alloc([4, TC])
    bc = arena.alloc([4, TC])
    ep = arena.alloc([4, TC])
    vf = arena.alloc([2, TC])
    gf = arena.alloc([2, TC])
    oraw = arena.alloc([2, TC])
    rs = arena.alloc([2, TC])
    qt = arena.alloc([4, TC], BF16)
    kt = arena.alloc([4, TC], BF16)
    vb = arena.alloc([2, TC], BF16)
    sqb = arena.alloc([2, TC], BF16)
    elb = arena.alloc([4, 4])
    attb = [arena.alloc([4, 64], BF16) for _ in range(2)]
    tmb = [arena.alloc([768], BF16) for _ in range(2)]
    S = arena.alloc([4, 64])
    Sb = arena.alloc([4, 64], BF16)
    segs = [(n0, TC, 0, n0 == 0, n0 + TC == 2048) for n0 in range(0, 2048, TC)] + \
           [(2048 + 4 * s, 4, 1 + s, True, True) for s in range(4)]
    it = 0
    for (n0, T, sq, first, last) in segs:
        CL = 64 if T >= 64 else T
        nck = T // CL
        if first:
            if sq == 0:
                c.op('dve', lambda e: e.memset(S, 0.0), [], ['S'])
            else:
                c.dma('sp', S.rearrange("p h v -> p (h v)"), cs_in[l][:, (sq - 1) * 256:sq * 256], writes=['S'])
            c.op('act', lambda e: e.activation(out=Sb, in_=S, func=AF.Copy), ['S'], ['Sb'])
        src, ks = ldcols(None, 1536, 4, n0, T)
        c.dma('sp', qf[:, :, :T], src, reads=ks, writes=['qf'])
        src, ks = ldcols(None, 2048, 4, n0, T)
        c.dma('sp', ff[:, :, :T], src, reads=ks, writes=['ff'])
        src, ks = ldcols(None, 2560, 2, n0, T)
        c.dma('sp', vf[:, :, :T], src, reads=ks, writes=['vf'])
        src, ks = ldcols(None, 2816, 2, n0, T)
        c.dma('sp', gf[:, :, :T], src, reads=ks, writes=['gf'])
        yield
        c.op('act', lambda e: e.activation(out=ff[:, :, :T], in_=ff[:, :, :T], func=AF.Sigmoid), ['ff'], ['ff'])
        yield
        for h in range(4):
            c.op('dve', lambda e: e.tensor_scalar(out=ff[:, h, :T], in0=ff[:, h, :T], scalar1=pder[:, 8 + h:9 + h], scalar2=pder[:, 4 + h:5 + h],
                                                  op0=ALU.mult, op1=ALU.add), ['ff', 'pder'], ['ff'])
        yield
        c.op('dve', lambda e: e.tensor_scalar(out=kf[:, :, :T], in0=ff[:, :, :T], scalar1=-1.0, scalar2=1.0, op0=ALU.mult, op1=ALU.add), ['ff'], ['kf'])
        yield
        c.op('act', lambda e: e.activation(out=ff[:, :, :T], in_=ff[:, :, :T], func=AF.Ln), ['ff'], ['ff'])
        yield
        for h in range(4):
            c.op('dve', lambda e: e.tensor_tensor_scan(out=bc[:, h, :T], data0=resetm[:, 0:T], data1=ff[:, h, :T], initial=0.0,
                                                       op0=ALU.mult, op1=ALU.add), ['ff', 'caf'], ['bc'])
        yield
        c.op('act', lambda e: e.activation(out=ep[:, :, :T], in_=bc[:, :, :T], func=AF.Exp), ['bc'], ['ep'])
        yield
        c.op('dve', lambda e: e.tensor_tensor(out=qt[:, :, :T], in0=qf[:, :, :T], in1=ep[:, :, :T], op=ALU.mult), ['qf', 'ep'], ['qt'])
        yield
        c.op('dve', lambda e: e.tensor_copy(out=elb[:, :, 0:nck], in_=ep[:, :, CL - 1:T:CL]), ['ep'], ['elb'])
        yield
        c.op('act', lambda e: e.activation(out=bc[:, :, :T], in_=bc[:, :, :T], func=AF.Exp, scale=-1.0), ['bc'], ['bc'])
        yield
        c.op('dve', lambda e: e.tensor_tensor(out=kt[:, :, :T], in0=kf[:, :, :T], in1=bc[:, :, :T], op=ALU.mult), ['kf', 'bc'], ['kt'])
        yield
        c.op('act', lambda e: e.activation(out=vb[:, :, :T], in_=vf[:, :, :T], func=AF.Copy), ['vf'], ['vb'])
        yield
        for ck in range(nck):
            cs_ = slice(ck * CL, (ck + 1) * CL)
            at, atk = attb[it % 2], ('attb', it % 2)
            tm, tmk = tmb[it % 2], ('tmb', it % 2)
            it += 1
            pa, pka = nextps()
            for h in range(4):
                c.op('pe', lambda e: e.matmul(pa[0:CL, h * 64:h * 64 + CL], lhsT=kt[:, h, cs_], rhs=qt[:, h, cs_], start=True, stop=True),
                     ['kt', 'qt'], [pka])
            c.op('dve', lambda e: e.tensor_tensor(out=at[0:CL, :, 0:CL], in0=pa[0:CL, 0:256].rearrange("p (h t) -> p h t", h=4)[:, :, 0:CL],
                                                  in1=tri_incl[0:CL, 0:CL].unsqueeze(1).to_broadcast([CL, 4, CL]), op=ALU.mult), [pka, 'cbf'], [atk])
            for h in range(4):
                c.op('pe', lambda e: e.transpose(psb[0:CL, h * 128:(h + 1) * 128], kt[:, h, cs_], identb), ['kt', 'cbb'], ['psb'])
            for hp in range(2):
                c.op('pe', lambda e: e.transpose(psb[0:CL, 512 + hp * 128:512 + (hp + 1) * 128], vb[:, hp, cs_], identb), ['vb', 'cbb'], ['psb'])
            c.op('act', lambda e: e.activation(out=tm[0:CL, :], in_=psb[0:CL, 0:768], func=AF.Copy), ['psb'], [tmk])
            yield
            po, pko = nextps()
            for h in range(4):
                hh, hp = h % 2, h // 2
                c.op('pe', lambda e: e.matmul(po[hh * 64:(hh + 1) * 64, hp * 64:hp * 64 + CL], lhsT=Sb[:, h, :], rhs=qt[:, h, cs_], start=True, stop=False),
                     ['Sb', 'qt'], [pko])
                c.op('pe', lambda e: e.matmul(po[hh * 64:(hh + 1) * 64, hp * 64:hp * 64 + CL], lhsT=tm[0:CL, 512 + h * 64:512 + (h + 1) * 64],
                                              rhs=at[0:CL, h, 0:CL], start=False, stop=True), [tmk, atk], [pko])
            c.op('act', lambda e: e.activation(out=oraw[:, :, cs_], in_=po[:, 0:128].rearrange("p (a t) -> p a t", a=2)[:, :, 0:CL], func=AF.Copy),
                 [pko], ['oraw'])
            pS, pkS = nextps()
            for h in range(4):
                c.op('pe', lambda e: e.matmul(pS[:, h * 64:(h + 1) * 64], lhsT=tm[0:CL, h * 128:(h + 1) * 128], rhs=tm[0:CL, 512 + h * 64:512 + (h + 1) * 64],
                                              start=True, stop=True), [tmk], [pkS])
            c.op('dve', lambda e: e.tensor_tensor(out=S, in0=pS[:, 0:256].rearrange("p (h v) -> p h v", h=4), in1=S, op=ALU.add), [pkS, 'S'], ['S'])
            c.op('dve', lambda e: e.tensor_tensor(out=S, in0=S, in1=elb[:, :, ck:ck + 1].to_broadcast([128, 4, 64]), op=ALU.mult), ['S', 'elb'], ['S'])
            c.op('act', lambda e: e.activation(out=Sb, in_=S, func=AF.Copy), ['S'], ['Sb'])
            yield
        c.op('pool', lambda e: e.tensor_tensor(out=sqb[:, :, :T], in0=oraw[:, :, :T], in1=oraw[:, :, :T], op=ALU.mult), ['oraw'], ['sqb'])
        pn, pkn = nextps()
        for hp in range(2):
            c.op('pe', lambda e: e.matmul(pn[:, hp * 256:hp * 256 + T], lhsT=blk64b, rhs=sqb[:, hp, :T], start=True, stop=True), ['sqb', 'cbb'], [pkn])
        c.op('act', lambda e: e.activation(out=rs[:, :, :T], in_=pn.rearrange("p (a t) -> p a t", a=2)[:, :, :T], func=AF.Ln, bias=1e-6, scale=1.0 / 64),
             [pkn], ['rs'])
        c.op('act', lambda e: e.activation(out=rs[:, :, :T], in_=rs[:, :, :T], func=AF.Exp, scale=-0.5), ['rs'], ['rs'])
        c.op('dve', lambda e: e.tensor_tensor(out=rs[:, :, :T], in0=rs[:, :, :T], in1=oraw[:, :, :T], op=ALU.mult), ['rs', 'oraw'], ['rs'])
        c.op('act', lambda e: e.activation(out=gf[:, :, :T], in_=gf[:, :, :T], func=AF.Silu), ['gf'], ['gf'])
        c.op('dve', lambda e: e.scalar_tensor_tensor(out=b1[:, 4:6, n0:n0 + T], in0=rs[:, :, :T], scalar=pcol('c_ng'), in1=gf[:, :, :T],
                                                     op0=ALU.mult, op1=ALU.mult), ['rs', 'gf', 'ppf'], k_b1([4, 5], n0, T))
        if last:
            c.dma('pool', o_cs[l, sq], S.rearrange("p h v -> p (h v)"), reads=['S'])
        yield


def build_B(c, nc, l, arena, nextps, psb, b1, colsT, ppf, pder, spf, pmat, wkv_in, o_wkv, osm, cbf, cbb, caf, k_b1, ldcols, pcol,
            tid, ST, STb, tok, xarena):
    c = CtxTag(c, 'B%d' % tid)
    TB = 64
    identb = cbb[:, CB['ident']:CB['ident'] + 128]
    identf = cbf[:, CB['ident']:CB['ident'] + 128]
    ones64 = cbb[0:64, CB['ones']:CB['ones'] + 64]
    mask2 = cbf[0:64, CB['tri_incl']:CB['tri_incl'] + 128].rearrange("p (a t) -> p a t", a=2)
    tri_sl = cbf[0:64, CB['tri_sl']:CB['tri_sl'] + 64]
    resetm = caf[:, CA['reset']:CA['reset'] + 512]
    w2b = pmat[0:64, 512:768]
    a2b = pmat[0:64, 768:1024]
    g2b = pmat[:, 1024:1280]

    def p64(name, n):
        return ppf[0:64, PK[name]:PK[name] + n]
    Xb = [xarena.alloc([14, TB + 1]) for _ in range(2)]
    Gb = [xarena.alloc([TB + 1]) for _ in range(2)]
    CM = arena.alloc([14, TB])
    gm = arena.alloc([TB])
    tw = arena.alloc([TB], BF16)
    alb = arena.alloc([TB], BF16)
    gs = arena.alloc([TB], BF16)
    ld = arena.alloc([4, TB])
    asg = arena.alloc([4, TB])
    gg = arena.alloc([4, TB])
    kk = arena.alloc([4, TB])
    km = arena.alloc([4, TB])
    cs = arena.alloc([4, TB])
    eb = arena.alloc([4, TB])
    t1 = arena.alloc([4, TB])
    yraw = arena.alloc([4, TB])
    AR = arena.alloc([4, 1, 128], BF16)
    Bt = arena.alloc([4, TB], BF16)
    Kt = arena.alloc([4, TB], BF16)
    Vt = arena.alloc([4, TB], BF16)
    h16 = arena.alloc([4, TB], BF16)
    outB = arena.alloc([4, TB], BF16)
    PC = arena.alloc([4, 2])
    tmb = [arena.alloc([768], BF16) for _ in range(1)]
    G1 = arena.alloc([4, 128], BF16)
    G2 = arena.alloc([4, 128], BF16)
    Wb = [arena.alloc([4, 192], BF16) for _ in range(2)]
    Zs = arena.alloc([4, 64], BF16)
    Us = arena.alloc([4, 64], BF16)
    allsegs = [(n0, TB, 0, n0 == 0, n0 + TB == 2048) for n0 in range(0, 2048, TB)] + \
              [(2048 + 4 * s, 4, 1 + s, True, True) for s in range(4)]
    it = 0
    mysegs = [(gi, sg) for gi, sg in enumerate(allsegs) if gi % 2 == tid]
    xsrc = colsT[512:1408].rearrange("(j k) t -> k j t", k=64)

    def issue_load(k):
        gi, (n0, T, sq, first, last) = mysegs[k]
        X, G, kx, kg = Xb[k % 2], Gb[k % 2], ('X', k % 2), ('G', k % 2)
        if first:
            if sq == 0:
                c.op('pool', lambda e: e.memset(X[0:64, :, 0:1], 0.0), [], [kx])
                c.op('pool', lambda e: e.memset(G[:, 0:1], 0.0), [], [kg])
            else:
                s_ = sq - 1
                c.op('pool', lambda e: e.tensor_copy(out=X[0:64, :, 0], in_=spf[0:64, SP['b_sh64'] + s_ * 14:SP['b_sh64'] + s_ * 14 + 14]), ['spf'], [kx])
                c.op('pool', lambda e: e.tensor_copy(out=G[:, 0:1], in_=spf[:, SP['b_shg'] + s_:SP['b_shg'] + s_ + 1]), ['spf'], [kg])
            c.dma('sp', X[0:64, :, 1:1 + T], xsrc[:, :, n0:n0 + T], reads=[('cols', rb, n0 // 512) for rb in range(4, 11)], writes=[kx])
            c.dma('sp', G[:, 1:1 + T], colsT[1408:1536, n0:n0 + T], reads=[('cols', 11, n0 // 512)], writes=[kg])
        else:
            jj = sorted(set([(n0 - 1) // 512, n0 // 512]))
            c.dma('sp', X[0:64, :, 0:1 + T], xsrc[:, :, n0 - 1:n0 + T], reads=[('cols', rb, j_) for rb in range(4, 11) for j_ in jj], writes=[kx])
            c.dma('sp', G[:, 0:1 + T], colsT[1408:1536, n0 - 1:n0 + T], reads=[('cols', 11, j_) for j_ in jj], writes=[kg])
    issue_load(0)
    for k, (gi, (n0, T, sq, first, last)) in enumerate(mysegs):
        if k + 1 < len(mysegs):
            issue_load(k + 1)
        X, G, kx, kg = Xb[k % 2], Gb[k % 2], ('X', k % 2), ('G', k % 2)
        CL = 64 if T >= 64 else T
        nck = T // CL
        nr = 6 if CL == 64 else 2
        yield
        c.op('dve', lambda e: e.tensor_tensor(out=CM[0:64, :, :T], in0=X[0:64, :, 0:T], in1=X[0:64, :, 1:1 + T], op=ALU.subtract), [kx], ['CM'])
        yield
        c.op('dve', lambda e: e.tensor_tensor(out=CM[0:64, :, :T], in0=CM[0:64, :, :T], in1=p64('b64_mu', 14).unsqueeze(2).to_broadcast([64, 14, T]),
                                              op=ALU.mult), ['CM', 'ppf'], ['CM'])
        yield
        c.op('dve', lambda e: e.tensor_tensor(out=CM[0:64, :, :T], in0=CM[0:64, :, :T], in1=X[0:64, :, 1:1 + T], op=ALU.add), ['CM', kx], ['CM'])
        yield
        c.op('dve', lambda e: e.tensor_tensor(out=gm[:, :T], in0=G[:, 0:T], in1=G[:, 1:1 + T], op=ALU.subtract), [kg], ['gm'])
        yield
        c.op('dve', lambda e: e.scalar_tensor_tensor(out=gm[:, :T], in0=gm[:, :T], scalar=pcol('b_mug'), in1=G[:, 1:1 + T], op0=ALU.mult, op1=ALU.add),
             ['gm', kg, 'ppf'], ['gm'])
        if last:
            c.op('pool', lambda e: e.tensor_copy(out=osm[0:64, OS['b_sh64'] + sq * 14:OS['b_sh64'] + sq * 14 + 14], in_=X[0:64, :, T]), [kx], ['osm'])
            c.op('pool', lambda e: e.tensor_copy(out=osm[:, OS['b_shg'] + sq:OS['b_shg'] + sq + 1], in_=G[:, T:T + 1]), [kg], ['osm'])
        rr, kr, vr = CM[0:64, 0:4, :T], CM[0:64, 4:8, :T], CM[0:64, 8:12, :T]
        yield
        c.op('act', lambda e: e.activation(out=tw[0:64, :T], in_=CM[0:64, 12, :T], func=AF.Tanh), ['CM'], ['tw'])
        yield
        c.op('act', lambda e: e.activation(out=alb[0:64, :T], in_=CM[0:64, 13, :T], func=AF.Copy), ['CM'], ['alb'])
        yield
        c.op('act', lambda e: e.activation(out=gs[:, :T], in_=gm[:, :T], func=AF.Sigmoid), ['gm'], ['gs'])
        yield
        pw, pkw = nextps()
        yield
        for h in range(4):
            c.op('pe', lambda e: e.matmul(pw[0:64, h * 128:h * 128 + T], lhsT=w2b[:, h * 64:(h + 1) * 64], rhs=tw[0:64, :T], start=True, stop=True),
                 ['pmat', 'tw'], [pkw])
        yield
        for h in range(4):
            c.op('act', lambda e: e.activation(out=ld[0:64, h, :T], in_=pw[0:64, h * 128:h * 128 + T], func=AF.Sigmoid, bias=p64('b64_w0', 4)[:, h:h + 1], scale=1.0),
                 [pkw, 'ppf'], ['ld'])
        yield
        c.op('dve', lambda e: e.tensor_scalar(out=ld[0:64, :, :T], in0=ld[0:64, :, :T], scalar1=-EM05, scalar2=None, op0=ALU.mult), ['ld'], ['ld'])
        pa, pka = nextps()
        yield
        for h in range(4):
            c.op('pe', lambda e: e.matmul(pa[0:64, h * 128:h * 128 + T], lhsT=a2b[:, h * 64:(h + 1) * 64], rhs=alb[0:64, :T], start=True, stop=True),
                 ['pmat', 'alb'], [pka])
        yield
        for h in range(4):
            c.op('act', lambda e: e.activation(out=asg[0:64, h, :T], in_=pa[0:64, h * 128:h * 128 + T], func=AF.Sigmoid, bias=p64('b64_a0', 4)[:, h:h + 1], scale=1.0),
                 [pka, 'ppf'], ['asg'])
        pg, pkg = nextps()
        yield
        for h in range(4):
            c.op('pe', lambda e: e.matmul(pg[0:64, h * 128:h * 128 + T], lhsT=g2b[:, h * 64:(h + 1) * 64], rhs=gs[:, :T], start=True, stop=True),
                 ['pmat', 'gs'], [pkg])
        yield
        c.op('act', lambda e: e.activation(out=gg[0:64, :, :T], in_=pg[0:64, :].rearrange("p (h t) -> p h t", h=4)[:, :, :T], func=AF.Copy), [pkg], ['gg'])
        yield
        yield
        c.op('dve', lambda e: e.tensor_tensor(out=kk[0:64, :, :T], in0=kr, in1=p64('b64_kk', 4).unsqueeze(2).to_broadcast([64, 4, T]), op=ALU.mult),
             ['CM', 'ppf'], ['kk'])
        yield
        c.op('dve', lambda e: e.tensor_tensor(out=h16[0:64, :, :T], in0=kk[0:64, :, :T], in1=kk[0:64, :, :T], op=ALU.mult), ['kk'], ['h16'])
        pn, pkn = nextps()
        yield
        for h in range(4):
            c.op('pe', lambda e: e.matmul(pn[0:64, h * 128:h * 128 + T], lhsT=ones64, rhs=h16[0:64, h, :T], start=True, stop=True), ['cbb', 'h16'], [pkn])
        yield
        c.op('act', lambda e: e.activation(out=t1[0:64, :, :T], in_=pn[0:64, :].rearrange("p (h t) -> p h t", h=4)[:, :, :T], func=AF.Ln,
                                           bias=1e-24, scale=1.0), [pkn], ['t1'])
        yield
        c.op('act', lambda e: e.activation(out=t1[0:64, :, :T], in_=t1[0:64, :, :T], func=AF.Exp, scale=-0.5), ['t1'], ['t1'])
        yield
        c.op('dve', lambda e: e.tensor_tensor(out=kk[0:64, :, :T], in0=kk[0:64, :, :T], in1=t1[0:64, :, :T], op=ALU.mult), ['kk', 't1'], ['kk'])
        yield
        c.op('dve', lambda e: e.tensor_tensor(out=t1[0:64, :, :T], in0=asg[0:64, :, :T], in1=p64('b64_ka', 4).unsqueeze(2).to_broadcast([64, 4, T]), op=ALU.mult),
             ['asg', 'ppf'], ['t1'])
        yield
        c.op('dve', lambda e: e.tensor_tensor(out=t1[0:64, :, :T], in0=t1[0:64, :, :T], in1=pder[0:64, 12:16].unsqueeze(2).to_broadcast([64, 4, T]), op=ALU.add),
             ['t1', 'pder'], ['t1'])
        yield
        c.op('dve', lambda e: e.tensor_tensor(out=km[0:64, :, :T], in0=kr, in1=t1[0:64, :, :T], op=ALU.mult), ['CM', 't1'], ['km'])
        yield
        yield
        for h in range(4):
            c.op('dve', lambda e: e.tensor_tensor_scan(out=cs[0:64, h, :T], data0=resetm[0:64, 0:T], data1=ld[0:64, h, :T], initial=0.0,
                                                       op0=ALU.mult, op1=ALU.add), ['ld', 'caf'], ['cs'])
        AR4 = AR[0:64].rearrange("p h c (a t) -> p h c a t", a=2)

        def ch4(ap):
            return ap.rearrange("p h (c t) -> p h c t", c=nck)
        yield
        c.op('dve', lambda e: e.tensor_tensor(out=eb[0:64, :, :T], in0=cs[0:64, :, :T], in1=ld[0:64, :, :T], op=ALU.subtract), ['cs', 'ld'], ['eb'])
        yield
        c.op('act', lambda e: e.activation(out=eb[0:64, :, :T], in_=eb[0:64, :, :T], func=AF.Exp), ['eb'], ['eb'])
        yield
        c.op('dve', lambda e: e.scalar_tensor_tensor(out=AR4[:, :, 0:nck, 1, 0:CL], in0=ch4(kk[0:64, :, :T]), scalar=-1.0, in1=ch4(eb[0:64, :, :T]),
                                                     op0=ALU.mult, op1=ALU.mult), ['kk', 'eb'], ['AR'])
        yield
        c.op('act', lambda e: e.activation(out=eb[0:64, :, :T], in_=cs[0:64, :, :T], func=AF.Exp), ['cs', 'AR'], ['eb'])
        yield
        c.op('dve', lambda e: e.tensor_tensor(out=AR4[:, :, 0:nck, 0, 0:CL], in0=ch4(rr), in1=ch4(eb[0:64, :, :T]), op=ALU.mult), ['CM', 'eb'], ['AR'])
        yield
        c.op('dve', lambda e: e.tensor_copy(out=PC[0:64, :, 0:nck], in_=eb[0:64, :, CL - 1:T:CL]), ['eb'], ['PC'])
        yield
        c.op('act', lambda e: e.activation(out=eb[0:64, :, :T], in_=cs[0:64, :, :T], func=AF.Exp, scale=-1.0), ['cs', 'AR', 'PC'], ['eb'])
        yield
        c.op('dve', lambda e: e.tensor_tensor(out=t1[0:64, :, :T], in0=kk[0:64, :, :T], in1=asg[0:64, :, :T], op=ALU.mult), ['kk', 'asg'], ['t1'])
        yield
        c.op('dve', lambda e: e.tensor_tensor(out=Bt[0:64, :, :T], in0=t1[0:64, :, :T], in1=eb[0:64, :, :T], op=ALU.mult), ['t1', 'eb'], ['Bt'])
        yield
        c.op('dve', lambda e: e.tensor_tensor(out=Kt[0:64, :, :T], in0=km[0:64, :, :T], in1=eb[0:64, :, :T], op=ALU.mult), ['km', 'eb'], ['Kt'])
        yield
        c.op('act', lambda e: e.activation(out=Vt[0:64, :, :T], in_=vr, func=AF.Copy), ['CM'], ['Vt'])
        yield
        for ck in range(nck):
            cs_ = slice(ck * CL, (ck + 1) * CL)
            tm, tmk = tmb[0], ('tmbB', 0)
            it += 1
            for j, src in enumerate((Bt, Kt, Vt)):
                for h in range(4):
                    c.op('pe', lambda e: e.transpose(psb[0:CL, (j * 4 + h) * 64:(j * 4 + h + 1) * 64], src[0:64, h, cs_], identb[0:64, 0:64]),
                         [('Bt', 'Kt', 'Vt')[j], 'cbb'], ['psb'])
            c.op('act', lambda e: e.activation(out=tm[0:CL, :], in_=psb[0:CL, 0:768], func=AF.Copy), ['psb'], [tmk])
            yield

            def Btm(h):
                return tm[0:CL, h * 64:(h + 1) * 64]

            def Ktm(h):
                return tm[0:CL, 256 + h * 64:256 + (h + 1) * 64]

            def Vtm(h):
                return tm[0:CL, 512 + h * 64:512 + (h + 1) * 64]
            P1, pk1 = nextps()
            P2, pk2 = nextps()
            for h in range(4):
                c.op('pe', lambda e: e.matmul(P1[0:CL, h * 128:h * 128 + 128], lhsT=Bt[0:64, h, cs_], rhs=AR[0:64, h, ck, :], start=True, stop=True),
                     ['Bt', 'AR'], [pk1])
            for h in range(4):
                c.op('pe', lambda e: e.matmul(P2[0:CL, h * 128:h * 128 + 128], lhsT=Kt[0:64, h, cs_], rhs=AR[0:64, h, ck, :], start=True, stop=True),
                     ['Kt', 'AR'], [pk2])
            m2b = mask2[0:CL, :, 0:CL].unsqueeze(1).to_broadcast([CL, 4, 2, CL])
            W0, W1 = Wb[0], Wb[1]
            W0v = W0[0:CL].rearrange("p h (a t) -> p h a t", a=3)
            W1v = W1[0:CL].rearrange("p h (a t) -> p h a t", a=3)
            G1v = G1[0:CL].rearrange("p h (a t) -> p h a t", a=2)
            G2v = G2[0:CL].rearrange("p h (a t) -> p h a t", a=2)
            P1v = P1[0:CL, :].rearrange("p (h a t) -> p h a t", h=4, a=2)
            P2v = P2[0:CL, :].rearrange("p (h a t) -> p h a t", h=4, a=2)
            c.op('dve', lambda e: e.tensor_tensor(out=G1v[:, :, :, 0:CL], in0=P1v[:, :, :, 0:CL], in1=m2b, op=ALU.mult), [pk1, 'cbf'], ['G1'])
            c.op('dve', lambda e: e.tensor_tensor(out=G2v[:, :, :, 0:CL], in0=P2v[:, :, :, 0:CL], in1=m2b, op=ALU.mult), [pk2, 'cbf'], ['G2'])
            P3, pk3 = nextps()
            for h in range(4):
                c.op('pe', lambda e: e.matmul(P3[0:CL, h * 64:h * 64 + CL], lhsT=AR[0:64, h, ck, 64:64 + CL], rhs=Bt[0:64, h, cs_], start=True, stop=True),
                     ['Bt', 'AR'], [pk3])
            c.op('act', lambda e: e.activation(out=W0v[:, :, 0, 0:CL], in_=G1v[:, :, 1, 0:CL], func=AF.Copy), ['G1'], ['W0'])
            c.op('act', lambda e: e.activation(out=W0v[:, :, 1, 0:CL], in_=identb[0:CL, 0:CL].unsqueeze(1).to_broadcast([CL, 4, CL]), func=AF.Copy),
                 ['cbb'], ['W0'])
            c.op('dve', lambda e: e.tensor_tensor(out=W0v[:, :, 2, 0:CL], in0=P3[0:CL, 0:256].rearrange("p (h t) -> p h t", h=4)[:, :, 0:CL],
                                                  in1=tri_sl[0:CL, 0:CL].unsqueeze(1).to_broadcast([CL, 4, CL]), op=ALU.mult), [pk3, 'cbf'], ['W0'])
            yield
            cur, nxt, curk, nxtk = W0v, W1v, 'W0', 'W1'
            for r_ in range(nr):
                lastr = (r_ == nr - 1)
                PA, pkA = nextps()
                PAv = PA[0:CL, :].rearrange("p (h a t) -> p h a t", h=4, a=2)
                for h in range(4):
                    for a_ in range(2):
                        if lastr and a_ == 0:
                            continue
                        c.op('pe', lambda e: e.matmul(PAv[:, h, a_, 0:CL], lhsT=cur[:, h, 2, 0:CL], rhs=cur[:, h, a_, 0:CL], start=True, stop=True),
                             [curk], [pkA])
                if not lastr:
                    PB, pkB = nextps()
                    PBv = PB[0:CL, 0:256].rearrange("p (h t) -> p h t", h=4)
                    for h in range(4):
                        c.op('pe', lambda e: e.matmul(PBv[:, h, 0:CL], lhsT=cur[:, h, 0, 0:CL], rhs=cur[:, h, 2, 0:CL], start=True, stop=True), [curk], [pkB])
                    c.op('act', lambda e: e.activation(out=nxt[:, :, 0, 0:CL], in_=PAv[:, :, 0, 0:CL], func=AF.Copy), [pkA], [nxtk])
                    c.op('act', lambda e: e.activation(out=nxt[:, :, 2, 0:CL], in_=PBv[:, :, 0:CL], func=AF.Copy), [pkB], [nxtk])
                c.op('dve', lambda e: e.tensor_tensor(out=nxt[:, :, 1, 0:CL], in0=PAv[:, :, 1, 0:CL], in1=cur[:, :, 1, 0:CL], op=ALU.add), [pkA, curk], [nxtk])
                cur, nxt, curk, nxtk = nxt, cur, nxtk, curk
                yield
            TTv, TTk = cur, curk

            def At(h):
                return AR[0:64, h, ck, 64:64 + CL]

            def Rt(h):
                return AR[0:64, h, ck, 0:CL]
            while tok['next'] != gi:
                yield
            if first:
                if sq == 0:
                    c.op('dve', lambda e: e.memset(ST[0:64], 0.0), [], ['ST'])
                else:
                    c.dma('sp', ST[0:64].rearrange("p h v -> p (h v)"), wkv_in[l][:, (sq - 1) * 256:sq * 256], writes=['ST'])
                c.op('act', lambda e: e.activation(out=STb[0:64], in_=ST[0:64], func=AF.Copy), ['ST'], ['STb'])
            PZ, pkZ = nextps()
            for h in range(4):
                c.op('pe', lambda e: e.matmul(PZ[0:CL, h * 64:(h + 1) * 64], lhsT=At(h), rhs=STb[0:64, h, :], start=True, stop=False), ['AR', 'STb'], [pkZ])
                c.op('pe', lambda e: e.matmul(PZ[0:CL, h * 64:(h + 1) * 64], lhsT=G2v[:, h, 1, 0:CL], rhs=Vtm(h), start=False, stop=True), ['G2', tmk], [pkZ])
            c.op('act', lambda e: e.activation(out=Zs[0:CL], in_=PZ[0:CL, 0:256].rearrange("p (h v) -> p h v", h=4), func=AF.Copy), [pkZ], ['Zs'])
            yield
            PU, pkU = nextps()
            for h in range(4):
                c.op('pe', lambda e: e.matmul(PU[0:CL, h * 64:(h + 1) * 64], lhsT=TTv[:, h, 1, 0:CL], rhs=Zs[0:CL, h, :], start=True, stop=True), [TTk, 'Zs'], [pkU])
            c.op('act', lambda e: e.activation(out=Us[0:CL], in_=PU[0:CL, 0:256].rearrange("p (h v) -> p h v", h=4), func=AF.Copy), [pkU], ['Us'])
            yield
            PY, pkY = nextps()
            for h in range(4):
                c.op('pe', lambda e: e.matmul(PY[0:64, h * 64:h * 64 + CL], lhsT=STb[0:64, h, :], rhs=Rt(h), start=True, stop=False), ['AR', 'STb'], [pkY])
                c.op('pe', lambda e: e.matmul(PY[0:64, h * 64:h * 64 + CL], lhsT=Us[0:CL, h, :], rhs=G1v[:, h, 0, 0:CL], start=False, stop=False), ['Us', 'G1'], [pkY])
                c.op('pe', lambda e: e.matmul(PY[0:64, h * 64:h * 64 + CL], lhsT=Vtm(h), rhs=G2v[:, h, 0, 0:CL], start=False, stop=True), [tmk, 'G2'], [pkY])
            c.op('act', lambda e: e.activation(out=yraw[0:64, :, cs_], in_=PY[0:64, 0:256].rearrange("p (h t) -> p h t", h=4)[:, :, 0:CL], func=AF.Copy),
                 [pkY], ['yraw'])
            PS, pkS = nextps()
            for h in range(4):
                c.op('pe', lambda e: e.matmul(PS[0:64, h * 64:(h + 1) * 64], lhsT=Btm(h), rhs=Us[0:CL, h, :], start=True, stop=False), [tmk, 'Us'], [pkS])
                c.op('pe', lambda e: e.matmul(PS[0:64, h * 64:(h + 1) * 64], lhsT=Ktm(h), rhs=Vtm(h), start=False, stop=True), [tmk], [pkS])
            c.op('dve', lambda e: e.tensor_tensor(out=ST[0:64], in0=PS[0:64, 0:256].rearrange("p (h v) -> p h v", h=4), in1=ST[0:64], op=ALU.add),
                 [pkS, 'ST'], ['ST'])
            c.op('dve', lambda e: e.tensor_tensor(out=ST[0:64], in0=ST[0:64], in1=PC[0:64, :, ck:ck + 1].to_broadcast([64, 4, 64]), op=ALU.mult),
                 ['ST', 'PC'], ['ST'])
            c.op('act', lambda e: e.activation(out=STb[0:64], in_=ST[0:64], func=AF.Copy), ['ST'], ['STb'])
            if last:
                c.dma('pool', o_wkv[l, sq], ST[0:64].rearrange("p h v -> p (h v)"), reads=['ST'])
            tok['next'] = gi + 1
            yield
        yield
        c.op('act', lambda e: e.activation(out=h16[0:64, :, :T], in_=yraw[0:64, :, :T], func=AF.Copy), ['yraw'], ['h16'])
        pm, pkm = nextps()
        yield
        for h in range(4):
            c.op('pe', lambda e: e.matmul(pm[0:64, h * 128:h * 128 + T], lhsT=ones64, rhs=h16[0:64, h, :T], start=True, stop=True), ['cbb', 'h16'], [pkm])
        pmv = pm[0:64, :].rearrange("p (h t) -> p h t", h=4)[:, :, :T]
        yield
        c.op('dve', lambda e: e.scalar_tensor_tensor(out=yraw[0:64, :, :T], in0=pmv, scalar=-1.0 / 64, in1=yraw[0:64, :, :T], op0=ALU.mult, op1=ALU.add),
             [pkm, 'yraw'], ['yraw'])
        yield
        c.op('dve', lambda e: e.tensor_tensor(out=h16[0:64, :, :T], in0=yraw[0:64, :, :T], in1=yraw[0:64, :, :T], op=ALU.mult), ['yraw'], ['h16'])
        pv_, pkv = nextps()
        yield
        for h in range(4):
            c.op('pe', lambda e: e.matmul(pv_[0:64, h * 128:h * 128 + T], lhsT=ones64, rhs=h16[0:64, h, :T], start=True, stop=True), ['cbb', 'h16'], [pkv])
        yield
        c.op('act', lambda e: e.activation(out=t1[0:64, :, :T], in_=pv_[0:64, :].rearrange("p (h t) -> p h t", h=4)[:, :, :T], func=AF.Ln,
                                           bias=64e-5, scale=1.0 / 64), [pkv], ['t1'])
        yield
        c.op('act', lambda e: e.activation(out=t1[0:64, :, :T], in_=t1[0:64, :, :T], func=AF.Exp, scale=-0.5), ['t1'], ['t1'])
        yield
        c.op('dve', lambda e: e.tensor_tensor(out=yraw[0:64, :, :T], in0=yraw[0:64, :, :T], in1=t1[0:64, :, :T], op=ALU.mult), ['yraw', 't1'], ['yraw'])
        yield
        c.op('dve', lambda e: e.tensor_tensor(out=yraw[0:64, :, :T], in0=yraw[0:64, :, :T], in1=p64('b64_lnw', 4).unsqueeze(2).to_broadcast([64, 4, T]), op=ALU.mult),
             ['yraw', 'ppf'], ['yraw'])
        yield
        c.op('dve', lambda e: e.tensor_tensor(out=yraw[0:64, :, :T], in0=yraw[0:64, :, :T], in1=p64('b64_lnb', 4).unsqueeze(2).to_broadcast([64, 4, T]), op=ALU.add),
             ['yraw', 'ppf'], ['yraw'])
        yield
        c.op('dve', lambda e: e.tensor_tensor(out=t1[0:64, :, :T], in0=rr, in1=km[0:64, :, :T], op=ALU.mult), ['CM', 'km'], ['t1'])
        yield
        c.op('dve', lambda e: e.tensor_tensor(out=h16[0:64, :, :T], in0=t1[0:64, :, :T], in1=p64('b64_rk', 4).unsqueeze(2).to_broadcast([64, 4, T]), op=ALU.mult),
             ['t1', 'ppf'], ['h16'])
        pb_, pkb = nextps()
        yield
        for h in range(4):
            c.op('pe', lambda e: e.matmul(pb_[0:64, h * 128:h * 128 + T], lhsT=ones64, rhs=h16[0:64, h, :T], start=True, stop=True), ['cbb', 'h16'], [pkb])
        yield
        c.op('dve', lambda e: e.tensor_tensor(out=t1[0:64, :, :T], in0=pb_[0:64, :].rearrange("p (h t) -> p h t", h=4)[:, :, :T], in1=vr, op=ALU.mult),
             [pkb, 'CM'], ['t1'])
        yield
        c.op('dve', lambda e: e.tensor_tensor(out=yraw[0:64, :, :T], in0=yraw[0:64, :, :T], in1=t1[0:64, :, :T], op=ALU.add), ['yraw', 't1'], ['yraw'])
        yield
        c.op('dve', lambda e: e.tensor_tensor(out=outB[0:64, :, :T], in0=yraw[0:64, :, :T], in1=gg[0:64, :, :T], op=ALU.mult), ['yraw', 'gg'], ['outB'])
        for hh in range(2):
            c.dma('pool', b1[hh * 64:(hh + 1) * 64, 2:4, n0:n0 + T], outB[0:64, hh::2, :T], reads=['outB'], writes=k_b1([2, 3], n0, T))
        yield


WNAMES = ['norm1_g', 'w_in', 'a_conv_w', 'a_conv_b', 'a_gx_w', 'a_gx_b', 'a_ga_w', 'a_ga_b', 'a_lambda', 'b_mu', 'b_w0', 'b_w2',
          'b_a0', 'b_a2', 'b_g2', 'b_k_k', 'b_k_a', 'b_r_k', 'b_ln_w', 'b_ln_b', 'c_lb', 'c_norm_g', 'w_branch', 'w_out',
          'norm2_g', 'ffn_w_gate', 'ffn_w_up', 'ffn_conv_w', 'ffn_conv_b', 'ffn_w_down', 'final_norm_g']
_CACHE = {}


def host_prep(inp):
    f = lambda a: np.ascontiguousarray(np.asarray(a, dtype=np.float32))
    W = {k: np.asarray(inp[k], np.float32) for k in WNAMES}
    cb, ca, mp = host_consts()
    pp = np.stack([host_ppack(W, l) for l in range(2)])
    gt = np.stack([W['norm1_g'][0], W['norm1_g'][1], W['norm2_g'][0], W['norm2_g'][1], W['final_norm_g']]).astype(np.float32)
    shared = {"w_in": f(W['w_in']), "w_branch": f(W['w_branch'].reshape(2, 1024, 1024)), "w_out": f(W['w_out']),
              "ffn_w_gate": f(W['ffn_w_gate']), "ffn_w_up": f(W['ffn_w_up']), "ffn_w_down": f(W['ffn_w_down']),
              "ppack": f(pp), "gtab": f(gt), "cbpack": cb, "capack": ca, "maskP": mp}
    xp = np.asarray(inp['x_prompt'], np.float32)
    xs = np.asarray(inp['x_sample'], np.float32)
    s_ah = np.asarray(inp['state_a_h'], np.float32)
    s_ac = np.asarray(inp['state_a_conv'], np.float32)
    s_wkv = np.asarray(inp['state_b_wkv'], np.float32)
    s_sh = np.asarray(inp['state_b_shift'], np.float32)
    s_cs = np.asarray(inp['state_c_s'], np.float32)
    s_fc = np.asarray(inp['state_ffn_conv'], np.float32)
    ck = np.asarray(inp['cache_d_k'], np.float32)
    cv = np.asarray(inp['cache_d_v'], np.float32)
    maps = []
    for b in range(8):
        sl = slice(4 * b, 4 * b + 4)
        m = dict(shared)
        m["x"] = f(np.concatenate([xp[b], xs[sl].reshape(16, 1024)], 0))
        spk = np.zeros((2, 128, NSP), np.float32)
        for l in range(2):
            spk[l, :, SP['a_h']:SP['a_h'] + 8] = s_ah[l, sl].reshape(4, 2, 128).transpose(2, 0, 1).reshape(128, 8)
            spk[l, :, SP['a_conv']:SP['a_conv'] + 24] = s_ac[l, sl].reshape(4, 3, 2, 128).transpose(3, 0, 2, 1).reshape(128, 24)
            spk[l, :, SP['b_shift']:SP['b_shift'] + 32] = s_sh[l, sl].reshape(4, 8, 128).transpose(2, 0, 1).reshape(128, 32)
            spk[l, :, SP['f_conv']:SP['f_conv'] + 192] = s_fc[l, sl].reshape(4, 2, 24, 128).transpose(3, 0, 2, 1).reshape(128, 192)
            spk[l, :64, SP['b_sh64']:SP['b_sh64'] + 56] = s_sh[l, sl][:, :896].reshape(4, 14, 64).transpose(2, 0, 1).reshape(64, 56)
            spk[l, :, SP['b_shg']:SP['b_shg'] + 4] = s_sh[l, sl][:, 896:].T
        m["spack"] = spk
        m["wkvT"] = f(s_wkv[:, sl].transpose(0, 4, 1, 2, 3).reshape(2, 64, 1024))
        m["cs0"] = f(s_cs[:, sl].transpose(0, 3, 1, 2, 4).reshape(2, 128, 1024))
        m["kcT"] = f(ck[:, sl].reshape(2, 4, 2048, 2, 2, 64).transpose(0, 1, 4, 5, 3, 2).reshape(2, 4, 128, 4096))
        m["vc"] = f(cv[:, sl].reshape(2, 4, 16, 128, 256).transpose(0, 1, 3, 2, 4).reshape(2, 4, 128, 4096))
        maps.append(m)
    return maps


def host_post(results):
    yp = np.zeros((8, 2048, 1024), np.float32)
    ys = np.zeros((32, 4, 1024), np.float32)
    p_a_h = np.zeros((2, 8, 256), np.float32)
    p_a_conv = np.zeros((2, 8, 3, 256), np.float32)
    p_wkv = np.zeros((2, 8, 4, 64, 64), np.float32)
    p_sh = np.zeros((2, 8, 1024), np.float32)
    p_cs = np.zeros((2, 8, 4, 128, 64), np.float32)
    p_dk = np.zeros((2, 8, 2048, 4, 64), np.float32)
    p_dv = np.zeros((2, 8, 2048, 4, 64), np.float32)
    p_fc = np.zeros((2, 8, 2, 3072), np.float32)
    s_a_h = np.zeros((2, 32, 256), np.float32)
    s_a_conv = np.zeros((2, 32, 3, 256), np.float32)
    s_wkv = np.zeros((2, 32, 4, 64, 64), np.float32)
    s_sh = np.zeros((2, 32, 1024), np.float32)
    s_cs = np.zeros((2, 32, 4, 128, 64), np.float32)
    s_dk = np.zeros((2, 32, 4, 4, 64), np.float32)
    s_dv = np.zeros((2, 32, 4, 4, 64), np.float32)
    s_fc = np.zeros((2, 32, 2, 3072), np.float32)
    for b in range(8):
        r = results[b]
        y = r["y"]
        yp[b] = y[:2048]
        ys[4 * b:4 * b + 4] = y[2048:].reshape(4, 4, 1024)
        dk, dv = r["o_dk"], r["o_dv"]
        p_dk[:, b] = dk[:, :2048].reshape(2, 2048, 4, 64)
        p_dv[:, b] = dv[:, :2048].reshape(2, 2048, 4, 64)
        s_dk[:, 4 * b:4 * b + 4] = dk[:, 2048:].reshape(2, 4, 4, 4, 64)
        s_dv[:, 4 * b:4 * b + 4] = dv[:, 2048:].reshape(2, 4, 4, 4, 64)
        sm = r["o_small"]
        ah = sm[:, :, OS['a_h']:OS['a_h'] + 10].reshape(2, 128, 5, 2).transpose(0, 2, 3, 1).reshape(2, 5, 256)
        ac = sm[:, :, OS['a_conv']:OS['a_conv'] + 30].reshape(2, 128, 5, 2, 3).transpose(0, 2, 4, 3, 1).reshape(2, 5, 3, 256)
        sh64 = sm[:, :64, OS['b_sh64']:OS['b_sh64'] + 70].reshape(2, 64, 5, 14).transpose(0, 2, 3, 1).reshape(2, 5, 896)
        shg = sm[:, :, OS['b_shg']:OS['b_shg'] + 5].transpose(0, 2, 1)
        sh = np.concatenate([sh64, shg], axis=2)
        fc = sm[:, :, OS['f_conv']:OS['f_conv'] + 240].reshape(2, 128, 5, 24, 2).transpose(0, 2, 4, 3, 1).reshape(2, 5, 2, 3072)
        p_a_h[:, b], s_a_h[:, 4 * b:4 * b + 4] = ah[:, 0], ah[:, 1:]
        p_a_conv[:, b], s_a_conv[:, 4 * b:4 * b + 4] = ac[:, 0], ac[:, 1:]
        p_sh[:, b], s_sh[:, 4 * b:4 * b + 4] = sh[:, 0], sh[:, 1:]
        p_fc[:, b], s_fc[:, 4 * b:4 * b + 4] = fc[:, 0], fc[:, 1:]
        wk = r["o_wkv"].reshape(2, 5, 64, 4, 64).transpose(0, 1, 3, 4, 2)
        p_wkv[:, b], s_wkv[:, 4 * b:4 * b + 4] = wk[:, 0], wk[:, 1:]
        cs = r["o_cs"].reshape(2, 5, 128, 4, 64).transpose(0, 1, 3, 2, 4)
        p_cs[:, b], s_cs[:, 4 * b:4 * b + 4] = cs[:, 0], cs[:, 1:]
    return (yp, ys, p_a_h, p_a_conv, p_wkv, p_sh, p_cs, p_dk, p_dv, p_fc,
            s_a_h, s_a_conv, s_wkv, s_sh, s_cs, s_dk, s_dv, s_fc)


ENABLE = ('A', 'B', 'C', 'D')


def kernel(**inputs):
    maps = host_prep(inputs)
    key = tuple(ENABLE)
    if key not in _CACHE:
        _CACHE[key] = build_program(ENABLE)
    nc = _CACHE[key]
    res = run_bass_kernel_spmd(nc, maps, core_ids=list(range(8)))
    return host_post(res.results)
```
